# Optimizing a Trainium2 kernel written in Bass

```python
import math
import jax, jax.numpy as jnp
from jax import lax
import numpy as np

D_MODEL = 1024
BATCH = 8
SEQ = 2048
DEPTH = 1
DEC_BATCH = 128
DEC_SEQ = 4
PAST_LEN = 16384
PAGE_SIZE = 128

N_META = 16
H_RET = 4
H_GDN = 4
HEAD_DIM = D_MODEL // (H_RET + H_GDN)
RET_WIDTH = H_RET * HEAD_DIM
GDN_WIDTH = H_GDN * HEAD_DIM
MIX_WIDTH = RET_WIDTH + GDN_WIDTH
IN_COLS = 4 * RET_WIDTH + 4 * GDN_WIDTH + 2 * H_GDN
CHUNK = 128
CONV_W = 4
D_FF = -(-8 * D_MODEL // (3 * 256)) * 256
ROPE_BASE = 10000.0
LN_EPS = 1e-5
RMS_EPS = 1e-6
ALPHA = (2 * DEPTH) ** 0.25
BETA = (8 * DEPTH) ** -0.25

kernel_name = 'hymba_retention_gated_deltanet_deepnorm_step'


def layer_norm(x, g, b):
    xf = x.astype(jnp.float32)
    mu = jnp.mean(xf, -1, keepdims=True)
    var = jnp.mean(jnp.square(xf - mu), -1, keepdims=True)
    return ((xf - mu) * lax.rsqrt(var + LN_EPS) * g.astype(jnp.float32) + b.astype(jnp.float32)).astype(x.dtype)


def rms_norm(x):
    return x * lax.rsqrt(jnp.mean(jnp.square(x), -1, keepdims=True) + RMS_EPS)


def l2_norm(x):
    return x * lax.rsqrt(jnp.sum(jnp.square(x), -1, keepdims=True) + RMS_EPS)


def rotary(x, pos):
    half = HEAD_DIM // 2
    inv = ROPE_BASE ** (-jnp.arange(half, dtype=jnp.float32) / half)
    ang = pos.astype(jnp.float32)[:, None] * inv[None, :]
    cos = jnp.cos(ang)[None, :, None, :]
    sin = jnp.sin(ang)[None, :, None, :]
    x1, x2 = x[..., :half], x[..., half:]
    return jnp.concatenate([x1 * cos - x2 * sin, x1 * sin + x2 * cos], -1)


def retention_chunks(q, k, v, log_gamma, s0):
    C = q.shape[3]
    idx = jnp.arange(C, dtype=jnp.float32)
    rel = idx[:, None] - idx[None, :]
    causal = rel >= 0
    lg = log_gamma[:, None, None]
    decay = jnp.where(causal, jnp.exp(lg * jnp.where(causal, rel, 0.0)), 0.0)
    scores = jnp.einsum('bnhid,bnhjd->bnhij', q, k) * decay
    intra = jnp.einsum('bnhij,bnhjd->bnhid', scores, v)
    q_d = q * jnp.exp(log_gamma[:, None] * (idx + 1.0))[..., None]
    k_d = k * jnp.exp(log_gamma[:, None] * (C - 1.0 - idx))[..., None]
    chunk_decay = jnp.exp(log_gamma * C)

    def step(s, xs):
        qc, kc, vc = xs
        inter = jnp.einsum('bhcd,bhde->bhce', qc, s)
        s = chunk_decay[:, None, None] * s + jnp.einsum('bhcd,bhce->bhde', kc, vc)
        return s, inter

    xs = (jnp.moveaxis(q_d, 1, 0), jnp.moveaxis(k_d, 1, 0), jnp.moveaxis(v, 1, 0))
    s, inter = lax.scan(step, s0, xs)
    return intra + jnp.moveaxis(inter, 0, 1), s


def gated_delta_chunks(q, k, v, g, beta, s0):
    C = q.shape[3]
    D = q.shape[4]
    G = jnp.cumsum(g, axis=-1)
    idx = jnp.arange(C)
    incl = idx[:, None] >= idx[None, :]
    strict = idx[:, None] > idx[None, :]
    decay = jnp.exp(jnp.where(incl, G[..., :, None] - G[..., None, :], -jnp.inf))
    kk = jnp.einsum('bnhid,bnhjd->bnhij', k, k)
    lower = jnp.where(strict, kk * decay, 0.0) * beta[..., :, None]
    a_mat = jnp.eye(C, dtype=q.dtype) + lower
    gam = jnp.exp(G)
    rhs = jnp.concatenate([v * beta[..., None], k * (beta * gam)[..., None]], -1)
    sol = lax.linalg.triangular_solve(a_mat, rhs, left_side=True, lower=True, unit_diagonal=True)
    w_v, w_k = sol[..., :D], sol[..., D:]
    attn = jnp.einsum('bnhid,bnhjd->bnhij', q, k) * decay
    q_g = q * gam[..., None]
    k_t = k * jnp.exp(G[..., -1:] - G)[..., None]
    chunk_decay = jnp.exp(G[..., -1])

    def step(s, xs):
        wv, wk, at, qg, kt, cd = xs
        u = wv - jnp.einsum('bhcd,bhde->bhce', wk, s)
        o = jnp.einsum('bhcd,bhde->bhce', qg, s) + jnp.einsum('bhij,bhje->bhie', at, u)
        s = cd[..., None, None] * s + jnp.einsum('bhcd,bhce->bhde', kt, u)
        return s, o

    xs = tuple(jnp.moveaxis(t, 1, 0) for t in (w_v, w_k, attn, q_g, k_t, chunk_decay))
    s, o = lax.scan(step, s0, xs)
    return jnp.moveaxis(o, 0, 1), s


def token_mixers(h, pos, ret_s0, gdn_s0, conv_buf, segments, w_in, conv_w, a_log, dt_bias, gdn_norm_w, w_out):
    bsz, L, _ = h.shape
    proj = h @ w_in
    R, W = RET_WIDTH, GDN_WIDTH
    rq = proj[..., 0:R]
    rk = proj[..., R:2 * R]
    rv = proj[..., 2 * R:3 * R]
    rg = proj[..., 3 * R:4 * R]
    o = 4 * R
    gqkv = proj[..., o:o + 3 * W]
    gz = proj[..., o + 3 * W:o + 4 * W]
    ga = proj[..., o + 4 * W:o + 4 * W + H_GDN]
    gb = proj[..., o + 4 * W + H_GDN:o + 4 * W + 2 * H_GDN]

    f32 = jnp.float32
    rq = rotary(rq.astype(f32).reshape(bsz, L, H_RET, HEAD_DIM), pos)
    rk = rotary(rk.astype(f32).reshape(bsz, L, H_RET, HEAD_DIM), pos) * (HEAD_DIM ** -0.5)
    rv = rv.astype(f32).reshape(bsz, L, H_RET, HEAD_DIM)
    log_gamma = jnp.log(1.0 - 2.0 ** (-5.0 - jnp.arange(H_RET, dtype=f32)))

    xc = jnp.concatenate([conv_buf.astype(gqkv.dtype), gqkv], 1)
    conv = sum(xc[:, w:w + L] * conv_w[w] for w in range(CONV_W))
    new_conv = xc[:, L:]
    conv = jax.nn.silu(conv.astype(f32))
    gq = l2_norm(conv[..., :W].reshape(bsz, L, H_GDN, HEAD_DIM)) * (HEAD_DIM ** -0.5)
    gk = l2_norm(conv[..., W:2 * W].reshape(bsz, L, H_GDN, HEAD_DIM))
    gv = conv[..., 2 * W:].reshape(bsz, L, H_GDN, HEAD_DIM)
    beta = jax.nn.sigmoid(gb.astype(f32))
    g = -jnp.exp(a_log.astype(f32)) * jax.nn.softplus(ga.astype(f32) + dt_bias.astype(f32))

    ret_s = ret_s0.astype(f32)
    gdn_s = gdn_s0.astype(f32)
    ret_outs, gdn_outs = [], []
    for start, length, chunk in segments:
        n = length // chunk

        def blk(t):
            t = t[:, start:start + length].reshape((bsz, n, chunk) + t.shape[2:])
            return jnp.swapaxes(t, 2, 3)

        def unblk(t):
            return jnp.swapaxes(t, 2, 3).reshape(bsz, length, t.shape[2], HEAD_DIM)

        o_r, ret_s = retention_chunks(blk(rq), blk(rk), blk(rv), log_gamma, ret_s)
        o_g, gdn_s = gated_delta_chunks(blk(gq), blk(gk), blk(gv), blk(g), blk(beta), gdn_s)
        ret_outs.append(unblk(o_r))
        gdn_outs.append(unblk(o_g))
    o_r = jnp.concatenate(ret_outs, 1)
    o_g = jnp.concatenate(gdn_outs, 1)

    o_r = rms_norm(o_r).reshape(bsz, L, R) * jax.nn.silu(rg.astype(f32))
    o_g = (rms_norm(o_g) * gdn_norm_w.astype(f32)).reshape(bsz, L, W) * jax.nn.silu(gz.astype(f32))
    mixed = jnp.concatenate([o_r, o_g], -1).astype(h.dtype) @ w_out
    return mixed, ret_s, gdn_s, new_conv


def swiglu(h, w_gate_up, w_down):
    gu = h @ w_gate_up
    return (jax.nn.silu(gu[..., :D_FF]) * gu[..., D_FF:]) @ w_down


def setup_inputs(seed: int = 0) -> dict:
    key = jax.random.key(seed)
    ks = jax.random.split(key, 24)
    f32 = jnp.float32
    nrm = lambda k, s, sc: jax.random.normal(k, s, f32) * sc
    dt = jnp.exp(jax.random.uniform(ks[10], (DEPTH, H_GDN), f32, math.log(1e-3), math.log(1e-1)))
    return {
        'x_prompt': nrm(ks[0], (BATCH, SEQ, D_MODEL), 1.0),
        'x_sample': nrm(ks[1], (DEC_BATCH, DEC_SEQ, D_MODEL), 1.0),
        'state_ret': nrm(ks[2], (DEPTH, DEC_BATCH, H_RET, HEAD_DIM, HEAD_DIM), 0.5),
        'state_gdn': nrm(ks[3], (DEPTH, DEC_BATCH, H_GDN, HEAD_DIM, HEAD_DIM), 0.1),
        'state_conv': nrm(ks[4], (DEPTH, DEC_BATCH, CONV_W - 1, 3 * GDN_WIDTH), 1.0),
        'meta_tokens': nrm(ks[5], (N_META, D_MODEL), 1.0),
        'emb_ln_g': 1.0 + nrm(ks[6], (D_MODEL,), 0.02),
        'emb_ln_b': nrm(ks[7], (D_MODEL,), 0.02),
        'w_in': nrm(ks[8], (DEPTH, D_MODEL, IN_COLS), D_MODEL ** -0.5),
        'conv_w': nrm(ks[9], (DEPTH, CONV_W, 3 * GDN_WIDTH), CONV_W ** -0.5),
        'a_log': jnp.log(jax.random.uniform(ks[11], (DEPTH, H_GDN), f32, 1.0, 16.0)),
        'dt_bias': dt + jnp.log(-jnp.expm1(-dt)),
        'gdn_norm_w': 1.0 + nrm(ks[12], (DEPTH, HEAD_DIM), 0.02),
        'w_out': nrm(ks[13], (DEPTH, MIX_WIDTH, D_MODEL), BETA * MIX_WIDTH ** -0.5),
        'ln1_g': 1.0 + nrm(ks[14], (DEPTH, D_MODEL), 0.02),
        'ln1_b': nrm(ks[15], (DEPTH, D_MODEL), 0.02),
        'w_gate_up': nrm(ks[16], (DEPTH, D_MODEL, 2 * D_FF), D_MODEL ** -0.5),
        'w_down': nrm(ks[17], (DEPTH, D_FF, D_MODEL), BETA * D_FF ** -0.5),
        'ln2_g': 1.0 + nrm(ks[18], (DEPTH, D_MODEL), 0.02),
        'ln2_b': nrm(ks[19], (DEPTH, D_MODEL), 0.02),
    }


def reference(x_prompt, x_sample, state_ret, state_gdn, state_conv, meta_tokens, emb_ln_g, emb_ln_b,
              w_in, conv_w, a_log, dt_bias, gdn_norm_w, w_out, ln1_g, ln1_b, w_gate_up, w_down,
              ln2_g, ln2_b):
    bsz = x_prompt.shape[0]
    meta = jnp.broadcast_to(meta_tokens.astype(x_prompt.dtype)[None], (bsz, N_META, D_MODEL))
    hp = layer_norm(jnp.concatenate([meta, x_prompt], 1), emb_ln_g, emb_ln_b)
    pos_p = jnp.arange(N_META + SEQ)
    seg_p = ((0, N_META, N_META), (N_META, SEQ, CHUNK))
    hs = layer_norm(x_sample, emb_ln_g, emb_ln_b)
    pos_s = PAST_LEN + jnp.arange(DEC_SEQ)
    seg_s = ((0, DEC_SEQ, DEC_SEQ),)

    f32 = jnp.float32
    ret_p, gdn_p, conv_p, ret_s, gdn_s, conv_s = [], [], [], [], [], []
    for layer in range(DEPTH):
        mix_args = (w_in[layer], conv_w[layer], a_log[layer], dt_bias[layer], gdn_norm_w[layer], w_out[layer])
        zeros_ret = jnp.zeros((bsz, H_RET, HEAD_DIM, HEAD_DIM), f32)
        zeros_gdn = jnp.zeros((bsz, H_GDN, HEAD_DIM, HEAD_DIM), f32)
        zeros_conv = jnp.zeros((bsz, CONV_W - 1, 3 * GDN_WIDTH), hp.dtype)
        mp, r1, g1, c1 = token_mixers(hp, pos_p, zeros_ret, zeros_gdn, zeros_conv, seg_p, *mix_args)
        ms, r2, g2, c2 = token_mixers(hs, pos_s, state_ret[layer], state_gdn[layer], state_conv[layer], seg_s, *mix_args)
        hp = layer_norm(ALPHA * hp + mp, ln1_g[layer], ln1_b[layer])
        hs = layer_norm(ALPHA * hs + ms, ln1_g[layer], ln1_b[layer])
        hp = layer_norm(ALPHA * hp + swiglu(hp, w_gate_up[layer], w_down[layer]), ln2_g[layer], ln2_b[layer])
        hs = layer_norm(ALPHA * hs + swiglu(hs, w_gate_up[layer], w_down[layer]), ln2_g[layer], ln2_b[layer])
        ret_p.append(r1); gdn_p.append(g1); conv_p.append(c1)
        ret_s.append(r2); gdn_s.append(g2); conv_s.append(c2)

    y_prompt = hp[:, N_META:]
    y_sample = hs
    return (y_prompt, y_sample, jnp.stack(ret_p), jnp.stack(gdn_p), jnp.stack(conv_p),
            jnp.stack(ret_s), jnp.stack(gdn_s), jnp.stack(conv_s))
```

```python
import math
from contextlib import ExitStack

import numpy as np
import concourse.bass as bass
import concourse.mybir as mybir
from concourse.bass_utils import run_bass_kernel_spmd

F32 = mybir.dt.float32
BF16 = mybir.dt.bfloat16
AF = mybir.ActivationFunctionType
ALU = mybir.AluOpType

NCORES = 8
D = 1024
SEQ = 2048
NPT = SEQ // 128
NMETA = 16
NSAMP = 16
DSEQ = 4
HD = 128
NH = 4
INC = 4104
DFF = 2816
NCH = DFF // 128
PAST = 16384
ALPHA = 2.0 ** 0.25
LN_EPS = 1e-5
RMS_EPS = 1e-6
ENGS = ("pe", "act", "dve", "pool", "sp")
SCHED_ENABLE = True
EMBED_WAIT = True


class V:
    __slots__ = ("ap", "key")

    def __init__(self, ap, key):
        self.ap = ap
        self.key = key

    def __getitem__(self, idx):
        return V(self.ap[idx], self.key)

    def re(self, pat, **kw):
        return V(self.ap.rearrange(pat, **kw), self.key)

    def bc(self, axis, shape):
        return V(self.ap.unsqueeze(axis).to_broadcast(list(shape)), self.key)

    def bitcast(self, dt):
        return V(self.ap.bitcast(dt), self.key)

    @property
    def shape(self):
        return self.ap.shape


class Prog:
    NDMA = {"sp": 16, "pool": 8, "act": 2}

    def __init__(self, nc):
        self.nc = nc
        self.ops = {e: [] for e in ENGS}
        self.cnt = {e: 0 for e in ENGS}
        self.seen = {e: {} for e in ENGS}
        self.last_w = {}
        self.readers = {}
        self.bank = {}
        self.dma_n = {q: [0] * n for q, n in self.NDMA.items()}
        self.dma_rr = {q: 0 for q in self.NDMA}
        self.n_ops = 0
        self.clock = {}
        self.issue = {}
        self.n_issue = 0

    def _need(self, eng, waits, tok):
        if tok is None:
            return
        k, v = tok
        if self.seen[eng].get(k, 0) >= v:
            return
        if waits.get(k, 0) < v:
            waits[k] = v

    def _deps(self, eng, ins, outs):
        cand = {}

        def need(tok):
            if tok is None:
                return
            k, v = tok
            if self.seen[eng].get(k, 0) >= v:
                return
            if cand.get(k, 0) < v:
                cand[k] = v
        for x in ins:
            if not isinstance(x, V):
                continue
            if x.key.startswith("ps"):
                for e2, t in self.bank.get(x.key, {}).items():
                    if e2 != eng or eng != "pe":
                        need(t)
            else:
                need(self.last_w.get(x.key))
        for x in outs:
            if x.key.startswith("ps"):
                for e2, t in self.bank.get(x.key, {}).items():
                    if e2 != eng or eng != "pe":
                        need(t)
            else:
                need(self.last_w.get(x.key))
                for t in self.readers.get(x.key, ()):
                    need(t)
        return self._take_waits(eng, cand)

    def _take_waits(self, eng, cand):
        waits = []
        seen = self.seen[eng]
        for k, v in sorted(cand.items(), key=lambda kv: -self.issue.get(kv, 0)):
            if seen.get(k, 0) >= v:
                continue
            waits.append((k, v))
            seen[k] = v
            for k2, v2 in self.clock.get((k, v), {}).items():
                if seen.get(k2, 0) < v2:
                    seen[k2] = v2
        return waits

    def _stamp(self, eng, tok):
        c = dict(self.seen[eng])
        c.pop(eng, None)
        self.clock[tok] = c
        self.n_issue += 1
        self.issue[tok] = self.n_issue

    def _commit(self, eng, tok, ins, outs):
        for x in ins:
            if not isinstance(x, V):
                continue
            if x.key.startswith("ps"):
                self.bank.setdefault(x.key, {})[eng] = tok
            else:
                self.readers.setdefault(x.key, []).append(tok)
        for x in outs:
            if x.key.startswith("ps"):
                self.bank.setdefault(x.key, {})[eng] = tok
            else:
                self.last_w[x.key] = tok
                self.readers[x.key] = []

    def op(self, eng, fn, ins=(), outs=()):
        waits = self._deps(eng, ins, outs)
        self.cnt[eng] += 1
        tok = (eng, self.cnt[eng])
        self.ops[eng].append((fn, waits, (eng, 1)))
        self._stamp(eng, tok)
        self._commit(eng, tok, ins, outs)
        self.n_ops += 1
        return tok

    def dma(self, out, in_, queue="sp"):
        s = self.dma_rr[queue]
        self.dma_rr[queue] = (s + 1) % self.NDMA[queue]
        key = ("dma_" + queue, s)
        ins = [in_] if isinstance(in_, V) else []
        outs = [out] if isinstance(out, V) else []
        waits = dict(self._deps(queue, ins, outs))
        prev = self.dma_n[queue][s] * 16
        if prev and self.seen[queue].get(key, 0) < prev:
            waits[key] = prev
            self.seen[queue][key] = prev
        self.dma_n[queue][s] += 1
        tok = (key, self.dma_n[queue][s] * 16)
        oap = out.ap if isinstance(out, V) else out
        iap = in_.ap if isinstance(in_, V) else in_

        def fn(e, oap=oap, iap=iap):
            return e.dma_start(out=oap, in_=iap)

        self.ops[queue].append((fn, list(waits.items()), (key, 16)))
        self._stamp(queue, tok)
        self._commit("dma", tok, ins, outs)
        self.n_ops += 1
        return tok

    def barrier(self):
        toks = [(e, self.cnt[e]) for e in ENGS if self.cnt[e]]
        toks += [(("dma_" + q, i), self.dma_n[q][i] * 16) for q in self.NDMA for i in range(self.NDMA[q]) if self.dma_n[q][i]]
        for e in ENGS:
            waits = {}
            for t in toks:
                if t[0] == e:
                    continue
                self._need(e, waits, t)
            for k, v in waits.items():
                self.seen[e][k] = v
            self.ops[e].append((None, list(waits.items()), None))

    def emit(self):
        nc = self.nc
        with ExitStack() as es:
            sem = {}
            for e in ENGS:
                sem[e] = es.enter_context(nc.semaphore("s_" + e))
            for q in self.NDMA:
                for i in range(self.NDMA[q]):
                    sem[("dma_" + q, i)] = es.enter_context(nc.semaphore("s_dma_%s%d" % (q, i)))
            block = es.enter_context(nc.Block())
            ops = self.ops

            def run(e, name):
                for fn, waits, inc in ops[name]:
                    if fn is None:
                        for k, v in waits:
                            e.wait_ge(sem[k], v)
                        continue
                    embed = EMBED_WAIT and bool(waits) and not isinstance(inc[0], tuple)
                    for k, v in (waits[:-1] if embed else waits):
                        e.wait_ge(sem[k], v)
                    ins = fn(e)
                    if embed:
                        ins._wait_ge(sem[waits[-1][0]], waits[-1][1])
                    ins.then_inc(sem[inc[0]], inc[1])

            @block.tensor
            def _(e):
                run(e, "pe")

            @block.scalar
            def _(e):
                run(e, "act")

            @block.vector
            def _(e):
                run(e, "dve")

            @block.gpsimd
            def _(e):
                run(e, "pool")

            @block.sync
            def _(e):
                run(e, "sp")


class Sched:
    LAT = 0.3

    def __init__(self, prog, enable=True):
        self.P = prog
        self.items = []
        self.enable = enable

    @staticmethod
    def _cols(v):
        n = 1
        for d in v.ap.shape[1:]:
            n *= int(d)
        return n

    def op(self, eng, fn, ins=(), outs=(), aset=None):
        ins = list(ins); outs = list(outs)
        if eng == "pe":
            rhs = ins[1]
            n = self._cols(rhs)
            cost = 0.02 + n * 0.00042 * (4.0 if rhs.ap.dtype == F32 else 1.0)
        elif eng == "act":
            cost = 0.22 + self._cols(outs[0]) * 0.00075
        elif eng == "dve":
            cost = 0.07 + self._cols(outs[0]) * 0.0012
        else:
            cost = 0.15 + self._cols(outs[0]) * 0.0023
        self.items.append(["op", eng, fn, ins, outs, cost, cost, aset])

    def dma(self, out, in_, queue="sp"):
        v = out if isinstance(out, V) else in_
        nbytes = self._cols(v) * int(v.ap.shape[0]) * 4
        lat = 2.0 + nbytes / 200e3
        busy = 1.0 if queue == "pool" else 0.1
        self.items.append(["dma", queue, (out, in_), [in_] if isinstance(in_, V) else [],
                           [out] if isinstance(out, V) else [], busy, lat, None])

    def barrier(self):
        self.items.append(["bar"])

    def _schedule(self, seg):
        n = len(seg)
        last_w, readers, bank, pe_bank = {}, {}, {}, {}
        deps = [set() for _ in range(n)]
        keys_of = lambda x: x.key if isinstance(x.key, tuple) else (x.key,)
        for i, it in enumerate(seg):
            eng = it[1]
            d = deps[i]
            for x in it[3]:
                if not isinstance(x, V):
                    continue
                for k in keys_of(x):
                    if k.startswith("ps"):
                        for e2, j in bank.get(k, {}).items():
                            d.add(j)
                    elif k in last_w:
                        d.add(last_w[k])
            for x in it[4]:
                for k in keys_of(x):
                    if k.startswith("ps"):
                        for e2, j in bank.get(k, {}).items():
                            d.add(j)
                    else:
                        if k in last_w:
                            d.add(last_w[k])
                        d.update(readers.get(k, ()))
            d.discard(i)
            ceng = "dma" if it[0] == "dma" else eng
            for x in it[3]:
                if not isinstance(x, V):
                    continue
                for k in keys_of(x):
                    if k.startswith("ps"):
                        bank.setdefault(k, {})[ceng] = i
                    else:
                        readers.setdefault(k, []).append(i)
            for x in it[4]:
                for k in keys_of(x):
                    if k.startswith("ps"):
                        bank.setdefault(k, {})[ceng] = i
                    else:
                        last_w[k] = i
                        readers[k] = []
        succ = [[] for _ in range(n)]
        for i in range(n):
            for j in deps[i]:
                succ[j].append(i)
        prio = [0.0] * n
        for i in range(n - 1, -1, -1):
            m = 0.0
            for j in succ[i]:
                if prio[j] > m:
                    m = prio[j]
            prio[i] = seg[i][6] + self.LAT + m
        pmax = max(prio) if n else 1.0
        ndep = [len(deps[i]) for i in range(n)]
        ready_t = [0.0] * n
        fin = [0.0] * n
        avail = {}
        cur_set = {}
        cand = set(i for i in range(n) if ndep[i] == 0)
        order = []
        while cand:
            best = None; bkey = None
            for i in cand:
                it = seg[i]
                eng = it[1] if it[0] == "op" else "q_" + it[1]
                s_ = max(avail.get(eng, 0.0), ready_t[i])
                if it[7] is not None and cur_set.get(eng) not in (None, it[7]):
                    s_ += 1.3
                key = (s_ - 0.6 * prio[i] / pmax, i)
                if bkey is None or key < bkey:
                    bkey = key; best = i
            i = best
            cand.discard(i)
            it = seg[i]
            eng = it[1] if it[0] == "op" else "q_" + it[1]
            s_ = max(avail.get(eng, 0.0), ready_t[i])
            if it[7] is not None:
                if cur_set.get(eng) not in (None, it[7]):
                    s_ += 1.3
                cur_set[eng] = it[7]
            avail[eng] = s_ + it[5]
            fin[i] = s_ + it[6]
            order.append(i)
            for j in succ[i]:
                ndep[j] -= 1
                lat = 0.0 if (seg[j][0] == "op" and it[0] == "op" and seg[j][1] == it[1]) else self.LAT
                if fin[i] + lat > ready_t[j]:
                    ready_t[j] = fin[i] + lat
                if ndep[j] == 0:
                    cand.add(j)
        assert len(order) == n
        return order, max(fin) if n else 0.0

    def flush(self):
        seg = []
        tot = 0.0

        def run_seg(seg):
            nonlocal tot
            if not seg:
                return
            if self.enable:
                order, t_end = self._schedule(seg)
                tot += t_end
            else:
                order = range(len(seg))
            for i in order:
                it = seg[i]
                if it[0] == "op":
                    self.P.op(it[1], it[2], it[3], it[4])
                else:
                    self.P.dma(it[2][0], it[2][1], queue=it[1])
        for it in self.items:
            if it[0] == "bar":
                run_seg(seg)
                seg = []
                self.P.barrier()
            else:
                seg.append(it)
        run_seg(seg)
        print("scheduler: simulated total %.1f us" % tot)


class Arena:
    def __init__(self, ap, nwords):
        self.ap = ap
        self.n = nwords
        self.off = 0
        self.uid = 0

    def _take(self, words, name):
        assert self.off + words <= self.n, ("SBUF arena overflow", name, self.off, words, self.n)
        v = self.ap[:, self.off:self.off + words]
        self.off += words
        self.uid += 1
        return v, "%s#%d" % (name, self.uid)

    def f32(self, name, *shape):
        shape = shape[1:]
        w = int(np.prod(shape))
        ap, key = self._take(w, name)
        if len(shape) == 2:
            ap = ap.rearrange("p (a b) -> p a b", a=shape[0])
        elif len(shape) == 3:
            ap = ap.rearrange("p (a b c) -> p a b c", a=shape[0], b=shape[1])
        return V(ap, key)

    def bf16(self, name, *shape):
        shape = shape[1:]
        w = int(np.prod(shape))
        ap, key = self._take((w + 1) // 2, name)
        ap = ap.bitcast(BF16)[:, 0:w]
        if len(shape) == 2:
            ap = ap.rearrange("p (a b) -> p a b", a=shape[0])
        elif len(shape) == 3:
            ap = ap.rearrange("p (a b c) -> p a b c", a=shape[0], b=shape[1])
        return V(ap, key)


def _consts():
    f = np.float32
    lg = np.log(1.0 - 2.0 ** (-5.0 - np.arange(NH, dtype=np.float64)))
    idx = np.arange(128)
    c = {}
    c["IDF"] = np.eye(128, dtype=f)
    c["TRI"] = (idx[:, None] <= idx[None, :]).astype(f)
    c["STRICT"] = (idx[:, None] > idx[None, :]).astype(f)
    c["ONES"] = np.ones((128, 128), f)
    blk = idx // DSEQ
    same = (blk[:, None] == blk[None, :])
    c["TRIS"] = (c["TRI"] * same).astype(f)[:, 0:64]
    c["STRICTS"] = (c["STRICT"] * same).astype(f)[:, 0:64]
    loc = idx % DSEQ
    mr = np.zeros((128, NH, 128), f)
    mrs = np.zeros((128, NH, 128), f)
    for h in range(NH):
        mr[:, h, :] = (np.exp(-lg[h] * (idx[:, None] + 1.0)) * (idx[:, None] <= idx[None, :])).astype(f)
        mrs[:, h, :] = (np.exp(-lg[h] * (loc[:, None] + 1.0)) * (idx[:, None] <= idx[None, :]) * same).astype(f)
    c["MASKR"] = mr.reshape(128, NH * 128)
    c["MASKRS"] = np.ascontiguousarray(mrs[:, :, 0:64]).reshape(128, NH * 64)
    sc = np.zeros((128, 16), f)
    for h in range(NH):
        sc[:, h] = np.exp(lg[h] * (idx + 1.0))
        sc[:, 4 + h] = np.exp(lg[h] * (127.0 - idx))
        sc[:, 8 + h] = np.exp(lg[h] * np.maximum(15.0 - idx, 0))
        sc[:, 12 + h] = np.exp(lg[h] * (3.0 - loc))
    qs = np.zeros((128, 4), f)
    for h in range(NH):
        qs[:, h] = np.exp(lg[h] * (loc + 1.0))
    c["SC"] = sc
    c["QS"] = qs
    c["BMASK"] = (blk[:, None] == np.arange(16)[None, :]).astype(f)
    names = ["IDF", "TRI", "STRICT", "ONES", "TRIS", "STRICTS", "MASKR", "MASKRS", "SC", "QS", "BMASK"]
    offs = {}
    o = 0
    for n in names:
        offs[n] = (o, c[n].shape[1])
        o += c[n].shape[1]
    cst = np.concatenate([c[n] for n in names], axis=1).astype(f)
    cdec = {128: [float(np.exp(lg[h] * 128.0)) for h in range(NH)],
            16: [float(np.exp(lg[h] * 16.0)) for h in range(NH)],
            4: [float(np.exp(lg[h] * 4.0)) for h in range(NH)]}
    return cst, offs, cdec


def _rot_tables():
    half = HD // 2
    inv = (np.float32(10000.0) ** (-np.arange(half, dtype=np.float32) / np.float32(half))).astype(np.float32)
    ntile = NPT + 2
    pos = np.zeros((ntile, 128), np.float32)
    pos[0, :NMETA] = np.arange(NMETA)
    for t in range(NPT):
        pos[1 + t] = NMETA + t * 128 + np.arange(128)
    pos[NPT + 1, :NSAMP * DSEQ] = PAST + (np.arange(NSAMP * DSEQ) % DSEQ)
    ang = pos[:, :, None].astype(np.float32) * inv[None, None, :]
    cos = np.cos(ang).astype(np.float32)
    sin = np.sin(ang).astype(np.float32)
    ks = np.float32(HD ** -0.5)
    return np.concatenate([cos, cos * ks, sin, sin * ks], axis=2).astype(np.float32)


CST, COFF, CDEC = _consts()
ROT = _rot_tables()
NCST = CST.shape[1]
ROWP_N = 6 * D + 512 + 8


def build(dbg=False):
    nc = bass.Bass("TRN2", target_bir_lowering=False)
    di = lambda n, s: nc.dram_tensor(n, list(s), F32, kind="ExternalInput").ap()
    do = lambda n, s: nc.dram_tensor(n, list(s), F32, kind="ExternalOutput").ap()
    xp = di("xp", (SEQ, D)); xm = di("xm", (NMETA, D)); xs = di("xs", (NSAMP * DSEQ, D))
    sret = di("sret", (NSAMP, NH, HD, HD)); sgdn = di("sgdn", (NSAMP, NH, HD, HD))
    sconv = di("sconv", (NSAMP * 3, 1536))
    w_in = di("w_in", (D, INC)); w_out = di("w_out", (D, D))
    w_gu = di("w_gu", (D, 2 * DFF)); w_dn = di("w_dn", (DFF, D))
    rowp = di("rowp", (1, ROWP_N)); colp = di("colp", (128, 48))
    cst = di("cst", (128, NCST)); rot = di("rot", (NPT + 2, 128, 256))
    y_p = do("y_p", (SEQ, D)); y_s = do("y_s", (NSAMP * DSEQ, D))
    ret_p = do("ret_p", (NH, HD, HD)); gdn_p = do("gdn_p", (NH, HD, HD)); conv_p = do("conv_p", (3, 1536))
    ret_s = do("ret_s", (NSAMP, NH, HD, HD)); gdn_s = do("gdn_s", (NSAMP, NH, HD, HD))
    conv_s = do("conv_s", (NSAMP * 3, 1536))

    NW = 53200
    with ExitStack() as es:
        arena_t = es.enter_context(nc.sbuf_tensor("arena", [128, NW], F32))
        psb = [es.enter_context(nc.psum_tensor("psb%d" % i, [128, 512], F32)) for i in range(8)]
        PS = [V(psb[i][:], "ps%d" % i) for i in range(8)]
        P0 = Prog(nc)
        P = Sched(P0, enable=SCHED_ENABLE)
        A = Arena(arena_t[:], NW)

        def mm(out, lhsT, rhs, start=True, stop=True):
            P.op("pe", lambda e: e.matmul(out.ap, lhsT=lhsT.ap, rhs=rhs.ap, start=start, stop=stop),
                 ins=[lhsT, rhs], outs=[out])

        def act(out, in_, func, bias=0.0, scale=1.0, accum=None, eng="act"):
            ins = [in_] + [x for x in (bias, scale) if isinstance(x, V)]
            outs = [out] + ([accum] if accum is not None else [])
            b = bias.ap if isinstance(bias, V) else bias
            s = scale.ap if isinstance(scale, V) else scale
            a = accum.ap if accum is not None else None
            if accum is not None:
                ins.append(accum)
            aset = "silu" if func == AF.Silu else ("lnexp" if func in (AF.Exp, AF.Ln) else None)
            P.op("act", lambda e: e.activation(out=out.ap, in_=in_.ap, func=func, bias=b, scale=s, accum_out=a)
                 if a is not None else e.activation(out=out.ap, in_=in_.ap, func=func, bias=b, scale=s),
                 ins=ins, outs=outs, aset=aset)

        def tt(eng, out, a, b, op):
            P.op(eng, lambda e: e.tensor_tensor(out=out.ap, in0=a.ap, in1=b.ap, op=op), ins=[a, b], outs=[out])

        def ts(eng, out, a, s1, op0, s2=None, op1=None):
            ins = [a] + [x for x in (s1, s2) if isinstance(x, V)]
            v1 = s1.ap if isinstance(s1, V) else s1
            v2 = s2.ap if isinstance(s2, V) else s2
            if op1 is None:
                P.op(eng, lambda e: e.tensor_scalar(out=out.ap, in0=a.ap, scalar1=v1, scalar2=None, op0=op0),
                     ins=ins, outs=[out])
            else:
                P.op(eng, lambda e: e.tensor_scalar(out=out.ap, in0=a.ap, scalar1=v1, scalar2=v2, op0=op0, op1=op1),
                     ins=ins, outs=[out])

        def stt(eng, out, a, scalar, b, op0, op1):
            ins = [a, b] + ([scalar] if isinstance(scalar, V) else [])
            sv = scalar.ap if isinstance(scalar, V) else scalar
            P.op(eng, lambda e: e.scalar_tensor_tensor(out=out.ap, in0=a.ap, scalar=sv, in1=b.ap, op0=op0, op1=op1),
                 ins=ins, outs=[out])

        def cp(eng, out, in_):
            if eng == "act":
                P.op("act", lambda e: e.copy(out=out.ap, in_=in_.ap), ins=[in_], outs=[out])
            else:
                P.op(eng, lambda e: e.tensor_copy(out=out.ap, in_=in_.ap), ins=[in_], outs=[out])

        def memset(eng, out, val):
            P.op(eng, lambda e: e.memset(out.ap, val), ins=[], outs=[out])

        def recip(out, in_):
            P.op("dve", lambda e: e.reciprocal(out=out.ap, in_=in_.ap), ins=[in_], outs=[out])

        def rstd_from(out, ssq, scale, eps, tmp):
            act(tmp, ssq, AF.Ln, bias=eps, scale=scale)
            act(out, tmp, AF.Exp, scale=-0.5)

        H1 = A.bf16("H1", 128, NPT + 1, D)
        mark_persist = A.off

        WIN_Q = A.bf16("WINQ", 128, 8, 1536)
        WIN_G = A.bf16("WING", 128, 8, 512)
        WIN_X = A.bf16("WINX", 128, 8, 1536)
        WIN_Z = A.bf16("WINZ", 128, 8, 512)
        WIN_AB = A.bf16("WINAB", 128, 8, 8)
        _wq = WIN_Q.re("p k n -> p (k n)")
        WG0 = _wq[:, 0:4096].re("p (k n) -> p k n", k=8)
        WU0 = _wq[:, 4096:8192].re("p (k n) -> p k n", k=8)
        WD0 = _wq[:, 8192:12288].re("p (c n) -> p c n", c=4)
        _wgrp = [(WIN_AB, 4096, 8), (WIN_Q, 0, 1536), (WIN_X, 2048, 1536), (WIN_G, 1536, 512), (WIN_Z, 3584, 512)]

        def WIN_cols(k, c0, n):
            for buf, b0, bn in _wgrp:
                if b0 <= c0 and c0 + n <= b0 + bn:
                    return buf[:, k, c0 - b0:c0 - b0 + n]
            raise AssertionError((c0, n))
        WOUT = A.bf16("WOUT", 128, 8, D)
        CS = A.f32("CST", 128, NCST)
        RP = A.f32("ROWP", 128, 2 * D + 128 + 8)
        CW = A.f32("CW", 128, 12, 4)
        IDB = A.bf16("IDB", 128, 128)

        def C(name, rows=slice(0, 128)):
            o, n = COFF[name]
            return CS[rows, o:o + n]

        P.dma(CS, cst[:, :])
        P.dma(RP[:, 0:2 * D], rowp[:, 0:2 * D].partition_broadcast(128))
        P.dma(RP[:, 2 * D:2 * D + 128], rowp[:, 6 * D:6 * D + 128].partition_broadcast(128))
        P.dma(RP[:, 2 * D + 128:2 * D + 136], rowp[:, 6 * D + 512:6 * D + 520].partition_broadcast(128))
        P.dma(CW, colp.rearrange("p (c w) -> p c w", c=12))
        for buf, b0, bn in _wgrp:
            P.dma(buf, w_in[:, b0:b0 + bn].rearrange("(k p) n -> p k n", p=128), queue="pool")
        P.dma(WOUT, w_out.rearrange("(k p) n -> p k n", p=128), queue="pool")
        cp("dve", IDB, C("IDF"))
        G0 = RP[:, 0:D]; B0 = RP[:, D:2 * D]
        GNW = RP[:, 2 * D:2 * D + 128].bc(1, (128, 4, 128))
        DTB = RP[:, 2 * D + 128:2 * D + 132]
        NEGA = A.f32("NEGA", 128, 4)
        act(NEGA, RP[:, 2 * D + 132:2 * D + 136], AF.Exp)
        ts("dve", NEGA, NEGA, -1.0, ALU.mult)

        XT2 = [A.f32("XT0", 128, D), A.f32("XT1", 128, D)]
        _o_HB = A.off
        HB = A.bf16("HB", 128, D)
        HT = A.bf16("HT", 128, 8, 128)
        RT = A.f32("RT", 128, 256)
        _o_QK = [A.off, A.off + 512]
        QK2 = [A.bf16("QK0", 128, 2, 512), A.bf16("QK1", 128, 2, 512)]
        VV2 = [A.bf16("VV0", 128, 512), A.bf16("VV1", 128, 512)]
        SGT2 = [A.bf16("SGT0", 128, 512), A.bf16("SGT1", 128, 512)]
        SGZ2 = [A.bf16("SGZ0", 128, 512), A.bf16("SGZ1", 128, 512)]
        GAB = A.f32("GAB", 128, 8)
        XC = A.f32("XC", 128, 12, 131)
        ACC = A.f32("ACC", 128, 4, 128)
        T1 = ACC.re("p a b -> p (a b)")
        T2 = T1
        CV = A.bf16("CV", 128, 12, 128)
        ST = A.f32("ST", 128, 12)
        MV = A.f32("MV", 128, 2)
        SM8 = A.f32("SM8", 128, 8)
        JUNK = A.bf16("JUNK", 128, 128)
        MIX = A.bf16("MIX", 128, D)
        QDT = A.bf16("QDT", 128, 4, 128)
        KT = A.bf16("KT", 128, 4, 128)
        SMK = A.bf16("SMK", 128, 4, 128)
        SSO = A.f32("SSO", 128, 8)
        RSO = A.f32("RSO", 128, 8)
        RSO2 = A.f32("RSO2", 128, 8)
        GG = A.f32("GG", 128, 4)
        BETA = A.f32("BETA", 128, 4)
        NBETA = A.f32("NBETA", 128, 4)
        GSH = A.f32("GSH", 128, 2, 512)
        GSH0 = V(GSH.ap[:, 0, :], GSH.key + "/0")
        GSH1 = V(GSH.ap[:, 1, :], GSH.key + "/1")
        GTRI = GSH0.re("p (h i) -> p h i", h=4)
        TMPF = GTRI
        DIFF = GSH1.re("p (h i) -> p h i", h=4)
        GPP = A.f32("GPP", 128, 8)
        FF = A.bf16("FF", 128, 4, 128)
        D2M = A.bf16("D2M", 128, 4, 128)
        GAMBC = A.bf16("GAMBC", 128, 4, 128)
        GAMPP = A.f32("GAMPP", 128, 4)
        KTSC = A.f32("KTSC", 128, 4)
        CDV = A.f32("CDV", 128, 4)
        SSQ = A.f32("SSQ", 128, 8)
        RS = A.f32("RS", 128, 8)
        SCL = A.f32("SCL", 128, 16)
        TG = A.f32("TG", 128, 8)
        QN = A.bf16("QN", 128, 4, 128); KN = A.bf16("KN", 128, 4, 128); KBG = A.bf16("KBG", 128, 4, 128)
        KTS = A.bf16("KTS", 128, 4, 128); VB = A.bf16("VB", 128, 4, 128)
        KQT = A.bf16("KQT", 128, 2, 4, 128)
        MIXT = KQT.re("p a h i -> p (a h) i")
        QGT = A.bf16("QGT", 128, 4, 128)
        YB = A.bf16("YB", 128, 4, 128)
        MB = A.bf16("MB", 128, 4, 128)
        QQ = A.bf16("QQ", 128, 4, 128)
        ATT = A.bf16("ATT", 128, 4, 128)
        NWK = A.bf16("NWK", 128, 4, 128)
        UU = A.bf16("UU", 128, 4, 128)
        _o_SR = A.off
        SR = A.f32("SR", 128, 4, 128)
        _o_SG = A.off
        SGs = A.f32("SGs", 128, 4, 128)
        SRB = A.bf16("SRB", 128, 4, 128)
        SGB = A.bf16("SGB", 128, 4, 128)
        memset("pool", SR, 0.0); memset("pool", SGs, 0.0)
        memset("pool", SRB, 0.0); memset("pool", SGB, 0.0)
        memset("pool", XC, 0.0)
        memset("dve", SSO, 0.0)
        memset("dve", SSQ, 0.0)
        NS = NSAMP
        CS_ = NS * DSEQ
        SSq = [A.f32("SSq%d" % q, 128, 4, 128) for q in range(4)]
        _psamp0 = ((NPT if not dbg else dbg) + 1) % 2

        def _alias512(off, owner):
            return V(arena_t[:, off:off + 512].rearrange("p (s e) -> p s e", s=4), owner.key)
        SSq_b = [_alias512(_o_SR, SR), _alias512(_o_SG, SGs), _alias512(_o_QK[1 - _psamp0], QK2[1 - _psamp0]),
                 _alias512(_o_HB, HB)]
        SS_sets = [SSq, SSq_b]
        ss_unit = [0]
        _psamp = ((NPT if not dbg else dbg) + 1) % 2
        _xo = XT2[1 - _psamp]
        SSB = V(_xo.ap.bitcast(BF16)[:, 0:NS * 128].rearrange("p (s e) -> p s e", s=NS), _xo.key)
        ZA = A.bf16("ZA", 128, NS * 68)
        KM = A.bf16("KM", 128, NS // 4, 128)
        CDVS = A.f32("CDVS", 128, 4, NS)
        memset("pool", ZA, 0.0)
        print("phase A arena words", A.off)

        out_toks = []
        F0, F1, F2 = PS[0], PS[1], PS[2]
        R0, R1 = PS[3], PS[4]
        GA, GB, GC = PS[5], PS[6], PS[7]

        def v4(x, Ct, w=None):
            w = Ct if w is None else w
            return x.re("p (h i) -> p h i", h=4)[:, :, 0:w]

        def chain_Fa(kind, ti, Ct, rot_i, p):
            XT = XT2[p]; QK = QK2[p]; VV = VV2[p]
            st = []

            def s_load():
                if kind == "meta":
                    memset("pool", XT, 0.0)
                    P.dma(XT[0:NMETA, :], xm[:, :])
                elif kind == "p":
                    P.dma(XT, xp[ti * 128:(ti + 1) * 128, :])
                else:
                    memset("pool", XT, 0.0)
                    P.dma(XT[0:CS_, :], xs[:, :])
                P.dma(RT, rot[rot_i, :, :])
            st.append(s_load)

            def s_ln_a():
                for j in range(2):
                    P.op("dve", lambda e, j=j: e.bn_stats(out=ST.ap[:, j * 6:(j + 1) * 6], in_=XT.ap[:, j * 512:(j + 1) * 512]),
                         ins=[XT], outs=[ST])
                P.op("dve", lambda e: e.bn_aggr(out=MV.ap, in_=ST.ap.rearrange("p (a b) -> p a b", a=2)), ins=[ST], outs=[MV])
                rstd_from(SM8[:, 0:1], MV[:, 1:2], 1.0, LN_EPS, SM8[:, 1:2])
                stt("dve", SM8[:, 2:3], MV[:, 0:1], -1.0, SM8[:, 0:1], ALU.mult, ALU.mult)
            st.append(s_ln_a)

            def s_ln_b():
                act(XT, XT, AF.Identity, bias=SM8[:, 2:3], scale=SM8[:, 0:1])
                tt("dve", XT, XT, G0, ALU.mult)
            st.append(s_ln_b)

            def s_ln_c():
                tt("pool", XT, XT, B0, ALU.add)
                cp("act", HB, XT)
            st.append(s_ln_c)

            def s_tr(half):
                bank = F0 if half == 0 else F1
                for k in range(4 * half, 4 * half + 4):
                    mm(bank[:, (k % 4) * 128:(k % 4 + 1) * 128], HB[:, k * 128:(k + 1) * 128], IDB)
                cp("act", HT[:, 4 * half:4 * half + 4, :], bank.re("p (a b) -> p a b", a=4))
            st.append(lambda: s_tr(0))
            st.append(lambda: s_tr(1))

            def s_proj(bank, c0, half):
                for k in range(4 * half, 4 * half + 4):
                    mm(bank, HT[:, k, :], WIN_cols(k, c0, 512), start=(k == 0), stop=(k == 7))
            st.append(lambda: s_proj(F0, 0, 0))
            st.append(lambda: s_proj(F0, 0, 1))
            st.append(lambda: s_proj(F1, 512, 0))
            st.append(lambda: s_proj(F1, 512, 1))

            def s_rot(qi, bank, part):
                xv = bank.re("p (h t f) -> p h t f", h=4, t=2)
                x1 = xv[:, :, 0, :]; x2 = xv[:, :, 1, :]
                cosv = RT[:, qi * 64:(qi + 1) * 64].bc(1, (128, 4, 64))
                sinv = RT[:, 128 + qi * 64:128 + (qi + 1) * 64].bc(1, (128, 4, 64))
                ov = QK[:, qi, :].re("p (h t f) -> p h t f", h=4, t=2)
                if part == 0:
                    t1 = T2[:, 0:256].re("p (h f) -> p h f", h=4); t2 = T2[:, 256:512].re("p (h f) -> p h f", h=4)
                    tt("dve", t1, x1, cosv, ALU.mult)
                    tt("dve", t2, x2, sinv, ALU.mult)
                    tt("dve", ov[:, :, 0, :], t1, t2, ALU.subtract)
                else:
                    t3 = T2[:, 0:256].re("p (h f) -> p h f", h=4); t4 = T2[:, 256:512].re("p (h f) -> p h f", h=4)
                    tt("dve", t3, x1, sinv, ALU.mult)
                    tt("dve", t4, x2, cosv, ALU.mult)
                    tt("dve", ov[:, :, 1, :], t3, t4, ALU.add)
            st.append(lambda: s_rot(0, F0, 0))
            st.append(lambda: s_rot(0, F0, 1))
            st.append(lambda: s_proj(F2, 1024, 0))
            st.append(lambda: s_proj(F2, 1024, 1))
            st.append(lambda: s_rot(1, F1, 0))
            st.append(lambda: s_rot(1, F1, 1))
            st.append(lambda: cp("act", VV, F2))
            return st

        def chain_Fb(kind, Ct, p):
            SGT = SGT2[p]; SGZ = SGZ2[p]
            st = []

            def s_gate(bank, c0, dst, half):
                for k in range(4 * half, 4 * half + 4):
                    mm(bank, HT[:, k, :], WIN_cols(k, c0, 512), start=(k == 0), stop=(k == 7))
                if half == 1:
                    cp("act", dst, bank)
            st.append(lambda: s_gate(F0, 1536, SGT, 0))
            st.append(lambda: s_gate(F0, 1536, SGT, 1))
            st.append(lambda: s_gate(F1, 3584, SGZ, 0))
            st.append(lambda: s_gate(F1, 3584, SGZ, 1))

            def s_gab():
                for k in range(8):
                    mm(F2[:, 0:8], HT[:, k, :], WIN_cols(k, 4096, 8), start=(k == 0), stop=(k == 7))
                cp("dve", GAB, F2[:, 0:8])
            st.append(s_gab)

            def s_gq(g3, half=None):
                bank = PS[g3]
                cr = range(4 * g3, 4 * g3 + 4) if half is None else range(4 * g3 + 2 * half, 4 * g3 + 2 * half + 2)
                for c in cr:
                    ob = bank[:, (c % 4) * 128:(c % 4) * 128 + 128]
                    for k in range(8):
                        mm(ob, WIN_cols(k, 2048 + c * 128, 128), HT[:, k, :], start=(k == 0), stop=(k == 7))
            if kind != "s":
                def s_conv(g3, w):
                    cs = slice(4 * g3, 4 * g3 + 4)
                    b3 = (128, 4, Ct)
                    acc = ACC[:, :, 0:Ct]
                    tmp = (GSH0 if w % 2 else GSH1).re("p (c t) -> p c t", c=4)[:, :, 0:Ct]
                    if w == 0:
                        cp("act", XC[:, cs, 3:3 + Ct], PS[g3].re("p (a b) -> p a b", a=4)[:, :, 0:Ct])
                        tt("dve", acc, XC[:, cs, 0:Ct], CW[:, cs, 0].bc(2, b3), ALU.mult)
                    else:
                        tt("pool", tmp, XC[:, cs, w:w + Ct], CW[:, cs, w].bc(2, b3), ALU.mult)
                        tt("dve", CV[:, cs, 0:Ct] if w == 3 else acc, acc, tmp, ALU.add)
                    if w == 3:
                        cp("pool", XC[:, cs, 0:3], XC[:, cs, Ct:Ct + 3])
                for g3 in range(3):
                    st.append(lambda g3=g3: s_gq(g3, 0))
                    st.append(lambda g3=g3: s_gq(g3, 1))
                    for w in range(4):
                        st.append(lambda g3=g3, w=w: s_conv(g3, w))
            else:
                for g3 in range(3):
                    st.append(lambda g3=g3: s_gq(g3))
                st.append(conv_sample)

            def s_silu(i):
                if i == 0:
                    act(SGT, SGT, AF.Silu)
                    act(SGZ, SGZ, AF.Silu)
                    d3 = SGZ.re("p (h i) -> p h i", h=4)
                    tt("dve", d3, d3, GNW, ALU.mult)
                elif kind != "s":
                    act(CV[:, :, 0:Ct], CV[:, :, 0:Ct], AF.Silu)
            st.append(lambda: s_silu(0))
            st.append(lambda: s_silu(1))
            return st

        def conv_sample():
            XCs = XC[:, :, 0:NS * 7].re("p c (s w) -> p c s w", s=NS)
            SCB = GSH0[0:NS * 3, :]
            for g3 in range(3):
                cs = slice(4 * g3, 4 * g3 + 4)
                cp("act", XCs[:, cs, :, 3:7],
                   PS[g3].re("p (a b) -> p a b", a=4)[:, :, 0:CS_].re("p a (s c) -> p a s c", s=NS))
            for g3 in range(3):
                cs = slice(4 * g3, 4 * g3 + 4)
                P.dma(SCB, sconv[:, g3 * 512:(g3 + 1) * 512])
                for c in range(4):
                    mm(R0[:, c * 128:c * 128 + NS * 3], SCB[:, c * 128:(c + 1) * 128], C("IDF")[0:NS * 3, 0:NS * 3])
                cp("dve", XCs[:, cs, :, 0:3],
                   R0.re("p (a b) -> p a b", a=4)[:, :, 0:NS * 3].re("p a (s r) -> p a s r", s=NS))
            for g3 in range(3):
                cs = slice(4 * g3, 4 * g3 + 4)
                ACCs = ACC[:, :, 0:CS_].re("p c (s k) -> p c s k", s=NS)
                tmp4 = GSH0[:, 0:4 * CS_].re("p (c s k) -> p c s k", c=4, s=NS)
                b4 = (128, 4, NS, DSEQ)

                def cwb(w):
                    return CW[:, cs, w].bc(2, (128, 4, NS)).bc(3, b4)
                tt("dve", ACCs, XCs[:, cs, :, 0:4], cwb(0), ALU.mult)
                for w in range(1, 4):
                    tt("dve", tmp4, XCs[:, cs, :, w:w + 4], cwb(w), ALU.mult)
                    tt("dve", ACCs, ACCs, tmp4, ALU.add)
                act(CV[:, cs, 0:CS_], ACC[:, :, 0:CS_], AF.Silu)
            for g3 in range(3):
                cs = slice(4 * g3, 4 * g3 + 4)
                X48 = T1[:, 0:4 * NS * 3].re("p (c r) -> p c r", c=4)
                cp("pool", X48.re("p c (s r) -> p c s r", s=NS), XCs[:, cs, :, 4:7])
                for c in range(4):
                    mm(R0[0:NS * 3, c * 128:(c + 1) * 128], X48[:, c, :], C("IDF"))
                cp("dve", SCB, R0[0:NS * 3, :])
                out_toks.append(P.dma(conv_s[:, g3 * 512:(g3 + 1) * 512], SCB))

        def chain_R(Ct, p, kds_off, cd, samp):
            QK = QK2[p]; VV = VV2[p]; SGT = SGT2[p]
            idc = IDB[0:Ct, 0:Ct]
            q3 = QK[0:Ct, 0, :].re("c (h d) -> c h d", h=4)
            k3 = QK[0:Ct, 1, :].re("c (h d) -> c h d", h=4)
            st = []

            def s1():
                qsc = (C("QS") if samp else C("SC")[:, 0:4])[0:Ct, :]
                tt("dve", q3, q3, qsc.bc(2, (Ct, 4, 128)), ALU.mult)
                for h in range(NH):
                    mm(R0[:, h * 128:h * 128 + Ct], q3[:, h, :], idc)
                cp("act", QDT[:, :, 0:Ct], v4(R0, Ct))
            st.append(s1)

            def s2():
                for h in range(NH):
                    mm(R1[:, h * 128:h * 128 + Ct], k3[:, h, :], idc)
                cp("act", KT[:, :, 0:Ct], v4(R1, Ct))
                tt("dve", k3, k3, C("SC")[0:Ct, kds_off:kds_off + 4].bc(2, (Ct, 4, 128)), ALU.mult)
            st.append(s2)

            def s3():
                for h in range(NH):
                    mm(R0[0:Ct, h * 128:h * 128 + Ct], KT[:, h, 0:Ct], QDT[:, h, 0:Ct])
                mk = (C("MASKRS").re("p (h i) -> p h i", h=4) if samp else C("MASKR").re("p (h i) -> p h i", h=4))[0:Ct, :, 0:Ct]
                tt("dve", SMK[0:Ct, :, 0:Ct], v4(R0, Ct)[0:Ct], mk, ALU.mult)
            st.append(s3)

            if not samp:
                def s4():
                    for h in range(NH):
                        hs = slice(h * 128, (h + 1) * 128)
                        mm(R1[0:Ct, hs], SMK[0:Ct, h, 0:Ct], VV[0:Ct, hs], start=True, stop=False)
                        mm(R1[0:Ct, hs], QDT[:, h, 0:Ct], SRB[:, h, :], start=False, stop=True)
                    for h in range(NH):
                        hs = slice(h * 128, (h + 1) * 128)
                        mm(R0[:, hs], k3[:, h, :], VV[0:Ct, hs])
                st.append(s4)

                def s5():
                    for h in range(NH):
                        hs = slice(h * 128, (h + 1) * 128)
                        stt("dve", SR[:, h, :], SR[:, h, :], cd[h], R0[:, hs], ALU.mult, ALU.add)
                    cp("act", SRB, SR)
                st.append(s5)
            else:
                for h in range(NH):
                    st.append(lambda h=h: sample_state_ret(h, Ct, cd, k3, VV))

            def s6():
                for h in range(NH):
                    hs = slice(h * 128, (h + 1) * 128)
                    act(JUNK[0:Ct, :], R1[0:Ct, hs], AF.Square, accum=SSO[0:Ct, h:h + 1])
                rstd_from(RSO[0:Ct, 0:4], SSO[0:Ct, 0:4], 1.0 / HD, RMS_EPS, RSO2[0:Ct, 0:4])
                for h in range(NH):
                    hs = slice(h * 128, (h + 1) * 128)
                    stt("dve", MIX[0:Ct, hs], R1[0:Ct, hs], RSO[0:Ct, h:h + 1], SGT[0:Ct, hs], ALU.mult, ALU.mult)
                memset("dve", SSO[:, 0:4], 0.0)
            st.append(s6)
            return st

        def zfill(Z, src):
            cp("dve", Z.re("p (s w) -> p s w", s=NS)[:, :, 0:DSEQ], src.re("p (s c) -> p s c", s=NS))

        def load_states(src, h):
            cur = SS_sets[ss_unit[0] % 2]
            ss_unit[0] += 1
            for q in range(4):
                P.dma(cur[q], src[4 * q:4 * q + 4, h, :, :].rearrange("s d e -> d s e"))
                cp("act", SSB[:, 4 * q:4 * q + 4, :], cur[q])
            return cur

        def sample_state_ret(h, Ct, cd, k3, VV):
            hs = slice(h * 128, (h + 1) * 128)
            SSc = load_states(sret, h)
            zfill(ZA, QDT[:, h, 0:Ct])
            mm(R1[0:Ct, hs], SMK[0:Ct, h, 0:Ct], VV[0:Ct, hs], start=True, stop=False)
            for s_ in range(NS):
                mm(R1[0:Ct, hs], ZA[:, s_ * 64:(s_ + 1) * 64], SSB[:, s_, :], start=False, stop=(s_ == NS - 1))
            for q4 in range(4):
                bank = PS[q4 % 2]
                tt("dve", KM[0:Ct, :, :], k3[:, h, :].bc(1, (Ct, 4, 128)),
                   C("BMASK")[0:Ct, q4 * 4:(q4 + 1) * 4].bc(2, (Ct, 4, 128)), ALU.mult)
                for j in range(4):
                    mm(bank[:, j * 128:(j + 1) * 128], KM[0:Ct, j, :], VV[0:Ct, hs])
                v = SSc[q4]
                stt("dve", v, v, cd[h], bank.re("p (a b) -> p a b", a=4), ALU.mult, ALU.add)
                out_toks.append(P.dma(ret_s[4 * q4:4 * q4 + 4, h, :, :].rearrange("s d e -> d s e"), v, queue="pool"))

        def sample_state_gdn(h, Ct):
            hs = slice(h * 128, (h + 1) * 128)
            Qf = QQ[0:Ct, h, 0:Ct]
            SSc = load_states(sgdn, h)
            zfill(ZA, NWK[:, h, 0:Ct])
            mm(GA[0:Ct, hs], Qf, VB[0:Ct, h, :], start=True, stop=False)
            for s_ in range(NS):
                mm(GA[0:Ct, hs], ZA[:, s_ * 64:(s_ + 1) * 64], SSB[:, s_, :], start=False, stop=(s_ == NS - 1))
            cp("act", UU[0:Ct, h, :], GA[0:Ct, hs])
            zfill(ZA, QGT[:, h, 0:Ct])
            for s_ in range(NS):
                mm(GB[0:Ct, hs], ZA[:, s_ * 64:(s_ + 1) * 64], SSB[:, s_, :], start=(s_ == 0), stop=False)
            mm(GB[0:Ct, hs], ATT[0:Ct, h, 0:Ct], UU[0:Ct, h, :], start=False, stop=True)
            for q4 in range(4):
                bank = PS[q4 % 2]
                v = SSc[q4]
                tt("dve", v, v, CDVS[:, h, 4 * q4:4 * q4 + 4].bc(2, (128, 4, 128)), ALU.mult)
                tt("dve", KM[0:Ct, :, :], KTS[0:Ct, h, :].bc(1, (Ct, 4, 128)),
                   C("BMASK")[0:Ct, q4 * 4:(q4 + 1) * 4].bc(2, (Ct, 4, 128)), ALU.mult)
                for j in range(4):
                    mm(bank[:, j * 128:(j + 1) * 128], KM[0:Ct, j, :], UU[0:Ct, h, :])
                tt("dve", v, v, bank.re("p (a b) -> p a b", a=4), ALU.add)
                out_toks.append(P.dma(gdn_s[4 * q4:4 * q4 + 4, h, :, :].rearrange("s d e -> d s e"), v, queue="pool"))

        def chain_G(Ct, nlev, samp, p):
            SGZ = SGZ2[p]
            idc = IDB[0:Ct, 0:Ct]
            tri = (C("TRIS") if samp else C("TRI"))[0:Ct, 0:Ct]
            strict = (C("STRICTS") if samp else C("STRICT"))[0:Ct, 0:Ct]
            sh = (Ct, 4, Ct)
            st = []

            def g1():
                tt("dve", TG[0:Ct, 0:4], GAB[0:Ct, 0:4], DTB[0:Ct, :], ALU.add)
                act(TG[0:Ct, 0:4], TG[0:Ct, 0:4], AF.Exp)
                act(TG[0:Ct, 0:4], TG[0:Ct, 0:4], AF.Ln, bias=1.0)
                tt("dve", GG[0:Ct, :], TG[0:Ct, 0:4], NEGA[0:Ct, :], ALU.mult)
                act(TG[0:Ct, 4:8], GAB[0:Ct, 4:8], AF.Exp, scale=-1.0)
                ts("dve", TG[0:Ct, 4:8], TG[0:Ct, 4:8], 1.0, ALU.add)
                recip(BETA[0:Ct, :], TG[0:Ct, 4:8])
                ts("dve", NBETA[0:Ct, :], BETA[0:Ct, :], -1.0, ALU.mult)
            st.append(g1)

            def g2():
                tt("dve", GTRI[0:Ct, :, 0:Ct], tri.bc(1, sh), GG[0:Ct, :].bc(2, sh), ALU.mult)
                mm(GA[0:Ct, 0:4], tri, GG[0:Ct, :])
                mm(GA[0:Ct, 4:8], strict, GG[0:Ct, :])
                cp("dve", GPP[0:Ct, :], GA[0:Ct, 0:8])
                for h in range(NH):
                    mm(GB[:, h * 128:h * 128 + Ct], C("ONES")[0:Ct, :], GTRI[0:Ct, h, 0:Ct])
            st.append(g2)
            gbc = GB.re("p (h i) -> p h i", h=4)

            def g3():
                tt("dve", DIFF[0:Ct, :, 0:Ct], gbc[0:Ct, :, 0:Ct], GPP[0:Ct, 0:4].bc(2, sh), ALU.subtract)
                act(GAMBC[:, :, 0:Ct], gbc[:, :, 0:Ct], AF.Exp)
                if not samp:
                    act(CDV, gbc[:, :, Ct - 1], AF.Exp)
                else:
                    act(CDVS, gbc[:, :, 0:Ct].re("p h (s c) -> p h s c", c=DSEQ)[:, :, :, DSEQ - 1], AF.Exp)
                act(GAMPP[0:Ct, :], GPP[0:Ct, 0:4], AF.Exp)
                act(KTSC[0:Ct, :], GPP[0:Ct, 4:8], AF.Exp)
            st.append(g3)

            def g4():
                ts("dve", TMPF[0:Ct, :, 0:Ct], DIFF[0:Ct, :, 0:Ct], 0.0, ALU.min)
                act(D2M[0:Ct, :, 0:Ct], TMPF[0:Ct, :, 0:Ct], AF.Exp)
                tt("dve", D2M[0:Ct, :, 0:Ct], D2M[0:Ct, :, 0:Ct], tri.bc(1, sh), ALU.mult)
            st.append(g4)

            def g5():
                ts("dve", TMPF[0:Ct, :, 0:Ct], DIFF[0:Ct, :, 0:Ct], -1.0, ALU.mult, 0.0, ALU.min)
                act(FF[0:Ct, :, 0:Ct], TMPF[0:Ct, :, 0:Ct], AF.Exp)
                tt("dve", FF[0:Ct, :, 0:Ct], FF[0:Ct, :, 0:Ct], NBETA[0:Ct, :].bc(2, sh), ALU.mult)
                tt("dve", FF[0:Ct, :, 0:Ct], FF[0:Ct, :, 0:Ct], strict.bc(1, sh), ALU.mult)
            st.append(g5)

            def m1():
                for j, bank in enumerate((GA, GB, GC)):
                    for h in range(NH):
                        mm(bank[0:Ct, h * 128:(h + 1) * 128], CV[:, j * 4 + h, 0:Ct], IDB)
                for h in range(NH):
                    hs = slice(h * 128, (h + 1) * 128)
                    act(JUNK[0:Ct, :], GA[0:Ct, hs], AF.Square, accum=SSQ[0:Ct, h:h + 1])
                    act(JUNK[0:Ct, :], GB[0:Ct, hs], AF.Square, accum=SSQ[0:Ct, 4 + h:5 + h])
                rstd_from(RS[0:Ct, :], SSQ[0:Ct, :], 1.0, RMS_EPS, SCL[0:Ct, 8:16])
                memset("dve", SSQ, 0.0)
                ts("dve", SCL[0:Ct, 0:4], RS[0:Ct, 0:4], float(HD ** -0.5), ALU.mult)
                tt("dve", SCL[0:Ct, 4:8], RS[0:Ct, 4:8], KTSC[0:Ct, :], ALU.mult)
                tt("dve", SCL[0:Ct, 8:12], RS[0:Ct, 4:8], BETA[0:Ct, :], ALU.mult)
                tt("dve", SCL[0:Ct, 8:12], SCL[0:Ct, 8:12], GAMPP[0:Ct, :], ALU.mult)
            st.append(m1)

            def m1b():
                b3 = (Ct, 4, 128)
                for h in range(NH):
                    hs = slice(h * 128, (h + 1) * 128)
                    act(QN[0:Ct, h, :], GA[0:Ct, hs], AF.Copy, scale=SCL[0:Ct, h:h + 1])
                tt("dve", KN[0:Ct], v4(GB, Ct, 128)[0:Ct], RS[0:Ct, 4:8].bc(2, b3), ALU.mult)
                tt("dve", KBG[0:Ct], v4(GB, Ct, 128)[0:Ct], SCL[0:Ct, 8:12].bc(2, b3), ALU.mult)
                tt("dve", KTS[0:Ct], v4(GB, Ct, 128)[0:Ct], SCL[0:Ct, 4:8].bc(2, b3), ALU.mult)
                for h in range(NH):
                    hs = slice(h * 128, (h + 1) * 128)
                    act(VB[0:Ct, h, :], GC[0:Ct, hs], AF.Copy, scale=BETA[0:Ct, h:h + 1])
            st.append(m1b)

            def m2():
                for h in range(NH):
                    mm(GA[:, h * 128:h * 128 + Ct], KN[0:Ct, h, :], idc)
                for h in range(NH):
                    mm(GB[:, h * 128:h * 128 + Ct], QN[0:Ct, h, :], idc)
                cp("act", KQT[:, 0, :, 0:Ct], v4(GA, Ct))
                cp("act", KQT[:, 1, :, 0:Ct], v4(GB, Ct))
                tt("dve", QGT[:, :, 0:Ct], v4(GB, Ct), GAMBC[:, :, 0:Ct], ALU.mult)
            st.append(m2)

            def m3():
                for h in range(NH):
                    mm(GC[0:Ct, h * 128:h * 128 + Ct], KQT[:, 0, h, 0:Ct], KQT[:, 0, h, 0:Ct])
                for h in range(NH):
                    mm(GA[0:Ct, h * 128:h * 128 + Ct], KQT[:, 0, h, 0:Ct], KQT[:, 1, h, 0:Ct])
                tt("dve", MB[0:Ct, :, 0:Ct], v4(GC, Ct)[0:Ct], FF[0:Ct, :, 0:Ct], ALU.mult)
                tt("dve", ATT[0:Ct, :, 0:Ct], v4(GA, Ct)[0:Ct], D2M[0:Ct, :, 0:Ct], ALU.mult)
            st.append(m3)

            def m4():
                for h in range(NH):
                    mm(GB[0:Ct, h * 128:h * 128 + Ct], MB[0:Ct, h, 0:Ct], idc)
                cp("act", YB[0:Ct, :, 0:Ct], v4(GB, Ct)[0:Ct])
                tt("dve", QQ[0:Ct, :, 0:Ct], v4(GB, Ct)[0:Ct], C("IDF")[0:Ct, 0:Ct].bc(1, sh), ALU.add)
            st.append(m4)

            def lvl_sq(lv):
                for h in range(NH):
                    mm(GC[0:Ct, h * 128:h * 128 + Ct], MB[0:Ct, h, 0:Ct], YB[0:Ct, h, 0:Ct])
                for h in range(NH):
                    mm(GA[0:Ct, h * 128:h * 128 + Ct], YB[0:Ct, h, 0:Ct], MB[0:Ct, h, 0:Ct])
                cp("act", YB[0:Ct, :, 0:Ct], v4(GC, Ct)[0:Ct])
                cp("dve", MB[0:Ct, :, 0:Ct], v4(GA, Ct)[0:Ct])

            def lvl_q(lv):
                for h in range(NH):
                    mm(GB[0:Ct, h * 128:h * 128 + Ct], idc, QQ[0:Ct, h, 0:Ct], start=True, stop=False)
                    mm(GB[0:Ct, h * 128:h * 128 + Ct], MB[0:Ct, h, 0:Ct], QQ[0:Ct, h, 0:Ct], start=False, stop=True)
                cp("act" if lv % 2 else "dve", QQ[0:Ct, :, 0:Ct], v4(GB, Ct)[0:Ct])
            def lvl_fused(j):
                for h in range(NH):
                    mm(GB[0:Ct, h * 128:h * 128 + Ct], idc, QQ[0:Ct, h, 0:Ct], start=True, stop=False)
                    mm(GB[0:Ct, h * 128:h * 128 + Ct], MB[0:Ct, h, 0:Ct], QQ[0:Ct, h, 0:Ct], start=False, stop=True)
                for h in range(NH):
                    mm(GC[0:Ct, h * 128:h * 128 + Ct], MB[0:Ct, h, 0:Ct], YB[0:Ct, h, 0:Ct])
                for h in range(NH):
                    mm(GA[0:Ct, h * 128:h * 128 + Ct], YB[0:Ct, h, 0:Ct], MB[0:Ct, h, 0:Ct])
                cp("act", YB[0:Ct, :, 0:Ct], v4(GC, Ct)[0:Ct])
                cp("dve" if j % 2 else "act", MB[0:Ct, :, 0:Ct], v4(GA, Ct)[0:Ct])
                cp("act" if j % 2 else "dve", QQ[0:Ct, :, 0:Ct], v4(GB, Ct)[0:Ct])
            st.append(lambda: lvl_sq(0))
            for j in range(1, nlev):
                st.append(lambda j=j: lvl_fused(j))
            st.append(lambda: lvl_q(nlev - 1))

            def m6():
                for h in range(NH):
                    mm(GC[:, h * 128:h * 128 + Ct], KBG[0:Ct, h, :], QQ[0:Ct, h, 0:Ct])
                act(NWK[:, :, 0:Ct], v4(GC, Ct), AF.Copy, scale=-1.0)
            st.append(m6)

            if not samp:
                def m7():
                    for h in range(NH):
                        hs = slice(h * 128, (h + 1) * 128)
                        mm(GA[0:Ct, hs], QQ[0:Ct, h, 0:Ct], VB[0:Ct, h, :], start=True, stop=False)
                        mm(GA[0:Ct, hs], NWK[:, h, 0:Ct], SGB[:, h, :], start=False, stop=True)
                    cp("act", UU[0:Ct], v4(GA, Ct, 128)[0:Ct])
                st.append(m7)

                def m8():
                    for h in range(NH):
                        hs = slice(h * 128, (h + 1) * 128)
                        mm(GB[0:Ct, hs], QGT[:, h, 0:Ct], SGB[:, h, :], start=True, stop=False)
                        mm(GB[0:Ct, hs], ATT[0:Ct, h, 0:Ct], UU[0:Ct, h, :], start=False, stop=True)
                    for h in range(NH):
                        hs = slice(h * 128, (h + 1) * 128)
                        mm(GC[:, hs], KTS[0:Ct, h, :], UU[0:Ct, h, :])
                    for h in range(NH):
                        hs = slice(h * 128, (h + 1) * 128)
                        stt("dve", SGs[:, h, :], SGs[:, h, :], CDV[:, h:h + 1], GC[:, hs], ALU.mult, ALU.add)
                    cp("act", SGB, SGs)
                st.append(m8)
            else:
                for h in range(NH):
                    st.append(lambda h=h: sample_state_gdn(h, Ct))

            def m9():
                for h in range(NH):
                    hs = slice(h * 128, (h + 1) * 128)
                    act(JUNK[0:Ct, :], GB[0:Ct, hs], AF.Square, accum=SSO[0:Ct, 4 + h:5 + h])
                rstd_from(RSO[0:Ct, 4:8], SSO[0:Ct, 4:8], 1.0 / HD, RMS_EPS, RSO2[0:Ct, 4:8])
                for h in range(NH):
                    hs = slice(h * 128, (h + 1) * 128)
                    stt("dve", MIX[0:Ct, 512 + h * 128:512 + (h + 1) * 128], GB[0:Ct, hs], RSO[0:Ct, 4 + h:5 + h],
                        SGZ[0:Ct, hs], ALU.mult, ALU.mult)
                memset("dve", SSO[:, 4:8], 0.0)
            st.append(m9)
            return st

        def chain_E(slot, p):
            XT = XT2[p]
            st = []

            def e1(half):
                bank = F0 if half == 0 else F1
                for k in range(4 * half, 4 * half + 4):
                    mm(bank[:, (k % 4) * 128:(k % 4 + 1) * 128], MIX[:, k * 128:(k + 1) * 128], IDB)
                cp("act", MIXT[:, 4 * half:4 * half + 4, :], bank.re("p (a b) -> p a b", a=4))
            st.append(lambda: e1(0))
            st.append(lambda: e1(1))

            def e2(hf):
                bank = F0 if hf == 0 else F1
                for k in range(8):
                    mm(bank, MIXT[:, k, :], WOUT[:, k, hf * 512:(hf + 1) * 512], start=(k == 0), stop=(k == 7))
                stt("dve", H1[:, slot, hf * 512:(hf + 1) * 512], XT[:, hf * 512:(hf + 1) * 512], ALPHA, bank,
                    ALU.mult, ALU.add)
            st.append(lambda: e2(0))
            st.append(lambda: e2(1))
            return st

        def merge(*chains):
            items = []
            for ci, ch in enumerate(chains):
                n = len(ch)
                for j, f in enumerate(ch):
                    items.append(((j + 0.5) / n, ci, j, f))
            items.sort(key=lambda x: (x[0], x[1]))
            for _, _, _, f in items:
                f()

        def run(ch):
            for f in ch:
                f()

        def load_w(wg, wu, wd, c0, n):
            P.dma(wg[:, :, 0:n * 128], w_gu[:, c0 * 128:(c0 + n) * 128].rearrange("(k p) n -> p k n", p=128), queue="pool")
            P.dma(wu[:, :, 0:n * 128], w_gu[:, DFF + c0 * 128:DFF + (c0 + n) * 128].rearrange("(k p) n -> p k n", p=128),
                  queue="pool")
            P.dma(wd[:, 0:n, :], w_dn[c0 * 128:(c0 + n) * 128, :].rearrange("(c p) n -> p c n", p=128), queue="pool")

        def load_block0_early():
            load_w(WG0, WU0, WD0, 0, 4)

        npt = NPT if not dbg else dbg
        tiles = [("meta", 0, NMETA, 0, None, 8, CDEC[16], 3)]
        for t in range(npt):
            tiles.append(("p", t, 128, 1 + t, t, 4, CDEC[128], 6))
        tiles.append(("s", 0, CS_, NPT + 1, npt, 12, CDEC[4], 1))
        run(chain_Fa(tiles[0][0], tiles[0][1], tiles[0][2], tiles[0][3], 0))
        run(chain_Fb(tiles[0][0], tiles[0][2], 0))
        pend_E = None
        NG_HEAD = 7
        for i, (kind, ti, Ct, rot_i, slot, kds, cd, nlev) in enumerate(tiles):
            p = i % 2
            samp = (kind == "s")
            nxt = tiles[i + 1] if i + 1 < len(tiles) else None
            G = chain_G(Ct, nlev, samp, p)
            R = chain_R(Ct, p, kds, cd, samp)
            partA = [G[:NG_HEAD], R]
            if pend_E is not None:
                if samp:
                    run(pend_E)
                else:
                    partA.append(pend_E)
                pend_E = None
            merge(*partA)
            partB = [G[NG_HEAD:]]
            if nxt is not None and nxt[0] == "s":
                def conv_out():
                    for g3 in range(3):
                        for c in range(4):
                            mm(F0[0:3, c * 128:(c + 1) * 128], XC[:, g3 * 4 + c, 0:3], C("IDF"))
                        cp("dve", T2[0:3, :], F0[0:3, :])
                        out_toks.append(P.dma(conv_p[:, g3 * 512:(g3 + 1) * 512], T2[0:3, :]))
                conv_out()
            if nxt is not None:
                partB.append(chain_Fa(nxt[0], nxt[1], nxt[2], nxt[3], 1 - p) + chain_Fb(nxt[0], nxt[2], 1 - p))
            merge(*partB)
            if nxt is not None and nxt[0] == "s":
                for h in range(NH):
                    out_toks.append(P.dma(ret_p[h, :, :], SR[:, h, :]))
                    out_toks.append(P.dma(gdn_p[h, :, :], SGs[:, h, :]))
                load_block0_early()
            if slot is not None:
                pend_E = chain_E(slot, p)
        run(pend_E)

        P.barrier()
        A.off = mark_persist + 6144
        NT = npt + 1
        ACCB_t = [A.f32("ACCB%d" % t, 128, D) for t in range(NT)]
        groups = [(g0, min(4, NT - g0)) for g0 in range(0, NT, 4)]
        H1T_g = [A.bf16("H1T%d" % gi, 128, 8, gn * 128) for gi, (g0, gn) in enumerate(groups)]
        RP2 = A.f32("RP2", 128, 2 * D)
        IDB2 = A.bf16("IDB2", 128, 128)
        CS2 = A.f32("IDF2", 128, 128)
        FB = 4
        blocks = [(c0, min(FB, NCH - c0)) for c0 in range(0, NCH, FB)]
        WG = [WG0, A.bf16("WG1", 128, 8, FB * 128)]
        WU = [WU0, A.bf16("WU1", 128, 8, FB * 128)]
        WD = [WD0, A.bf16("WD1", 128, FB, D)]
        SGB2 = A.f32("SGB2", 128, 512)
        ACT_T = A.bf16("ACTT", 128, FB, 512)
        NROT = 3
        ST2_r = [A.f32("ST2_%d" % i, 128, 12) for i in range(NROT)]
        MV2_r = [A.f32("MV2_%d" % i, 128, 2) for i in range(NROT)]
        SM82_r = [A.f32("SM82_%d" % i, 128, 8) for i in range(NROT)]
        HB2_r = [A.bf16("HB2_%d" % i, 128, D) for i in range(2)]
        rot_ctr = [0]
        print("phase B arena words", A.off)
        P.dma(RP2, rowp[:, 2 * D:4 * D].partition_broadcast(128))
        P.dma(CS2, cst[:, COFF["IDF"][0]:COFF["IDF"][0] + 128])
        cp("dve", IDB2, CS2)

        def load_block(bi):
            c0, n = blocks[bi]
            s = bi % 2
            load_w(WG[s], WU[s], WD[s], c0, n)

        GRP_ROT = 3
        MVG_r = [A.f32("MVG%d" % i, 128, 4, 2) for i in range(GRP_ROT)]
        SMG_r = [A.f32("SMG%d" % i, 128, 3, 4) for i in range(GRP_ROT)]
        grp_ctr = [0]

        def ln_group(tiles, x_in_of, x_of, out_of, gv, bv):
            r = grp_ctr[0] % GRP_ROT
            grp_ctr[0] += 1
            MVG = MVG_r[r]; SMG = SMG_r[r]
            n = len(tiles)
            pre = []

            def s_stats(i, t):
                ST2 = ST2_r[i % NROT]
                x_in = x_in_of(t)
                for j in range(2):
                    P.op("dve", lambda e, j=j: e.bn_stats(out=ST2.ap[:, j * 6:(j + 1) * 6], in_=x_in.ap[:, j * 512:(j + 1) * 512]),
                         ins=[x_in], outs=[ST2])
                P.op("dve", lambda e: e.bn_aggr(out=MVG.ap[:, i, :], in_=ST2.ap.rearrange("p (a b) -> p a b", a=2)),
                     ins=[ST2], outs=[MVG])
            for i, t in enumerate(tiles):
                pre.append(lambda i=i, t=t: s_stats(i, t))

            def s_rstd():
                rstd_from(SMG[:, 0, 0:n], MVG[:, 0:n, 1], 1.0, LN_EPS, SMG[:, 1, 0:n])
                stt("dve", SMG[:, 2, 0:n], MVG[:, 0:n, 0], -1.0, SMG[:, 0, 0:n], ALU.mult, ALU.mult)
            pre.append(s_rstd)
            per = {}
            for i, t in enumerate(tiles):
                def s2(i=i, t=t):
                    x = x_of(t)
                    act(x, x_in_of(t), AF.Identity, bias=SMG[:, 2, i:i + 1], scale=SMG[:, 0, i:i + 1])
                    tt("dve", x, x, gv, ALU.mult)
                    tt("dve", out_of(t), x, bv, ALU.add)
                per[t] = [s2]
            return pre, per

        def b0_tail_steps(t):
            x = ACCB_t[t]
            HB2 = HB2_r[t % 2]
            H1T = H1T_g[t // 4]
            tl = t % 4
            pa = PS[4 + 2 * (t % 2)]; pb_ = PS[5 + 2 * (t % 2)]

            def s3():
                cp("act", HB2, x)
                act(x, x, AF.Copy, scale=ALPHA)
                for k in range(8):
                    mm((pa if k < 4 else pb_)[:, (k % 4) * 128:(k % 4 + 1) * 128], HB2[:, k * 128:(k + 1) * 128], IDB2)

            def s4():
                cp("act", H1T[:, 0:4, tl * 128:(tl + 1) * 128], pa.re("p (a b) -> p a b", a=4))
                cp("dve", H1T[:, 4:8, tl * 128:(tl + 1) * 128], pb_.re("p (a b) -> p a b", a=4))
            return [s3, s4]

        def out_step(t):
            def s():
                x = ACCB_t[t]
                if t < npt:
                    out_toks.append(P.dma(y_p[t * 128:(t + 1) * 128, :], x))
                else:
                    out_toks.append(P.dma(y_s[:, :], x[0:CS_, :]))
            return s

        class Pump:
            def __init__(self):
                self.seq = []
                self.done = {}
                self.pos = 0

            def add_seq(self, steps):
                self.seq.extend(steps)

            def add(self, tiles_steps):
                waves = {}
                for i, (t, st) in enumerate(tiles_steps):
                    for j, f in enumerate(st):
                        waves.setdefault(i + j, []).append((t, j == len(st) - 1, f))
                for w in sorted(waves):
                    for t, last, f in waves[w]:
                        self.seq.append(f)
                        if last:
                            self.done[t] = len(self.seq)

            def pump(self, n):
                e = min(len(self.seq), self.pos + n)
                while self.pos < e:
                    self.seq[self.pos]()
                    self.pos += 1

            def ensure(self, t):
                self.pump(self.done[t] - self.pos)

            def flush(self):
                self.pump(len(self.seq))

        pump1 = Pump()
        for gi, (g0, gn) in enumerate(groups):
            tl_ = list(range(g0, g0 + gn))
            pre, per = ln_group(tl_, lambda t: H1[:, t, :], lambda t: ACCB_t[t], lambda t: ACCB_t[t],
                                RP2[:, 0:D], RP2[:, D:2 * D])
            pump1.add_seq(pre)
            pump1.add([(t, per[t] + b0_tail_steps(t)) for t in tl_])
        pump2 = Pump()
        pump1.ensure(groups[0][0] + groups[0][1] - 1)

        for bi, (c0, n) in enumerate(blocks):
            if bi + 1 < len(blocks):
                load_block(bi + 1)
            s = bi % 2
            for gi, (g0, gn) in enumerate(groups):
                if bi == 0:
                    pump1.ensure(g0 + gn - 1)
                ntok = gn * 128
                for c in range(n):
                    if bi == 0:
                        pump1.pump(5)
                    if bi == len(blocks) - 1:
                        pump2.pump(3)
                    pg = PS[(c % 2) * 2]; pu = PS[(c % 2) * 2 + 1]
                    for k in range(8):
                        mm(pg[:, 0:ntok], WG[s][:, k, c * 128:(c + 1) * 128], H1T_g[gi][:, k, 0:ntok],
                           start=(k == 0), stop=(k == 7))
                    for k in range(8):
                        mm(pu[:, 0:ntok], WU[s][:, k, c * 128:(c + 1) * 128], H1T_g[gi][:, k, 0:ntok],
                           start=(k == 0), stop=(k == 7))
                    act(SGB2[:, 0:ntok], pg[:, 0:ntok], AF.Silu)
                    tt("dve", ACT_T[:, c, 0:ntok], SGB2[:, 0:ntok], pu[:, 0:ntok], ALU.mult)
                for ti in range(gn):
                    t = g0 + ti
                    for hf in range(2):
                        pb = PS[4 + (ti % 2) * 2 + hf]
                        for c in range(n):
                            mm(pb, ACT_T[:, c, ti * 128:(ti + 1) * 128], WD[s][:, c, hf * 512:(hf + 1) * 512],
                               start=(c == 0), stop=(c == n - 1))
                        tt("dve", ACCB_t[t][:, hf * 512:(hf + 1) * 512], ACCB_t[t][:, hf * 512:(hf + 1) * 512], pb, ALU.add)
                if bi == len(blocks) - 1:
                    tl_ = list(range(g0, g0 + gn))
                    pre, per = ln_group(tl_, lambda t: ACCB_t[t], lambda t: ACCB_t[t], lambda t: ACCB_t[t],
                                        RP2[:, 0:D], RP2[:, D:2 * D])
                    pump2.add_seq(pre)
                    pump2.add([(t, per[t] + [out_step(t)]) for t in tl_])
            if bi == 0:
                pump1.flush()
                P.dma(RP2, rowp[:, 4 * D:6 * D].partition_broadcast(128))
        pump2.flush()

        P.barrier()
        P.flush()
        print("ops", P0.n_ops, {e: len(P0.ops[e]) for e in ENGS}, "arena words", A.off)
        P0.emit()
    return nc


def _prep_inputs(inp):
    f = np.float32
    g = lambda k: np.ascontiguousarray(np.asarray(inp[k], dtype=f))
    rowp = np.concatenate([g("emb_ln_g"), g("emb_ln_b"), g("ln1_g")[0], g("ln1_b")[0], g("ln2_g")[0], g("ln2_b")[0],
                           np.tile(g("gdn_norm_w")[0], 4), g("dt_bias")[0], g("a_log")[0]]).reshape(1, -1).astype(f)
    cw = g("conv_w")[0]
    colp = np.ascontiguousarray(cw.reshape(4, 12, 128).transpose(2, 1, 0).reshape(128, 48))
    shared = {"w_in": g("w_in")[0], "w_out": g("w_out")[0], "w_gu": g("w_gate_up")[0], "w_dn": g("w_down")[0],
              "rowp": rowp, "colp": colp, "cst": CST, "rot": ROT, "xm": g("meta_tokens")}
    xp = g("x_prompt"); xs = g("x_sample")
    sr = g("state_ret")[0]; sg = g("state_gdn")[0]; sc = g("state_conv")[0]
    maps = []
    for b in range(NCORES):
        m = dict(shared)
        m["xp"] = xp[b]
        m["xs"] = np.ascontiguousarray(xs[b * NSAMP:(b + 1) * NSAMP].reshape(NSAMP * DSEQ, D))
        m["sret"] = np.ascontiguousarray(sr[b * NSAMP:(b + 1) * NSAMP])
        m["sgdn"] = np.ascontiguousarray(sg[b * NSAMP:(b + 1) * NSAMP])
        m["sconv"] = np.ascontiguousarray(sc[b * NSAMP:(b + 1) * NSAMP].reshape(NSAMP * 3, 1536))
        maps.append(m)
    return maps


_NC_CACHE = {}


def kernel(**inputs):
    if "nc" not in _NC_CACHE:
        _NC_CACHE["nc"] = build()
    nc = _NC_CACHE["nc"]
    maps = _prep_inputs(inputs)
    res = run_bass_kernel_spmd(nc, maps, core_ids=list(range(NCORES)))
    r = res.results
    f = np.float32
    y_prompt = np.stack([r[b]["y_p"] for b in range(NCORES)]).astype(f)
    y_sample = np.concatenate([r[b]["y_s"].reshape(NSAMP, DSEQ, D) for b in range(NCORES)]).astype(f)
    ret_p = np.stack([r[b]["ret_p"] for b in range(NCORES)])[None].astype(f)
    gdn_p = np.stack([r[b]["gdn_p"] for b in range(NCORES)])[None].astype(f)
    conv_p = np.stack([r[b]["conv_p"] for b in range(NCORES)])[None].astype(f)
    ret_s = np.concatenate([r[b]["ret_s"] for b in range(NCORES)])[None].astype(f)
    gdn_s = np.concatenate([r[b]["gdn_s"] for b in range(NCORES)])[None].astype(f)
    conv_s = np.concatenate([r[b]["conv_s"].reshape(NSAMP, 3, 1536) for b in range(NCORES)])[None].astype(f)
    return (y_prompt, y_sample, ret_p, gdn_p, conv_p, ret_s, gdn_s, conv_s)
```

```python
import math
from contextlib import ExitStack

import numpy as np
import concourse.bass as bass
import concourse.mybir as mybir
from concourse.bass_utils import run_bass_kernel_spmd

F32 = mybir.dt.float32
BF16 = mybir.dt.bfloat16
AF = mybir.ActivationFunctionType
ALU = mybir.AluOpType

NCORES = 8
D = 1024
SEQ = 2048
NPT = SEQ // 128
NMETA = 16
NSAMP = 16
DSEQ = 4
HD = 128
NH = 4
INC = 4104
DFF = 2816
NCH = DFF // 128
PAST = 16384
ALPHA = 2.0 ** 0.25
LN_EPS = 1e-5
RMS_EPS = 1e-6
ENGS = ("pe", "act", "dve", "pool", "sp")
SCHED_ENABLE = True
EMBED_WAIT = True


class V:
    __slots__ = ("ap", "key")

    def __init__(self, ap, key):
        self.ap = ap
        self.key = key

    def __getitem__(self, idx):
        return V(self.ap[idx], self.key)

    def re(self, pat, **kw):
        return V(self.ap.rearrange(pat, **kw), self.key)

    def bc(self, axis, shape):
        return V(self.ap.unsqueeze(axis).to_broadcast(list(shape)), self.key)

    def bitcast(self, dt):
        return V(self.ap.bitcast(dt), self.key)

    @property
    def shape(self):
        return self.ap.shape


class Prog:
    NDMA = {"sp": 16, "pool": 8, "act": 2}

    def __init__(self, nc):
        self.nc = nc
        self.ops = {e: [] for e in ENGS}
        self.cnt = {e: 0 for e in ENGS}
        self.seen = {e: {} for e in ENGS}
        self.last_w = {}
        self.readers = {}
        self.bank = {}
        self.dma_n = {q: [0] * n for q, n in self.NDMA.items()}
        self.dma_rr = {q: 0 for q in self.NDMA}
        self.n_ops = 0
        self.clock = {}
        self.issue = {}
        self.n_issue = 0

    def _need(self, eng, waits, tok):
        if tok is None:
            return
        k, v = tok
        if self.seen[eng].get(k, 0) >= v:
            return
        if waits.get(k, 0) < v:
            waits[k] = v

    def _deps(self, eng, ins, outs):
        cand = {}

        def need(tok):
            if tok is None:
                return
            k, v = tok
            if self.seen[eng].get(k, 0) >= v:
                return
            if cand.get(k, 0) < v:
                cand[k] = v
        for x in ins:
            if not isinstance(x, V):
                continue
            if x.key.startswith("ps"):
                for e2, t in self.bank.get(x.key, {}).items():
                    if e2 != eng or eng != "pe":
                        need(t)
            else:
                need(self.last_w.get(x.key))
        for x in outs:
            if x.key.startswith("ps"):
                for e2, t in self.bank.get(x.key, {}).items():
                    if e2 != eng or eng != "pe":
                        need(t)
            else:
                need(self.last_w.get(x.key))
                for t in self.readers.get(x.key, ()):
                    need(t)
        return self._take_waits(eng, cand)

    def _take_waits(self, eng, cand):
        waits = []
        seen = self.seen[eng]
        for k, v in sorted(cand.items(), key=lambda kv: -self.issue.get(kv, 0)):
            if seen.get(k, 0) >= v:
                continue
            waits.append((k, v))
            seen[k] = v
            for k2, v2 in self.clock.get((k, v), {}).items():
                if seen.get(k2, 0) < v2:
                    seen[k2] = v2
        return waits

    def _stamp(self, eng, tok):
        c = dict(self.seen[eng])
        c.pop(eng, None)
        self.clock[tok] = c
        self.n_issue += 1
        self.issue[tok] = self.n_issue

    def _commit(self, eng, tok, ins, outs):
        for x in ins:
            if not isinstance(x, V):
                continue
            if x.key.startswith("ps"):
                self.bank.setdefault(x.key, {})[eng] = tok
            else:
                self.readers.setdefault(x.key, []).append(tok)
        for x in outs:
            if x.key.startswith("ps"):
                self.bank.setdefault(x.key, {})[eng] = tok
            else:
                self.last_w[x.key] = tok
                self.readers[x.key] = []

    def op(self, eng, fn, ins=(), outs=()):
        waits = self._deps(eng, ins, outs)
        self.cnt[eng] += 1
        tok = (eng, self.cnt[eng])
        self.ops[eng].append((fn, waits, (eng, 1)))
        self._stamp(eng, tok)
        self._commit(eng, tok, ins, outs)
        self.n_ops += 1
        return tok

    def dma(self, out, in_, queue="sp"):
        s = self.dma_rr[queue]
        self.dma_rr[queue] = (s + 1) % self.NDMA[queue]
        key = ("dma_" + queue, s)
        ins = [in_] if isinstance(in_, V) else []
        outs = [out] if isinstance(out, V) else []
        waits = dict(self._deps(queue, ins, outs))
        prev = self.dma_n[queue][s] * 16
        if prev and self.seen[queue].get(key, 0) < prev:
            waits[key] = prev
            self.seen[queue][key] = prev
        self.dma_n[queue][s] += 1
        tok = (key, self.dma_n[queue][s] * 16)
        oap = out.ap if isinstance(out, V) else out
        iap = in_.ap if isinstance(in_, V) else in_

        def fn(e, oap=oap, iap=iap):
            return e.dma_start(out=oap, in_=iap)

        self.ops[queue].append((fn, list(waits.items()), (key, 16)))
        self._stamp(queue, tok)
        self._commit("dma", tok, ins, outs)
        self.n_ops += 1
        return tok

    def barrier(self):
        toks = [(e, self.cnt[e]) for e in ENGS if self.cnt[e]]
        toks += [(("dma_" + q, i), self.dma_n[q][i] * 16) for q in self.NDMA for i in range(self.NDMA[q]) if self.dma_n[q][i]]
        for e in ENGS:
            waits = {}
            for t in toks:
                if t[0] == e:
                    continue
                self._need(e, waits, t)
            for k, v in waits.items():
                self.seen[e][k] = v
            self.ops[e].append((None, list(waits.items()), None))

    def emit(self):
        nc = self.nc
        with ExitStack() as es:
            sem = {}
            for e in ENGS:
                sem[e] = es.enter_context(nc.semaphore("s_" + e))
            for q in self.NDMA:
                for i in range(self.NDMA[q]):
                    sem[("dma_" + q, i)] = es.enter_context(nc.semaphore("s_dma_%s%d" % (q, i)))
            block = es.enter_context(nc.Block())
            ops = self.ops

            def run(e, name):
                for fn, waits, inc in ops[name]:
                    if fn is None:
                        for k, v in waits:
                            e.wait_ge(sem[k], v)
                        continue
                    embed = EMBED_WAIT and bool(waits) and not isinstance(inc[0], tuple)
                    for k, v in (waits[:-1] if embed else waits):
                        e.wait_ge(sem[k], v)
                    ins = fn(e)
                    if embed:
                        ins._wait_ge(sem[waits[-1][0]], waits[-1][1])
                    ins.then_inc(sem[inc[0]], inc[1])

            @block.tensor
            def _(e):
                run(e, "pe")

            @block.scalar
            def _(e):
                run(e, "act")

            @block.vector
            def _(e):
                run(e, "dve")

            @block.gpsimd
            def _(e):
                run(e, "pool")

            @block.sync
            def _(e):
                run(e, "sp")


class Sched:
    LAT = 0.2

    def __init__(self, prog, enable=True):
        self.P = prog
        self.items = []
        self.enable = enable

    @staticmethod
    def _cols(v):
        n = 1
        for d in v.ap.shape[1:]:
            n *= int(d)
        return n

    def op(self, eng, fn, ins=(), outs=(), aset=None):
        ins = list(ins); outs = list(outs)
        if eng == "pe":
            rhs = ins[1]
            n = self._cols(rhs)
            cost = 0.02 + n * 0.00042 * (4.0 if rhs.ap.dtype == F32 else 1.0)
        elif eng == "act":
            cost = 0.22 + self._cols(outs[0]) * 0.00075
        elif eng == "dve":
            cost = 0.07 + self._cols(outs[0]) * 0.0012
        else:
            cost = 0.15 + self._cols(outs[0]) * 0.0023
        self.items.append(["op", eng, fn, ins, outs, cost, cost, aset])

    def dma(self, out, in_, queue="sp"):
        v = out if isinstance(out, V) else in_
        nbytes = self._cols(v) * int(v.ap.shape[0]) * 4
        lat = 2.0 + nbytes / 200e3
        busy = 1.0 if queue == "pool" else 0.1
        self.items.append(["dma", queue, (out, in_), [in_] if isinstance(in_, V) else [],
                           [out] if isinstance(out, V) else [], busy, lat, None])

    def barrier(self):
        self.items.append(["bar"])

    def _schedule(self, seg):
        n = len(seg)
        last_w, readers, bank, pe_bank = {}, {}, {}, {}
        deps = [set() for _ in range(n)]
        keys_of = lambda x: x.key if isinstance(x.key, tuple) else (x.key,)
        for i, it in enumerate(seg):
            eng = it[1]
            d = deps[i]
            for x in it[3]:
                if not isinstance(x, V):
                    continue
                for k in keys_of(x):
                    if k.startswith("ps"):
                        for e2, j in bank.get(k, {}).items():
                            d.add(j)
                    elif k in last_w:
                        d.add(last_w[k])
            for x in it[4]:
                for k in keys_of(x):
                    if k.startswith("ps"):
                        for e2, j in bank.get(k, {}).items():
                            d.add(j)
                    else:
                        if k in last_w:
                            d.add(last_w[k])
                        d.update(readers.get(k, ()))
            d.discard(i)
            ceng = "dma" if it[0] == "dma" else eng
            for x in it[3]:
                if not isinstance(x, V):
                    continue
                for k in keys_of(x):
                    if k.startswith("ps"):
                        bank.setdefault(k, {})[ceng] = i
                    else:
                        readers.setdefault(k, []).append(i)
            for x in it[4]:
                for k in keys_of(x):
                    if k.startswith("ps"):
                        bank.setdefault(k, {})[ceng] = i
                    else:
                        last_w[k] = i
                        readers[k] = []
        succ = [[] for _ in range(n)]
        for i in range(n):
            for j in deps[i]:
                succ[j].append(i)
        prio = [0.0] * n
        for i in range(n - 1, -1, -1):
            m = 0.0
            for j in succ[i]:
                if prio[j] > m:
                    m = prio[j]
            prio[i] = seg[i][6] + self.LAT + m
        pmax = max(prio) if n else 1.0
        ndep = [len(deps[i]) for i in range(n)]
        ready_t = [0.0] * n
        fin = [0.0] * n
        avail = {}
        cur_set = {}
        cand = set(i for i in range(n) if ndep[i] == 0)
        order = []
        while cand:
            best = None; bkey = None
            for i in cand:
                it = seg[i]
                eng = it[1] if it[0] == "op" else "q_" + it[1]
                s_ = max(avail.get(eng, 0.0), ready_t[i])
                if it[7] is not None and cur_set.get(eng) not in (None, it[7]):
                    s_ += 1.3
                key = (s_ - 0.6 * prio[i] / pmax, i)
                if bkey is None or key < bkey:
                    bkey = key; best = i
            i = best
            cand.discard(i)
            it = seg[i]
            eng = it[1] if it[0] == "op" else "q_" + it[1]
            s_ = max(avail.get(eng, 0.0), ready_t[i])
            if it[7] is not None:
                if cur_set.get(eng) not in (None, it[7]):
                    s_ += 1.3
                cur_set[eng] = it[7]
            avail[eng] = s_ + it[5]
            fin[i] = s_ + it[6]
            order.append(i)
            for j in succ[i]:
                ndep[j] -= 1
                lat = 0.0 if (seg[j][0] == "op" and it[0] == "op" and seg[j][1] == it[1]) else self.LAT
                if fin[i] + lat > ready_t[j]:
                    ready_t[j] = fin[i] + lat
                if ndep[j] == 0:
                    cand.add(j)
        assert len(order) == n
        return order, max(fin) if n else 0.0

    def flush(self):
        seg = []
        tot = 0.0

        def run_seg(seg):
            nonlocal tot
            if not seg:
                return
            if self.enable:
                order, t_end = self._schedule(seg)
                tot += t_end
            else:
                order = range(len(seg))
            for i in order:
                it = seg[i]
                if it[0] == "op":
                    self.P.op(it[1], it[2], it[3], it[4])
                else:
                    self.P.dma(it[2][0], it[2][1], queue=it[1])
        for it in self.items:
            if it[0] == "bar":
                run_seg(seg)
                seg = []
                self.P.barrier()
            else:
                seg.append(it)
        run_seg(seg)
        print("scheduler: simulated total %.1f us" % tot)


class Arena:
    def __init__(self, ap, nwords):
        self.ap = ap
        self.n = nwords
        self.off = 0
        self.uid = 0

    def _take(self, words, name):
        assert self.off + words <= self.n, ("SBUF arena overflow", name, self.off, words, self.n)
        v = self.ap[:, self.off:self.off + words]
        self.off += words
        self.uid += 1
        return v, "%s#%d" % (name, self.uid)

    def f32(self, name, *shape):
        shape = shape[1:]
        w = int(np.prod(shape))
        ap, key = self._take(w, name)
        if len(shape) == 2:
            ap = ap.rearrange("p (a b) -> p a b", a=shape[0])
        elif len(shape) == 3:
            ap = ap.rearrange("p (a b c) -> p a b c", a=shape[0], b=shape[1])
        return V(ap, key)

    def bf16(self, name, *shape):
        shape = shape[1:]
        w = int(np.prod(shape))
        ap, key = self._take((w + 1) // 2, name)
        ap = ap.bitcast(BF16)[:, 0:w]
        if len(shape) == 2:
            ap = ap.rearrange("p (a b) -> p a b", a=shape[0])
        elif len(shape) == 3:
            ap = ap.rearrange("p (a b c) -> p a b c", a=shape[0], b=shape[1])
        return V(ap, key)


def _consts():
    f = np.float32
    lg = np.log(1.0 - 2.0 ** (-5.0 - np.arange(NH, dtype=np.float64)))
    idx = np.arange(128)
    c = {}
    c["IDF"] = np.eye(128, dtype=f)
    c["TRI"] = (idx[:, None] <= idx[None, :]).astype(f)
    c["STRICT"] = (idx[:, None] > idx[None, :]).astype(f)
    c["ONES"] = np.ones((128, 128), f)
    blk = idx // DSEQ
    same = (blk[:, None] == blk[None, :])
    c["TRIS"] = (c["TRI"] * same).astype(f)[:, 0:64]
    c["STRICTS"] = (c["STRICT"] * same).astype(f)[:, 0:64]
    loc = idx % DSEQ
    mr = np.zeros((128, NH, 128), f)
    mrs = np.zeros((128, NH, 128), f)
    for h in range(NH):
        mr[:, h, :] = (np.exp(-lg[h] * (idx[:, None] + 1.0)) * (idx[:, None] <= idx[None, :])).astype(f)
        mrs[:, h, :] = (np.exp(-lg[h] * (loc[:, None] + 1.0)) * (idx[:, None] <= idx[None, :]) * same).astype(f)
    c["MASKR"] = mr.reshape(128, NH * 128)
    c["MASKRS"] = np.ascontiguousarray(mrs[:, :, 0:64]).reshape(128, NH * 64)
    sc = np.zeros((128, 16), f)
    for h in range(NH):
        sc[:, h] = np.exp(lg[h] * (idx + 1.0))
        sc[:, 4 + h] = np.exp(lg[h] * (127.0 - idx))
        sc[:, 8 + h] = np.exp(lg[h] * np.maximum(15.0 - idx, 0))
        sc[:, 12 + h] = np.exp(lg[h] * (3.0 - loc))
    qs = np.zeros((128, 4), f)
    for h in range(NH):
        qs[:, h] = np.exp(lg[h] * (loc + 1.0))
    c["SC"] = sc
    c["QS"] = qs
    c["BMASK"] = (blk[:, None] == np.arange(16)[None, :]).astype(f)
    names = ["IDF", "TRI", "STRICT", "ONES", "TRIS", "STRICTS", "MASKR", "MASKRS", "SC", "QS", "BMASK"]
    offs = {}
    o = 0
    for n in names:
        offs[n] = (o, c[n].shape[1])
        o += c[n].shape[1]
    cst = np.concatenate([c[n] for n in names], axis=1).astype(f)
    cdec = {128: [float(np.exp(lg[h] * 128.0)) for h in range(NH)],
            16: [float(np.exp(lg[h] * 16.0)) for h in range(NH)],
            4: [float(np.exp(lg[h] * 4.0)) for h in range(NH)]}
    return cst, offs, cdec


def _rot_tables():
    half = HD // 2
    inv = (np.float32(10000.0) ** (-np.arange(half, dtype=np.float32) / np.float32(half))).astype(np.float32)
    ntile = NPT + 2
    pos = np.zeros((ntile, 128), np.float32)
    pos[0, :NMETA] = np.arange(NMETA)
    for t in range(NPT):
        pos[1 + t] = NMETA + t * 128 + np.arange(128)
    pos[NPT + 1, :NSAMP * DSEQ] = PAST + (np.arange(NSAMP * DSEQ) % DSEQ)
    ang = pos[:, :, None].astype(np.float32) * inv[None, None, :]
    cos = np.cos(ang).astype(np.float32)
    sin = np.sin(ang).astype(np.float32)
    ks = np.float32(HD ** -0.5)
    return np.concatenate([cos, cos * ks, sin, sin * ks], axis=2).astype(np.float32)


CST, COFF, CDEC = _consts()
ROT = _rot_tables()
NCST = CST.shape[1]
ROWP_N = 6 * D + 512 + 8


def build(dbg=False):
    nc = bass.Bass("TRN2", target_bir_lowering=False)
    di = lambda n, s: nc.dram_tensor(n, list(s), F32, kind="ExternalInput").ap()
    do = lambda n, s: nc.dram_tensor(n, list(s), F32, kind="ExternalOutput").ap()
    xp = di("xp", (SEQ, D)); xm = di("xm", (NMETA, D)); xs = di("xs", (NSAMP * DSEQ, D))
    sret = di("sret", (NSAMP, NH, HD, HD)); sgdn = di("sgdn", (NSAMP, NH, HD, HD))
    sconv = di("sconv", (NSAMP * 3, 1536))
    w_in = di("w_in", (D, INC)); w_out = di("w_out", (D, D))
    w_gu = di("w_gu", (D, 2 * DFF)); w_dn = di("w_dn", (DFF, D))
    rowp = di("rowp", (1, ROWP_N)); colp = di("colp", (128, 48))
    cst = di("cst", (128, NCST)); rot = di("rot", (NPT + 2, 128, 256))
    y_p = do("y_p", (SEQ, D)); y_s = do("y_s", (NSAMP * DSEQ, D))
    ret_p = do("ret_p", (NH, HD, HD)); gdn_p = do("gdn_p", (NH, HD, HD)); conv_p = do("conv_p", (3, 1536))
    ret_s = do("ret_s", (NSAMP, NH, HD, HD)); gdn_s = do("gdn_s", (NSAMP, NH, HD, HD))
    conv_s = do("conv_s", (NSAMP * 3, 1536))

    NW = 53200
    with ExitStack() as es:
        arena_t = es.enter_context(nc.sbuf_tensor("arena", [128, NW], F32))
        psb = [es.enter_context(nc.psum_tensor("psb%d" % i, [128, 512], F32)) for i in range(8)]
        PS = [V(psb[i][:], "ps%d" % i) for i in range(8)]
        P0 = Prog(nc)
        P = Sched(P0, enable=SCHED_ENABLE)
        A = Arena(arena_t[:], NW)

        def mm(out, lhsT, rhs, start=True, stop=True):
            P.op("pe", lambda e: e.matmul(out.ap, lhsT=lhsT.ap, rhs=rhs.ap, start=start, stop=stop),
                 ins=[lhsT, rhs], outs=[out])

        def act(out, in_, func, bias=0.0, scale=1.0, accum=None, eng="act"):
            ins = [in_] + [x for x in (bias, scale) if isinstance(x, V)]
            outs = [out] + ([accum] if accum is not None else [])
            b = bias.ap if isinstance(bias, V) else bias
            s = scale.ap if isinstance(scale, V) else scale
            a = accum.ap if accum is not None else None
            if accum is not None:
                ins.append(accum)
            aset = "silu" if func == AF.Silu else ("lnexp" if func in (AF.Exp, AF.Ln) else None)
            P.op("act", lambda e: e.activation(out=out.ap, in_=in_.ap, func=func, bias=b, scale=s, accum_out=a)
                 if a is not None else e.activation(out=out.ap, in_=in_.ap, func=func, bias=b, scale=s),
                 ins=ins, outs=outs, aset=aset)

        def tt(eng, out, a, b, op):
            P.op(eng, lambda e: e.tensor_tensor(out=out.ap, in0=a.ap, in1=b.ap, op=op), ins=[a, b], outs=[out])

        def ts(eng, out, a, s1, op0, s2=None, op1=None):
            ins = [a] + [x for x in (s1, s2) if isinstance(x, V)]
            v1 = s1.ap if isinstance(s1, V) else s1
            v2 = s2.ap if isinstance(s2, V) else s2
            if op1 is None:
                P.op(eng, lambda e: e.tensor_scalar(out=out.ap, in0=a.ap, scalar1=v1, scalar2=None, op0=op0),
                     ins=ins, outs=[out])
            else:
                P.op(eng, lambda e: e.tensor_scalar(out=out.ap, in0=a.ap, scalar1=v1, scalar2=v2, op0=op0, op1=op1),
                     ins=ins, outs=[out])

        def stt(eng, out, a, scalar, b, op0, op1):
            ins = [a, b] + ([scalar] if isinstance(scalar, V) else [])
            sv = scalar.ap if isinstance(scalar, V) else scalar
            P.op(eng, lambda e: e.scalar_tensor_tensor(out=out.ap, in0=a.ap, scalar=sv, in1=b.ap, op0=op0, op1=op1),
                 ins=ins, outs=[out])

        def cp(eng, out, in_):
            if eng == "act":
                P.op("act", lambda e: e.copy(out=out.ap, in_=in_.ap), ins=[in_], outs=[out])
            else:
                P.op(eng, lambda e: e.tensor_copy(out=out.ap, in_=in_.ap), ins=[in_], outs=[out])

        def memset(eng, out, val):
            P.op(eng, lambda e: e.memset(out.ap, val), ins=[], outs=[out])

        def recip(out, in_):
            P.op("dve", lambda e: e.reciprocal(out=out.ap, in_=in_.ap), ins=[in_], outs=[out])

        def rstd_from(out, ssq, scale, eps, tmp):
            act(tmp, ssq, AF.Ln, bias=eps, scale=scale)
            act(out, tmp, AF.Exp, scale=-0.5)

        H1 = A.bf16("H1", 128, NPT + 1, D)
        mark_persist = A.off

        WIN_Q = A.bf16("WINQ", 128, 8, 1536)
        WIN_G = A.bf16("WING", 128, 8, 512)
        WIN_X = A.bf16("WINX", 128, 8, 1536)
        WIN_Z = A.bf16("WINZ", 128, 8, 512)
        WIN_AB = A.bf16("WINAB", 128, 8, 8)
        _wq = WIN_Q.re("p k n -> p (k n)")
        WG0 = _wq[:, 0:4096].re("p (k n) -> p k n", k=8)
        WU0 = _wq[:, 4096:8192].re("p (k n) -> p k n", k=8)
        WD0 = _wq[:, 8192:12288].re("p (c n) -> p c n", c=4)
        _wgrp = [(WIN_AB, 4096, 8), (WIN_Q, 0, 1536), (WIN_X, 2048, 1536), (WIN_G, 1536, 512), (WIN_Z, 3584, 512)]

        def WIN_cols(k, c0, n):
            for buf, b0, bn in _wgrp:
                if b0 <= c0 and c0 + n <= b0 + bn:
                    return buf[:, k, c0 - b0:c0 - b0 + n]
            raise AssertionError((c0, n))
        WOUT = A.bf16("WOUT", 128, 8, D)
        CS = A.f32("CST", 128, NCST)
        RP = A.f32("ROWP", 128, 2 * D + 128 + 8)
        CW = A.f32("CW", 128, 12, 4)
        IDB = A.bf16("IDB", 128, 128)

        def C(name, rows=slice(0, 128)):
            o, n = COFF[name]
            return CS[rows, o:o + n]

        P.dma(CS, cst[:, :])
        P.dma(RP[:, 0:2 * D], rowp[:, 0:2 * D].partition_broadcast(128))
        P.dma(RP[:, 2 * D:2 * D + 128], rowp[:, 6 * D:6 * D + 128].partition_broadcast(128))
        P.dma(RP[:, 2 * D + 128:2 * D + 136], rowp[:, 6 * D + 512:6 * D + 520].partition_broadcast(128))
        P.dma(CW, colp.rearrange("p (c w) -> p c w", c=12))
        for buf, b0, bn in _wgrp:
            P.dma(buf, w_in[:, b0:b0 + bn].rearrange("(k p) n -> p k n", p=128), queue="pool")
        P.dma(WOUT, w_out.rearrange("(k p) n -> p k n", p=128), queue="pool")
        cp("dve", IDB, C("IDF"))
        G0 = RP[:, 0:D]; B0 = RP[:, D:2 * D]
        GNW = RP[:, 2 * D:2 * D + 128].bc(1, (128, 4, 128))
        DTB = RP[:, 2 * D + 128:2 * D + 132]
        NEGA = A.f32("NEGA", 128, 4)
        act(NEGA, RP[:, 2 * D + 132:2 * D + 136], AF.Exp)
        ts("dve", NEGA, NEGA, -1.0, ALU.mult)

        XT2 = [A.f32("XT0", 128, D), A.f32("XT1", 128, D)]
        _o_HB = A.off
        HB = A.bf16("HB", 128, D)
        HT = A.bf16("HT", 128, 8, 128)
        RT = A.f32("RT", 128, 256)
        _o_QK = [A.off, A.off + 512]
        QK2 = [A.bf16("QK0", 128, 2, 512), A.bf16("QK1", 128, 2, 512)]
        VV2 = [A.bf16("VV0", 128, 512), A.bf16("VV1", 128, 512)]
        SGT2 = [A.bf16("SGT0", 128, 512), A.bf16("SGT1", 128, 512)]
        SGZ2 = [A.bf16("SGZ0", 128, 512), A.bf16("SGZ1", 128, 512)]
        GAB = A.f32("GAB", 128, 8)
        XC = A.f32("XC", 128, 12, 131)
        ACC = A.f32("ACC", 128, 4, 128)
        T1 = ACC.re("p a b -> p (a b)")
        T2 = T1
        CV = A.bf16("CV", 128, 12, 128)
        ST = A.f32("ST", 128, 12)
        MV = A.f32("MV", 128, 2)
        SM8 = A.f32("SM8", 128, 8)
        JUNK = A.bf16("JUNK", 128, 128)
        MIX = A.bf16("MIX", 128, D)
        QDT = A.bf16("QDT", 128, 4, 128)
        KT = A.bf16("KT", 128, 4, 128)
        SMK = A.bf16("SMK", 128, 4, 128)
        SSO = A.f32("SSO", 128, 8)
        RSO = A.f32("RSO", 128, 8)
        RSO2 = A.f32("RSO2", 128, 8)
        GG = A.f32("GG", 128, 4)
        BETA = A.f32("BETA", 128, 4)
        NBETA = A.f32("NBETA", 128, 4)
        GSH = A.f32("GSH", 128, 2, 512)
        GSH0 = V(GSH.ap[:, 0, :], GSH.key + "/0")
        GSH1 = V(GSH.ap[:, 1, :], GSH.key + "/1")
        GTRI = GSH0.re("p (h i) -> p h i", h=4)
        TMPF = GTRI
        DIFF = GSH1.re("p (h i) -> p h i", h=4)
        GPP = A.f32("GPP", 128, 8)
        FF = A.bf16("FF", 128, 4, 128)
        D2M = A.bf16("D2M", 128, 4, 128)
        GAMBC = A.bf16("GAMBC", 128, 4, 128)
        GAMPP = A.f32("GAMPP", 128, 4)
        KTSC = A.f32("KTSC", 128, 4)
        CDV = A.f32("CDV", 128, 4)
        SSQ = A.f32("SSQ", 128, 8)
        RS = A.f32("RS", 128, 8)
        SCL = A.f32("SCL", 128, 16)
        TG = A.f32("TG", 128, 8)
        QN = A.bf16("QN", 128, 4, 128); KN = A.bf16("KN", 128, 4, 128); KBG = A.bf16("KBG", 128, 4, 128)
        KTS = A.bf16("KTS", 128, 4, 128); VB = A.bf16("VB", 128, 4, 128)
        KQT = A.bf16("KQT", 128, 2, 4, 128)
        MIXT = KQT.re("p a h i -> p (a h) i")
        QGT = A.bf16("QGT", 128, 4, 128)
        YB = A.bf16("YB", 128, 4, 128)
        MB = A.bf16("MB", 128, 4, 128)
        QQ = A.bf16("QQ", 128, 4, 128)
        ATT = A.bf16("ATT", 128, 4, 128)
        NWK = A.bf16("NWK", 128, 4, 128)
        UU = A.bf16("UU", 128, 4, 128)
        _o_SR = A.off
        SR = A.f32("SR", 128, 4, 128)
        _o_SG = A.off
        SGs = A.f32("SGs", 128, 4, 128)
        SRB = A.bf16("SRB", 128, 4, 128)
        SGB = A.bf16("SGB", 128, 4, 128)
        memset("pool", SR, 0.0); memset("pool", SGs, 0.0)
        memset("pool", SRB, 0.0); memset("pool", SGB, 0.0)
        memset("pool", XC, 0.0)
        memset("dve", SSO, 0.0)
        memset("dve", SSQ, 0.0)
        NS = NSAMP
        CS_ = NS * DSEQ
        SSq = [A.f32("SSq%d" % q, 128, 4, 128) for q in range(4)]
        _psamp0 = ((NPT if not dbg else dbg) + 1) % 2

        def _alias512(off, owner):
            return V(arena_t[:, off:off + 512].rearrange("p (s e) -> p s e", s=4), owner.key)
        SSq_b = [_alias512(_o_SR, SR), _alias512(_o_SG, SGs), _alias512(_o_QK[1 - _psamp0], QK2[1 - _psamp0]),
                 _alias512(_o_HB, HB)]
        SS_sets = [SSq, SSq_b]
        ss_unit = [0]
        _psamp = ((NPT if not dbg else dbg) + 1) % 2
        _xo = XT2[1 - _psamp]
        SSB = V(_xo.ap.bitcast(BF16)[:, 0:NS * 128].rearrange("p (s e) -> p s e", s=NS), _xo.key)
        ZA = A.bf16("ZA", 128, NS * 68)
        KM = A.bf16("KM", 128, NS // 4, 128)
        CDVS = A.f32("CDVS", 128, 4, NS)
        memset("pool", ZA, 0.0)
        print("phase A arena words", A.off)

        out_toks = []
        F0, F1, F2 = PS[0], PS[1], PS[2]
        R0, R1 = PS[3], PS[4]
        GA, GB, GC = PS[5], PS[6], PS[7]

        def v4(x, Ct, w=None):
            w = Ct if w is None else w
            return x.re("p (h i) -> p h i", h=4)[:, :, 0:w]

        def chain_Fa(kind, ti, Ct, rot_i, p):
            XT = XT2[p]; QK = QK2[p]; VV = VV2[p]
            st = []

            def s_load():
                if kind == "meta":
                    memset("pool", XT, 0.0)
                    P.dma(XT[0:NMETA, :], xm[:, :])
                elif kind == "p":
                    P.dma(XT, xp[ti * 128:(ti + 1) * 128, :])
                else:
                    memset("pool", XT, 0.0)
                    P.dma(XT[0:CS_, :], xs[:, :])
                P.dma(RT, rot[rot_i, :, :])
            st.append(s_load)

            def s_ln_a():
                for j in range(2):
                    P.op("dve", lambda e, j=j: e.bn_stats(out=ST.ap[:, j * 6:(j + 1) * 6], in_=XT.ap[:, j * 512:(j + 1) * 512]),
                         ins=[XT], outs=[ST])
                P.op("dve", lambda e: e.bn_aggr(out=MV.ap, in_=ST.ap.rearrange("p (a b) -> p a b", a=2)), ins=[ST], outs=[MV])
                rstd_from(SM8[:, 0:1], MV[:, 1:2], 1.0, LN_EPS, SM8[:, 1:2])
                stt("dve", SM8[:, 2:3], MV[:, 0:1], -1.0, SM8[:, 0:1], ALU.mult, ALU.mult)
            st.append(s_ln_a)

            def s_ln_b():
                act(XT, XT, AF.Identity, bias=SM8[:, 2:3], scale=SM8[:, 0:1])
                tt("dve", XT, XT, G0, ALU.mult)
            st.append(s_ln_b)

            def s_ln_c():
                tt("pool", XT, XT, B0, ALU.add)
                cp("act", HB, XT)
            st.append(s_ln_c)

            def s_tr(half):
                bank = F0 if half == 0 else F1
                for k in range(4 * half, 4 * half + 4):
                    mm(bank[:, (k % 4) * 128:(k % 4 + 1) * 128], HB[:, k * 128:(k + 1) * 128], IDB)
                cp("act", HT[:, 4 * half:4 * half + 4, :], bank.re("p (a b) -> p a b", a=4))
            st.append(lambda: s_tr(0))
            st.append(lambda: s_tr(1))

            def s_proj(bank, c0, half):
                for k in range(4 * half, 4 * half + 4):
                    mm(bank, HT[:, k, :], WIN_cols(k, c0, 512), start=(k == 0), stop=(k == 7))
            st.append(lambda: s_proj(F0, 0, 0))
            st.append(lambda: s_proj(F0, 0, 1))
            st.append(lambda: s_proj(F1, 512, 0))
            st.append(lambda: s_proj(F1, 512, 1))

            def s_rot(qi, bank, part):
                xv = bank.re("p (h t f) -> p h t f", h=4, t=2)
                x1 = xv[:, :, 0, :]; x2 = xv[:, :, 1, :]
                cosv = RT[:, qi * 64:(qi + 1) * 64].bc(1, (128, 4, 64))
                sinv = RT[:, 128 + qi * 64:128 + (qi + 1) * 64].bc(1, (128, 4, 64))
                ov = QK[:, qi, :].re("p (h t f) -> p h t f", h=4, t=2)
                if part == 0:
                    t1 = T2[:, 0:256].re("p (h f) -> p h f", h=4); t2 = T2[:, 256:512].re("p (h f) -> p h f", h=4)
                    tt("dve", t1, x1, cosv, ALU.mult)
                    tt("dve", t2, x2, sinv, ALU.mult)
                    tt("dve", ov[:, :, 0, :], t1, t2, ALU.subtract)
                else:
                    t3 = T2[:, 0:256].re("p (h f) -> p h f", h=4); t4 = T2[:, 256:512].re("p (h f) -> p h f", h=4)
                    tt("dve", t3, x1, sinv, ALU.mult)
                    tt("dve", t4, x2, cosv, ALU.mult)
                    tt("dve", ov[:, :, 1, :], t3, t4, ALU.add)
            st.append(lambda: s_rot(0, F0, 0))
            st.append(lambda: s_rot(0, F0, 1))
            st.append(lambda: s_proj(F2, 1024, 0))
            st.append(lambda: s_proj(F2, 1024, 1))
            st.append(lambda: s_rot(1, F1, 0))
            st.append(lambda: s_rot(1, F1, 1))
            st.append(lambda: cp("act", VV, F2))
            return st

        def chain_Fb(kind, Ct, p):
            SGT = SGT2[p]; SGZ = SGZ2[p]
            st = []

            def s_gate(bank, c0, dst, half):
                for k in range(4 * half, 4 * half + 4):
                    mm(bank, HT[:, k, :], WIN_cols(k, c0, 512), start=(k == 0), stop=(k == 7))
                if half == 1:
                    cp("act", dst, bank)
            st.append(lambda: s_gate(F0, 1536, SGT, 0))
            st.append(lambda: s_gate(F0, 1536, SGT, 1))
            st.append(lambda: s_gate(F1, 3584, SGZ, 0))
            st.append(lambda: s_gate(F1, 3584, SGZ, 1))

            def s_gab():
                for k in range(8):
                    mm(F2[:, 0:8], HT[:, k, :], WIN_cols(k, 4096, 8), start=(k == 0), stop=(k == 7))
                cp("dve", GAB, F2[:, 0:8])
            st.append(s_gab)

            def s_gq(g3, half=None):
                bank = PS[g3]
                cr = range(4 * g3, 4 * g3 + 4) if half is None else range(4 * g3 + 2 * half, 4 * g3 + 2 * half + 2)
                for c in cr:
                    ob = bank[:, (c % 4) * 128:(c % 4) * 128 + 128]
                    for k in range(8):
                        mm(ob, WIN_cols(k, 2048 + c * 128, 128), HT[:, k, :], start=(k == 0), stop=(k == 7))
            if kind != "s":
                def s_conv(g3, w):
                    cs = slice(4 * g3, 4 * g3 + 4)
                    b3 = (128, 4, Ct)
                    acc = ACC[:, :, 0:Ct]
                    tmp = (GSH0 if w % 2 else GSH1).re("p (c t) -> p c t", c=4)[:, :, 0:Ct]
                    if w == 0:
                        cp("act", XC[:, cs, 3:3 + Ct], PS[g3].re("p (a b) -> p a b", a=4)[:, :, 0:Ct])
                        tt("dve", acc, XC[:, cs, 0:Ct], CW[:, cs, 0].bc(2, b3), ALU.mult)
                    else:
                        tt("pool", tmp, XC[:, cs, w:w + Ct], CW[:, cs, w].bc(2, b3), ALU.mult)
                        tt("dve", CV[:, cs, 0:Ct] if w == 3 else acc, acc, tmp, ALU.add)
                    if w == 3:
                        cp("pool", XC[:, cs, 0:3], XC[:, cs, Ct:Ct + 3])
                for g3 in range(3):
                    st.append(lambda g3=g3: s_gq(g3, 0))
                    st.append(lambda g3=g3: s_gq(g3, 1))
                    for w in range(4):
                        st.append(lambda g3=g3, w=w: s_conv(g3, w))
            else:
                for g3 in range(3):
                    st.append(lambda g3=g3: s_gq(g3))
                st.append(conv_sample)

            def s_silu(i):
                if i == 0:
                    act(SGT, SGT, AF.Silu)
                    act(SGZ, SGZ, AF.Silu)
                    d3 = SGZ.re("p (h i) -> p h i", h=4)
                    tt("dve", d3, d3, GNW, ALU.mult)
                elif kind != "s":
                    act(CV[:, :, 0:Ct], CV[:, :, 0:Ct], AF.Silu)
            st.append(lambda: s_silu(0))
            st.append(lambda: s_silu(1))
            return st

        def conv_sample():
            XCs = XC[:, :, 0:NS * 7].re("p c (s w) -> p c s w", s=NS)
            SCB = GSH0[0:NS * 3, :]
            for g3 in range(3):
                cs = slice(4 * g3, 4 * g3 + 4)
                cp("act", XCs[:, cs, :, 3:7],
                   PS[g3].re("p (a b) -> p a b", a=4)[:, :, 0:CS_].re("p a (s c) -> p a s c", s=NS))
            for g3 in range(3):
                cs = slice(4 * g3, 4 * g3 + 4)
                P.dma(SCB, sconv[:, g3 * 512:(g3 + 1) * 512])
                for c in range(4):
                    mm(R0[:, c * 128:c * 128 + NS * 3], SCB[:, c * 128:(c + 1) * 128], C("IDF")[0:NS * 3, 0:NS * 3])
                cp("dve", XCs[:, cs, :, 0:3],
                   R0.re("p (a b) -> p a b", a=4)[:, :, 0:NS * 3].re("p a (s r) -> p a s r", s=NS))
            for g3 in range(3):
                cs = slice(4 * g3, 4 * g3 + 4)
                ACCs = ACC[:, :, 0:CS_].re("p c (s k) -> p c s k", s=NS)
                tmp4 = GSH0[:, 0:4 * CS_].re("p (c s k) -> p c s k", c=4, s=NS)
                b4 = (128, 4, NS, DSEQ)

                def cwb(w):
                    return CW[:, cs, w].bc(2, (128, 4, NS)).bc(3, b4)
                tt("dve", ACCs, XCs[:, cs, :, 0:4], cwb(0), ALU.mult)
                for w in range(1, 4):
                    tt("dve", tmp4, XCs[:, cs, :, w:w + 4], cwb(w), ALU.mult)
                    tt("dve", ACCs, ACCs, tmp4, ALU.add)
                act(CV[:, cs, 0:CS_], ACC[:, :, 0:CS_], AF.Silu)
            for g3 in range(3):
                cs = slice(4 * g3, 4 * g3 + 4)
                X48 = T1[:, 0:4 * NS * 3].re("p (c r) -> p c r", c=4)
                cp("pool", X48.re("p c (s r) -> p c s r", s=NS), XCs[:, cs, :, 4:7])
                for c in range(4):
                    mm(R0[0:NS * 3, c * 128:(c + 1) * 128], X48[:, c, :], C("IDF"))
                cp("dve", SCB, R0[0:NS * 3, :])
                out_toks.append(P.dma(conv_s[:, g3 * 512:(g3 + 1) * 512], SCB))

        def chain_R(Ct, p, kds_off, cd, samp):
            QK = QK2[p]; VV = VV2[p]; SGT = SGT2[p]
            idc = IDB[0:Ct, 0:Ct]
            q3 = QK[0:Ct, 0, :].re("c (h d) -> c h d", h=4)
            k3 = QK[0:Ct, 1, :].re("c (h d) -> c h d", h=4)
            st = []

            def s1():
                qsc = (C("QS") if samp else C("SC")[:, 0:4])[0:Ct, :]
                tt("dve", q3, q3, qsc.bc(2, (Ct, 4, 128)), ALU.mult)
                for h in range(NH):
                    mm(R0[:, h * 128:h * 128 + Ct], q3[:, h, :], idc)
                cp("act", QDT[:, :, 0:Ct], v4(R0, Ct))
            st.append(s1)

            def s2():
                for h in range(NH):
                    mm(R1[:, h * 128:h * 128 + Ct], k3[:, h, :], idc)
                cp("act", KT[:, :, 0:Ct], v4(R1, Ct))
                tt("dve", k3, k3, C("SC")[0:Ct, kds_off:kds_off + 4].bc(2, (Ct, 4, 128)), ALU.mult)
            st.append(s2)

            def s3():
                for h in range(NH):
                    mm(R0[0:Ct, h * 128:h * 128 + Ct], KT[:, h, 0:Ct], QDT[:, h, 0:Ct])
                mk = (C("MASKRS").re("p (h i) -> p h i", h=4) if samp else C("MASKR").re("p (h i) -> p h i", h=4))[0:Ct, :, 0:Ct]
                tt("dve", SMK[0:Ct, :, 0:Ct], v4(R0, Ct)[0:Ct], mk, ALU.mult)
            st.append(s3)

            if not samp:
                def s4():
                    for h in range(NH):
                        hs = slice(h * 128, (h + 1) * 128)
                        mm(R1[0:Ct, hs], SMK[0:Ct, h, 0:Ct], VV[0:Ct, hs], start=True, stop=False)
                        mm(R1[0:Ct, hs], QDT[:, h, 0:Ct], SRB[:, h, :], start=False, stop=True)
                    for h in range(NH):
                        hs = slice(h * 128, (h + 1) * 128)
                        mm(R0[:, hs], k3[:, h, :], VV[0:Ct, hs])
                st.append(s4)

                def s5():
                    for h in range(NH):
                        hs = slice(h * 128, (h + 1) * 128)
                        stt("dve", SR[:, h, :], SR[:, h, :], cd[h], R0[:, hs], ALU.mult, ALU.add)
                    cp("act", SRB, SR)
                st.append(s5)
            else:
                for h in range(NH):
                    st.append(lambda h=h: sample_state_ret(h, Ct, cd, k3, VV))

            def s6():
                for h in range(NH):
                    hs = slice(h * 128, (h + 1) * 128)
                    act(JUNK[0:Ct, :], R1[0:Ct, hs], AF.Square, accum=SSO[0:Ct, h:h + 1])
                rstd_from(RSO[0:Ct, 0:4], SSO[0:Ct, 0:4], 1.0 / HD, RMS_EPS, RSO2[0:Ct, 0:4])
                for h in range(NH):
                    hs = slice(h * 128, (h + 1) * 128)
                    stt("dve", MIX[0:Ct, hs], R1[0:Ct, hs], RSO[0:Ct, h:h + 1], SGT[0:Ct, hs], ALU.mult, ALU.mult)
                memset("dve", SSO[:, 0:4], 0.0)
            st.append(s6)
            return st

        def zfill(Z, src):
            cp("dve", Z.re("p (s w) -> p s w", s=NS)[:, :, 0:DSEQ], src.re("p (s c) -> p s c", s=NS))

        def load_states(src, h):
            cur = SS_sets[ss_unit[0] % 2]
            ss_unit[0] += 1
            for q in range(4):
                P.dma(cur[q], src[4 * q:4 * q + 4, h, :, :].rearrange("s d e -> d s e"))
                cp("act", SSB[:, 4 * q:4 * q + 4, :], cur[q])
            return cur

        def sample_state_ret(h, Ct, cd, k3, VV):
            hs = slice(h * 128, (h + 1) * 128)
            SSc = load_states(sret, h)
            zfill(ZA, QDT[:, h, 0:Ct])
            mm(R1[0:Ct, hs], SMK[0:Ct, h, 0:Ct], VV[0:Ct, hs], start=True, stop=False)
            for s_ in range(NS):
                mm(R1[0:Ct, hs], ZA[:, s_ * 64:(s_ + 1) * 64], SSB[:, s_, :], start=False, stop=(s_ == NS - 1))
            for q4 in range(4):
                bank = PS[q4 % 2]
                tt("dve", KM[0:Ct, :, :], k3[:, h, :].bc(1, (Ct, 4, 128)),
                   C("BMASK")[0:Ct, q4 * 4:(q4 + 1) * 4].bc(2, (Ct, 4, 128)), ALU.mult)
                for j in range(4):
                    mm(bank[:, j * 128:(j + 1) * 128], KM[0:Ct, j, :], VV[0:Ct, hs])
                v = SSc[q4]
                stt("dve", v, v, cd[h], bank.re("p (a b) -> p a b", a=4), ALU.mult, ALU.add)
                out_toks.append(P.dma(ret_s[4 * q4:4 * q4 + 4, h, :, :].rearrange("s d e -> d s e"), v, queue="pool"))

        def sample_state_gdn(h, Ct):
            hs = slice(h * 128, (h + 1) * 128)
            Qf = QQ[0:Ct, h, 0:Ct]
            SSc = load_states(sgdn, h)
            zfill(ZA, NWK[:, h, 0:Ct])
            mm(GA[0:Ct, hs], Qf, VB[0:Ct, h, :], start=True, stop=False)
            for s_ in range(NS):
                mm(GA[0:Ct, hs], ZA[:, s_ * 64:(s_ + 1) * 64], SSB[:, s_, :], start=False, stop=(s_ == NS - 1))
            cp("act", UU[0:Ct, h, :], GA[0:Ct, hs])
            zfill(ZA, QGT[:, h, 0:Ct])
            for s_ in range(NS):
                mm(GB[0:Ct, hs], ZA[:, s_ * 64:(s_ + 1) * 64], SSB[:, s_, :], start=(s_ == 0), stop=False)
            mm(GB[0:Ct, hs], ATT[0:Ct, h, 0:Ct], UU[0:Ct, h, :], start=False, stop=True)
            for q4 in range(4):
                bank = PS[q4 % 2]
                v = SSc[q4]
                tt("dve", v, v, CDVS[:, h, 4 * q4:4 * q4 + 4].bc(2, (128, 4, 128)), ALU.mult)
                tt("dve", KM[0:Ct, :, :], KTS[0:Ct, h, :].bc(1, (Ct, 4, 128)),
                   C("BMASK")[0:Ct, q4 * 4:(q4 + 1) * 4].bc(2, (Ct, 4, 128)), ALU.mult)
                for j in range(4):
                    mm(bank[:, j * 128:(j + 1) * 128], KM[0:Ct, j, :], UU[0:Ct, h, :])
                tt("dve", v, v, bank.re("p (a b) -> p a b", a=4), ALU.add)
                out_toks.append(P.dma(gdn_s[4 * q4:4 * q4 + 4, h, :, :].rearrange("s d e -> d s e"), v, queue="pool"))

        def chain_G(Ct, nlev, samp, p):
            SGZ = SGZ2[p]
            idc = IDB[0:Ct, 0:Ct]
            tri = (C("TRIS") if samp else C("TRI"))[0:Ct, 0:Ct]
            strict = (C("STRICTS") if samp else C("STRICT"))[0:Ct, 0:Ct]
            sh = (Ct, 4, Ct)
            st = []

            def g1():
                tt("dve", TG[0:Ct, 0:4], GAB[0:Ct, 0:4], DTB[0:Ct, :], ALU.add)
                act(TG[0:Ct, 0:4], TG[0:Ct, 0:4], AF.Exp)
                act(TG[0:Ct, 0:4], TG[0:Ct, 0:4], AF.Ln, bias=1.0)
                tt("dve", GG[0:Ct, :], TG[0:Ct, 0:4], NEGA[0:Ct, :], ALU.mult)
                act(TG[0:Ct, 4:8], GAB[0:Ct, 4:8], AF.Exp, scale=-1.0)
                ts("dve", TG[0:Ct, 4:8], TG[0:Ct, 4:8], 1.0, ALU.add)
                recip(BETA[0:Ct, :], TG[0:Ct, 4:8])
                ts("dve", NBETA[0:Ct, :], BETA[0:Ct, :], -1.0, ALU.mult)
            st.append(g1)

            def g2():
                tt("dve", GTRI[0:Ct, :, 0:Ct], tri.bc(1, sh), GG[0:Ct, :].bc(2, sh), ALU.mult)
                mm(GA[0:Ct, 0:4], tri, GG[0:Ct, :])
                mm(GA[0:Ct, 4:8], strict, GG[0:Ct, :])
                cp("dve", GPP[0:Ct, :], GA[0:Ct, 0:8])
                for h in range(NH):
                    mm(GB[:, h * 128:h * 128 + Ct], C("ONES")[0:Ct, :], GTRI[0:Ct, h, 0:Ct])
            st.append(g2)
            gbc = GB.re("p (h i) -> p h i", h=4)

            def g3():
                tt("dve", DIFF[0:Ct, :, 0:Ct], gbc[0:Ct, :, 0:Ct], GPP[0:Ct, 0:4].bc(2, sh), ALU.subtract)
                act(GAMBC[:, :, 0:Ct], gbc[:, :, 0:Ct], AF.Exp)
                if not samp:
                    act(CDV, gbc[:, :, Ct - 1], AF.Exp)
                else:
                    act(CDVS, gbc[:, :, 0:Ct].re("p h (s c) -> p h s c", c=DSEQ)[:, :, :, DSEQ - 1], AF.Exp)
                act(GAMPP[0:Ct, :], GPP[0:Ct, 0:4], AF.Exp)
                act(KTSC[0:Ct, :], GPP[0:Ct, 4:8], AF.Exp)
            st.append(g3)

            def g4():
                ts("dve", TMPF[0:Ct, :, 0:Ct], DIFF[0:Ct, :, 0:Ct], 0.0, ALU.min)
                act(D2M[0:Ct, :, 0:Ct], TMPF[0:Ct, :, 0:Ct], AF.Exp)
                tt("dve", D2M[0:Ct, :, 0:Ct], D2M[0:Ct, :, 0:Ct], tri.bc(1, sh), ALU.mult)
            st.append(g4)

            def g5():
                ts("dve", TMPF[0:Ct, :, 0:Ct], DIFF[0:Ct, :, 0:Ct], -1.0, ALU.mult, 0.0, ALU.min)
                act(FF[0:Ct, :, 0:Ct], TMPF[0:Ct, :, 0:Ct], AF.Exp)
                tt("dve", FF[0:Ct, :, 0:Ct], FF[0:Ct, :, 0:Ct], NBETA[0:Ct, :].bc(2, sh), ALU.mult)
                tt("dve", FF[0:Ct, :, 0:Ct], FF[0:Ct, :, 0:Ct], strict.bc(1, sh), ALU.mult)
            st.append(g5)

            def m1():
                for j, bank in enumerate((GA, GB, GC)):
                    for h in range(NH):
                        mm(bank[0:Ct, h * 128:(h + 1) * 128], CV[:, j * 4 + h, 0:Ct], IDB)
                for h in range(NH):
                    hs = slice(h * 128, (h + 1) * 128)
                    act(JUNK[0:Ct, :], GA[0:Ct, hs], AF.Square, accum=SSQ[0:Ct, h:h + 1])
                    act(JUNK[0:Ct, :], GB[0:Ct, hs], AF.Square, accum=SSQ[0:Ct, 4 + h:5 + h])
                rstd_from(RS[0:Ct, :], SSQ[0:Ct, :], 1.0, RMS_EPS, SCL[0:Ct, 8:16])
                memset("dve", SSQ, 0.0)
                ts("dve", SCL[0:Ct, 0:4], RS[0:Ct, 0:4], float(HD ** -0.5), ALU.mult)
                tt("dve", SCL[0:Ct, 4:8], RS[0:Ct, 4:8], KTSC[0:Ct, :], ALU.mult)
                tt("dve", SCL[0:Ct, 8:12], RS[0:Ct, 4:8], BETA[0:Ct, :], ALU.mult)
                tt("dve", SCL[0:Ct, 8:12], SCL[0:Ct, 8:12], GAMPP[0:Ct, :], ALU.mult)
            st.append(m1)

            def m1b():
                b3 = (Ct, 4, 128)
                for h in range(NH):
                    hs = slice(h * 128, (h + 1) * 128)
                    act(QN[0:Ct, h, :], GA[0:Ct, hs], AF.Copy, scale=SCL[0:Ct, h:h + 1])
                tt("dve", KN[0:Ct], v4(GB, Ct, 128)[0:Ct], RS[0:Ct, 4:8].bc(2, b3), ALU.mult)
                tt("dve", KBG[0:Ct], v4(GB, Ct, 128)[0:Ct], SCL[0:Ct, 8:12].bc(2, b3), ALU.mult)
                tt("dve", KTS[0:Ct], v4(GB, Ct, 128)[0:Ct], SCL[0:Ct, 4:8].bc(2, b3), ALU.mult)
                for h in range(NH):
                    hs = slice(h * 128, (h + 1) * 128)
                    act(VB[0:Ct, h, :], GC[0:Ct, hs], AF.Copy, scale=BETA[0:Ct, h:h + 1])
            st.append(m1b)

            def m2():
                for h in range(NH):
                    mm(GA[:, h * 128:h * 128 + Ct], KN[0:Ct, h, :], idc)
                for h in range(NH):
                    mm(GB[:, h * 128:h * 128 + Ct], QN[0:Ct, h, :], idc)
                cp("act", KQT[:, 0, :, 0:Ct], v4(GA, Ct))
                cp("act", KQT[:, 1, :, 0:Ct], v4(GB, Ct))
                tt("dve", QGT[:, :, 0:Ct], v4(GB, Ct), GAMBC[:, :, 0:Ct], ALU.mult)
            st.append(m2)

            def m3():
                for h in range(NH):
                    mm(GC[0:Ct, h * 128:h * 128 + Ct], KQT[:, 0, h, 0:Ct], KQT[:, 0, h, 0:Ct])
                for h in range(NH):
                    mm(GA[0:Ct, h * 128:h * 128 + Ct], KQT[:, 0, h, 0:Ct], KQT[:, 1, h, 0:Ct])
                tt("dve", MB[0:Ct, :, 0:Ct], v4(GC, Ct)[0:Ct], FF[0:Ct, :, 0:Ct], ALU.mult)
                tt("dve", ATT[0:Ct, :, 0:Ct], v4(GA, Ct)[0:Ct], D2M[0:Ct, :, 0:Ct], ALU.mult)
            st.append(m3)

            def m4():
                for h in range(NH):
                    mm(GB[0:Ct, h * 128:h * 128 + Ct], MB[0:Ct, h, 0:Ct], idc)
                cp("act", YB[0:Ct, :, 0:Ct], v4(GB, Ct)[0:Ct])
                tt("dve", QQ[0:Ct, :, 0:Ct], v4(GB, Ct)[0:Ct], C("IDF")[0:Ct, 0:Ct].bc(1, sh), ALU.add)
            st.append(m4)

            def lvl_sq(lv):
                for h in range(NH):
                    mm(GC[0:Ct, h * 128:h * 128 + Ct], MB[0:Ct, h, 0:Ct], YB[0:Ct, h, 0:Ct])
                for h in range(NH):
                    mm(GA[0:Ct, h * 128:h * 128 + Ct], YB[0:Ct, h, 0:Ct], MB[0:Ct, h, 0:Ct])
                cp("act", YB[0:Ct, :, 0:Ct], v4(GC, Ct)[0:Ct])
                cp("dve", MB[0:Ct, :, 0:Ct], v4(GA, Ct)[0:Ct])

            def lvl_q(lv):
                for h in range(NH):
                    mm(GB[0:Ct, h * 128:h * 128 + Ct], idc, QQ[0:Ct, h, 0:Ct], start=True, stop=False)
                    mm(GB[0:Ct, h * 128:h * 128 + Ct], MB[0:Ct, h, 0:Ct], QQ[0:Ct, h, 0:Ct], start=False, stop=True)
                cp("act" if lv % 2 else "dve", QQ[0:Ct, :, 0:Ct], v4(GB, Ct)[0:Ct])
            def lvl_fused(j):
                for h in range(NH):
                    mm(GB[0:Ct, h * 128:h * 128 + Ct], idc, QQ[0:Ct, h, 0:Ct], start=True, stop=False)
                    mm(GB[0:Ct, h * 128:h * 128 + Ct], MB[0:Ct, h, 0:Ct], QQ[0:Ct, h, 0:Ct], start=False, stop=True)
                for h in range(NH):
                    mm(GC[0:Ct, h * 128:h * 128 + Ct], MB[0:Ct, h, 0:Ct], YB[0:Ct, h, 0:Ct])
                for h in range(NH):
                    mm(GA[0:Ct, h * 128:h * 128 + Ct], YB[0:Ct, h, 0:Ct], MB[0:Ct, h, 0:Ct])
                cp("act", YB[0:Ct, :, 0:Ct], v4(GC, Ct)[0:Ct])
                cp("dve" if j % 2 else "act", MB[0:Ct, :, 0:Ct], v4(GA, Ct)[0:Ct])
                cp("act" if j % 2 else "dve", QQ[0:Ct, :, 0:Ct], v4(GB, Ct)[0:Ct])
            st.append(lambda: lvl_sq(0))
            for j in range(1, nlev):
                st.append(lambda j=j: lvl_fused(j))
            st.append(lambda: lvl_q(nlev - 1))

            def m6():
                for h in range(NH):
                    mm(GC[:, h * 128:h * 128 + Ct], KBG[0:Ct, h, :], QQ[0:Ct, h, 0:Ct])
                act(NWK[:, :, 0:Ct], v4(GC, Ct), AF.Copy, scale=-1.0)
            st.append(m6)

            if not samp:
                def m7():
                    for h in range(NH):
                        hs = slice(h * 128, (h + 1) * 128)
                        mm(GA[0:Ct, hs], QQ[0:Ct, h, 0:Ct], VB[0:Ct, h, :], start=True, stop=False)
                        mm(GA[0:Ct, hs], NWK[:, h, 0:Ct], SGB[:, h, :], start=False, stop=True)
                    cp("act", UU[0:Ct], v4(GA, Ct, 128)[0:Ct])
                st.append(m7)

                def m8():
                    for h in range(NH):
                        hs = slice(h * 128, (h + 1) * 128)
                        mm(GB[0:Ct, hs], QGT[:, h, 0:Ct], SGB[:, h, :], start=True, stop=False)
                        mm(GB[0:Ct, hs], ATT[0:Ct, h, 0:Ct], UU[0:Ct, h, :], start=False, stop=True)
                    for h in range(NH):
                        hs = slice(h * 128, (h + 1) * 128)
                        mm(GC[:, hs], KTS[0:Ct, h, :], UU[0:Ct, h, :])
                    for h in range(NH):
                        hs = slice(h * 128, (h + 1) * 128)
                        stt("dve", SGs[:, h, :], SGs[:, h, :], CDV[:, h:h + 1], GC[:, hs], ALU.mult, ALU.add)
                    cp("act", SGB, SGs)
                st.append(m8)
            else:
                for h in range(NH):
                    st.append(lambda h=h: sample_state_gdn(h, Ct))

            def m9():
                for h in range(NH):
                    hs = slice(h * 128, (h + 1) * 128)
                    act(JUNK[0:Ct, :], GB[0:Ct, hs], AF.Square, accum=SSO[0:Ct, 4 + h:5 + h])
                rstd_from(RSO[0:Ct, 4:8], SSO[0:Ct, 4:8], 1.0 / HD, RMS_EPS, RSO2[0:Ct, 4:8])
                for h in range(NH):
                    hs = slice(h * 128, (h + 1) * 128)
                    stt("dve", MIX[0:Ct, 512 + h * 128:512 + (h + 1) * 128], GB[0:Ct, hs], RSO[0:Ct, 4 + h:5 + h],
                        SGZ[0:Ct, hs], ALU.mult, ALU.mult)
                memset("dve", SSO[:, 4:8], 0.0)
            st.append(m9)
            return st

        def chain_E(slot, p):
            XT = XT2[p]
            st = []

            def e1(half):
                bank = F0 if half == 0 else F1
                for k in range(4 * half, 4 * half + 4):
                    mm(bank[:, (k % 4) * 128:(k % 4 + 1) * 128], MIX[:, k * 128:(k + 1) * 128], IDB)
                cp("act", MIXT[:, 4 * half:4 * half + 4, :], bank.re("p (a b) -> p a b", a=4))
            st.append(lambda: e1(0))
            st.append(lambda: e1(1))

            def e2(hf):
                bank = F0 if hf == 0 else F1
                for k in range(8):
                    mm(bank, MIXT[:, k, :], WOUT[:, k, hf * 512:(hf + 1) * 512], start=(k == 0), stop=(k == 7))
                stt("dve", H1[:, slot, hf * 512:(hf + 1) * 512], XT[:, hf * 512:(hf + 1) * 512], ALPHA, bank,
                    ALU.mult, ALU.add)
            st.append(lambda: e2(0))
            st.append(lambda: e2(1))
            return st

        def merge(*chains):
            items = []
            for ci, ch in enumerate(chains):
                n = len(ch)
                for j, f in enumerate(ch):
                    items.append(((j + 0.5) / n, ci, j, f))
            items.sort(key=lambda x: (x[0], x[1]))
            for _, _, _, f in items:
                f()

        def run(ch):
            for f in ch:
                f()

        def load_w(wg, wu, wd, c0, n):
            P.dma(wg[:, :, 0:n * 128], w_gu[:, c0 * 128:(c0 + n) * 128].rearrange("(k p) n -> p k n", p=128), queue="pool")
            P.dma(wu[:, :, 0:n * 128], w_gu[:, DFF + c0 * 128:DFF + (c0 + n) * 128].rearrange("(k p) n -> p k n", p=128),
                  queue="pool")
            P.dma(wd[:, 0:n, :], w_dn[c0 * 128:(c0 + n) * 128, :].rearrange("(c p) n -> p c n", p=128), queue="pool")

        def load_block0_early():
            load_w(WG0, WU0, WD0, 0, 4)

        npt = NPT if not dbg else dbg
        tiles = [("meta", 0, NMETA, 0, None, 8, CDEC[16], 3)]
        for t in range(npt):
            tiles.append(("p", t, 128, 1 + t, t, 4, CDEC[128], 6))
        tiles.append(("s", 0, CS_, NPT + 1, npt, 12, CDEC[4], 1))
        run(chain_Fa(tiles[0][0], tiles[0][1], tiles[0][2], tiles[0][3], 0))
        run(chain_Fb(tiles[0][0], tiles[0][2], 0))
        pend_E = None
        NG_HEAD = 7
        for i, (kind, ti, Ct, rot_i, slot, kds, cd, nlev) in enumerate(tiles):
            p = i % 2
            samp = (kind == "s")
            nxt = tiles[i + 1] if i + 1 < len(tiles) else None
            G = chain_G(Ct, nlev, samp, p)
            R = chain_R(Ct, p, kds, cd, samp)
            partA = [G[:NG_HEAD], R]
            if pend_E is not None:
                if samp:
                    run(pend_E)
                else:
                    partA.append(pend_E)
                pend_E = None
            merge(*partA)
            partB = [G[NG_HEAD:]]
            if nxt is not None and nxt[0] == "s":
                def conv_out():
                    for g3 in range(3):
                        for c in range(4):
                            mm(F0[0:3, c * 128:(c + 1) * 128], XC[:, g3 * 4 + c, 0:3], C("IDF"))
                        cp("dve", T2[0:3, :], F0[0:3, :])
                        out_toks.append(P.dma(conv_p[:, g3 * 512:(g3 + 1) * 512], T2[0:3, :]))
                conv_out()
            if nxt is not None:
                partB.append(chain_Fa(nxt[0], nxt[1], nxt[2], nxt[3], 1 - p) + chain_Fb(nxt[0], nxt[2], 1 - p))
            merge(*partB)
            if nxt is not None and nxt[0] == "s":
                for h in range(NH):
                    out_toks.append(P.dma(ret_p[h, :, :], SR[:, h, :]))
                    out_toks.append(P.dma(gdn_p[h, :, :], SGs[:, h, :]))
                load_block0_early()
            if slot is not None:
                pend_E = chain_E(slot, p)
        run(pend_E)

        P.barrier()
        A.off = mark_persist + 6144
        NT = npt + 1
        ACCB_t = [A.f32("ACCB%d" % t, 128, D) for t in range(NT)]
        groups = [(g0, min(4, NT - g0)) for g0 in range(0, NT, 4)]
        H1T_g = [A.bf16("H1T%d" % gi, 128, 8, gn * 128) for gi, (g0, gn) in enumerate(groups)]
        RP2 = A.f32("RP2", 128, 2 * D)
        IDB2 = A.bf16("IDB2", 128, 128)
        CS2 = A.f32("IDF2", 128, 128)
        FB = 4
        blocks = [(c0, min(FB, NCH - c0)) for c0 in range(0, NCH, FB)]
        WG = [WG0, A.bf16("WG1", 128, 8, FB * 128)]
        WU = [WU0, A.bf16("WU1", 128, 8, FB * 128)]
        WD = [WD0, A.bf16("WD1", 128, FB, D)]
        SGB2 = A.f32("SGB2", 128, 512)
        ACT_T = A.bf16("ACTT", 128, FB, 512)
        NROT = 3
        ST2_r = [A.f32("ST2_%d" % i, 128, 12) for i in range(NROT)]
        MV2_r = [A.f32("MV2_%d" % i, 128, 2) for i in range(NROT)]
        SM82_r = [A.f32("SM82_%d" % i, 128, 8) for i in range(NROT)]
        HB2_r = [A.bf16("HB2_%d" % i, 128, D) for i in range(2)]
        rot_ctr = [0]
        print("phase B arena words", A.off)
        P.dma(RP2, rowp[:, 2 * D:4 * D].partition_broadcast(128))
        P.dma(CS2, cst[:, COFF["IDF"][0]:COFF["IDF"][0] + 128])
        cp("dve", IDB2, CS2)

        def load_block(bi):
            c0, n = blocks[bi]
            s = bi % 2
            load_w(WG[s], WU[s], WD[s], c0, n)

        GRP_ROT = 3
        MVG_r = [A.f32("MVG%d" % i, 128, 4, 2) for i in range(GRP_ROT)]
        SMG_r = [A.f32("SMG%d" % i, 128, 3, 4) for i in range(GRP_ROT)]
        grp_ctr = [0]

        def ln_group(tiles, x_in_of, x_of, out_of, gv, bv):
            r = grp_ctr[0] % GRP_ROT
            grp_ctr[0] += 1
            MVG = MVG_r[r]; SMG = SMG_r[r]
            n = len(tiles)
            pre = []

            def s_stats(i, t):
                ST2 = ST2_r[i % NROT]
                x_in = x_in_of(t)
                for j in range(2):
                    P.op("dve", lambda e, j=j: e.bn_stats(out=ST2.ap[:, j * 6:(j + 1) * 6], in_=x_in.ap[:, j * 512:(j + 1) * 512]),
                         ins=[x_in], outs=[ST2])
                P.op("dve", lambda e: e.bn_aggr(out=MVG.ap[:, i, :], in_=ST2.ap.rearrange("p (a b) -> p a b", a=2)),
                     ins=[ST2], outs=[MVG])
            for i, t in enumerate(tiles):
                pre.append(lambda i=i, t=t: s_stats(i, t))

            def s_rstd():
                rstd_from(SMG[:, 0, 0:n], MVG[:, 0:n, 1], 1.0, LN_EPS, SMG[:, 1, 0:n])
                stt("dve", SMG[:, 2, 0:n], MVG[:, 0:n, 0], -1.0, SMG[:, 0, 0:n], ALU.mult, ALU.mult)
            pre.append(s_rstd)
            per = {}
            for i, t in enumerate(tiles):
                def s2(i=i, t=t):
                    x = x_of(t)
                    act(x, x_in_of(t), AF.Identity, bias=SMG[:, 2, i:i + 1], scale=SMG[:, 0, i:i + 1])
                    tt("dve", x, x, gv, ALU.mult)
                    tt("dve", out_of(t), x, bv, ALU.add)
                per[t] = [s2]
            return pre, per

        def b0_tail_steps(t):
            x = ACCB_t[t]
            HB2 = HB2_r[t % 2]
            H1T = H1T_g[t // 4]
            tl = t % 4
            pa = PS[4 + 2 * (t % 2)]; pb_ = PS[5 + 2 * (t % 2)]

            def s3():
                cp("act", HB2, x)
                act(x, x, AF.Copy, scale=ALPHA)
                for k in range(8):
                    mm((pa if k < 4 else pb_)[:, (k % 4) * 128:(k % 4 + 1) * 128], HB2[:, k * 128:(k + 1) * 128], IDB2)

            def s4():
                cp("act", H1T[:, 0:4, tl * 128:(tl + 1) * 128], pa.re("p (a b) -> p a b", a=4))
                cp("dve", H1T[:, 4:8, tl * 128:(tl + 1) * 128], pb_.re("p (a b) -> p a b", a=4))
            return [s3, s4]

        def out_step(t):
            def s():
                x = ACCB_t[t]
                if t < npt:
                    out_toks.append(P.dma(y_p[t * 128:(t + 1) * 128, :], x))
                else:
                    out_toks.append(P.dma(y_s[:, :], x[0:CS_, :]))
            return s

        class Pump:
            def __init__(self):
                self.seq = []
                self.done = {}
                self.pos = 0

            def add_seq(self, steps):
                self.seq.extend(steps)

            def add(self, tiles_steps):
                waves = {}
                for i, (t, st) in enumerate(tiles_steps):
                    for j, f in enumerate(st):
                        waves.setdefault(i + j, []).append((t, j == len(st) - 1, f))
                for w in sorted(waves):
                    for t, last, f in waves[w]:
                        self.seq.append(f)
                        if last:
                            self.done[t] = len(self.seq)

            def pump(self, n):
                e = min(len(self.seq), self.pos + n)
                while self.pos < e:
                    self.seq[self.pos]()
                    self.pos += 1

            def ensure(self, t):
                self.pump(self.done[t] - self.pos)

            def flush(self):
                self.pump(len(self.seq))

        pump1 = Pump()
        for gi, (g0, gn) in enumerate(groups):
            tl_ = list(range(g0, g0 + gn))
            pre, per = ln_group(tl_, lambda t: H1[:, t, :], lambda t: ACCB_t[t], lambda t: ACCB_t[t],
                                RP2[:, 0:D], RP2[:, D:2 * D])
            pump1.add_seq(pre)
            pump1.add([(t, per[t] + b0_tail_steps(t)) for t in tl_])
        pump2 = Pump()
        pump1.ensure(groups[0][0] + groups[0][1] - 1)

        for bi, (c0, n) in enumerate(blocks):
            if bi + 1 < len(blocks):
                load_block(bi + 1)
            s = bi % 2
            for gi, (g0, gn) in enumerate(groups):
                if bi == 0:
                    pump1.ensure(g0 + gn - 1)
                ntok = gn * 128
                for c in range(n):
                    if bi == 0:
                        pump1.pump(5)
                    if bi == len(blocks) - 1:
                        pump2.pump(3)
                    pg = PS[(c % 2) * 2]; pu = PS[(c % 2) * 2 + 1]
                    for k in range(8):
                        mm(pg[:, 0:ntok], WG[s][:, k, c * 128:(c + 1) * 128], H1T_g[gi][:, k, 0:ntok],
                           start=(k == 0), stop=(k == 7))
                    for k in range(8):
                        mm(pu[:, 0:ntok], WU[s][:, k, c * 128:(c + 1) * 128], H1T_g[gi][:, k, 0:ntok],
                           start=(k == 0), stop=(k == 7))
                    act(SGB2[:, 0:ntok], pg[:, 0:ntok], AF.Silu)
                    tt("dve", ACT_T[:, c, 0:ntok], SGB2[:, 0:ntok], pu[:, 0:ntok], ALU.mult)
                for ti in range(gn):
                    t = g0 + ti
                    for hf in range(2):
                        pb = PS[4 + (ti % 2) * 2 + hf]
                        for c in range(n):
                            mm(pb, ACT_T[:, c, ti * 128:(ti + 1) * 128], WD[s][:, c, hf * 512:(hf + 1) * 512],
                               start=(c == 0), stop=(c == n - 1))
                        tt("dve", ACCB_t[t][:, hf * 512:(hf + 1) * 512], ACCB_t[t][:, hf * 512:(hf + 1) * 512], pb, ALU.add)
                if bi == len(blocks) - 1:
                    tl_ = list(range(g0, g0 + gn))
                    pre, per = ln_group(tl_, lambda t: ACCB_t[t], lambda t: ACCB_t[t], lambda t: ACCB_t[t],
                                        RP2[:, 0:D], RP2[:, D:2 * D])
                    pump2.add_seq(pre)
                    pump2.add([(t, per[t] + [out_step(t)]) for t in tl_])
            if bi == 0:
                pump1.flush()
                P.dma(RP2, rowp[:, 4 * D:6 * D].partition_broadcast(128))
        pump2.flush()

        P.barrier()
        P.flush()
        print("ops", P0.n_ops, {e: len(P0.ops[e]) for e in ENGS}, "arena words", A.off)
        P0.emit()
    return nc


def _prep_inputs(inp):
    f = np.float32
    g = lambda k: np.ascontiguousarray(np.asarray(inp[k], dtype=f))
    rowp = np.concatenate([g("emb_ln_g"), g("emb_ln_b"), g("ln1_g")[0], g("ln1_b")[0], g("ln2_g")[0], g("ln2_b")[0],
                           np.tile(g("gdn_norm_w")[0], 4), g("dt_bias")[0], g("a_log")[0]]).reshape(1, -1).astype(f)
    cw = g("conv_w")[0]
    colp = np.ascontiguousarray(cw.reshape(4, 12, 128).transpose(2, 1, 0).reshape(128, 48))
    shared = {"w_in": g("w_in")[0], "w_out": g("w_out")[0], "w_gu": g("w_gate_up")[0], "w_dn": g("w_down")[0],
              "rowp": rowp, "colp": colp, "cst": CST, "rot": ROT, "xm": g("meta_tokens")}
    xp = g("x_prompt"); xs = g("x_sample")
    sr = g("state_ret")[0]; sg = g("state_gdn")[0]; sc = g("state_conv")[0]
    maps = []
    for b in range(NCORES):
        m = dict(shared)
        m["xp"] = xp[b]
        m["xs"] = np.ascontiguousarray(xs[b * NSAMP:(b + 1) * NSAMP].reshape(NSAMP * DSEQ, D))
        m["sret"] = np.ascontiguousarray(sr[b * NSAMP:(b + 1) * NSAMP])
        m["sgdn"] = np.ascontiguousarray(sg[b * NSAMP:(b + 1) * NSAMP])
        m["sconv"] = np.ascontiguousarray(sc[b * NSAMP:(b + 1) * NSAMP].reshape(NSAMP * 3, 1536))
        maps.append(m)
    return maps


_NC_CACHE = {}


def kernel(**inputs):
    if "nc" not in _NC_CACHE:
        _NC_CACHE["nc"] = build()
    nc = _NC_CACHE["nc"]
    maps = _prep_inputs(inputs)
    res = run_bass_kernel_spmd(nc, maps, core_ids=list(range(NCORES)))
    r = res.results
    f = np.float32
    y_prompt = np.stack([r[b]["y_p"] for b in range(NCORES)]).astype(f)
    y_sample = np.concatenate([r[b]["y_s"].reshape(NSAMP, DSEQ, D) for b in range(NCORES)]).astype(f)
    ret_p = np.stack([r[b]["ret_p"] for b in range(NCORES)])[None].astype(f)
    gdn_p = np.stack([r[b]["gdn_p"] for b in range(NCORES)])[None].astype(f)
    conv_p = np.stack([r[b]["conv_p"] for b in range(NCORES)])[None].astype(f)
    ret_s = np.concatenate([r[b]["ret_s"] for b in range(NCORES)])[None].astype(f)
    gdn_s = np.concatenate([r[b]["gdn_s"] for b in range(NCORES)])[None].astype(f)
    conv_s = np.concatenate([r[b]["conv_s"].reshape(NSAMP, 3, 1536) for b in range(NCORES)])[None].astype(f)
    return (y_prompt, y_sample, ret_p, gdn_p, conv_p, ret_s, gdn_s, conv_s)
```

```python
import math
from contextlib import ExitStack

import numpy as np
import concourse.bass as bass
import concourse.mybir as mybir
from concourse.bass_utils import run_bass_kernel_spmd

F32 = mybir.dt.float32
BF16 = mybir.dt.bfloat16
AF = mybir.ActivationFunctionType
ALU = mybir.AluOpType

NCORES = 8
D = 1024
SEQ = 2048
NPT = SEQ // 128
NMETA = 16
NSAMP = 16
DSEQ = 4
HD = 128
NH = 4
INC = 4104
DFF = 2816
NCH = DFF // 128
PAST = 16384
ALPHA = 2.0 ** 0.25
LN_EPS = 1e-5
RMS_EPS = 1e-6
ENGS = ("pe", "act", "dve", "pool", "sp")
SCHED_ENABLE = True
EMBED_WAIT = True


class V:
    __slots__ = ("ap", "key")

    def __init__(self, ap, key):
        self.ap = ap
        self.key = key

    def __getitem__(self, idx):
        return V(self.ap[idx], self.key)

    def re(self, pat, **kw):
        return V(self.ap.rearrange(pat, **kw), self.key)

    def bc(self, axis, shape):
        return V(self.ap.unsqueeze(axis).to_broadcast(list(shape)), self.key)

    def bitcast(self, dt):
        return V(self.ap.bitcast(dt), self.key)

    @property
    def shape(self):
        return self.ap.shape


class Prog:
    NDMA = {"sp": 16, "pool": 8, "act": 2}

    def __init__(self, nc):
        self.nc = nc
        self.ops = {e: [] for e in ENGS}
        self.cnt = {e: 0 for e in ENGS}
        self.seen = {e: {} for e in ENGS}
        self.last_w = {}
        self.readers = {}
        self.bank = {}
        self.dma_n = {q: [0] * n for q, n in self.NDMA.items()}
        self.dma_rr = {q: 0 for q in self.NDMA}
        self.n_ops = 0
        self.clock = {}
        self.issue = {}
        self.n_issue = 0

    def _need(self, eng, waits, tok):
        if tok is None:
            return
        k, v = tok
        if self.seen[eng].get(k, 0) >= v:
            return
        if waits.get(k, 0) < v:
            waits[k] = v

    def _deps(self, eng, ins, outs):
        cand = {}

        def need(tok):
            if tok is None:
                return
            k, v = tok
            if self.seen[eng].get(k, 0) >= v:
                return
            if cand.get(k, 0) < v:
                cand[k] = v
        for x in ins:
            if not isinstance(x, V):
                continue
            if x.key.startswith("ps"):
                for e2, t in self.bank.get(x.key, {}).items():
                    if e2 != eng or eng != "pe":
                        need(t)
            else:
                need(self.last_w.get(x.key))
        for x in outs:
            if x.key.startswith("ps"):
                for e2, t in self.bank.get(x.key, {}).items():
                    if e2 != eng or eng != "pe":
                        need(t)
            else:
                need(self.last_w.get(x.key))
                for t in self.readers.get(x.key, ()):
                    need(t)
        return self._take_waits(eng, cand)

    def _take_waits(self, eng, cand):
        waits = []
        seen = self.seen[eng]
        for k, v in sorted(cand.items(), key=lambda kv: -self.issue.get(kv, 0)):
            if seen.get(k, 0) >= v:
                continue
            waits.append((k, v))
            seen[k] = v
            for k2, v2 in self.clock.get((k, v), {}).items():
                if seen.get(k2, 0) < v2:
                    seen[k2] = v2
        return waits

    def _stamp(self, eng, tok):
        c = dict(self.seen[eng])
        c.pop(eng, None)
        self.clock[tok] = c
        self.n_issue += 1
        self.issue[tok] = self.n_issue

    def _commit(self, eng, tok, ins, outs):
        for x in ins:
            if not isinstance(x, V):
                continue
            if x.key.startswith("ps"):
                self.bank.setdefault(x.key, {})[eng] = tok
            else:
                self.readers.setdefault(x.key, []).append(tok)
        for x in outs:
            if x.key.startswith("ps"):
                self.bank.setdefault(x.key, {})[eng] = tok
            else:
                self.last_w[x.key] = tok
                self.readers[x.key] = []

    def op(self, eng, fn, ins=(), outs=()):
        waits = self._deps(eng, ins, outs)
        self.cnt[eng] += 1
        tok = (eng, self.cnt[eng])
        self.ops[eng].append((fn, waits, (eng, 1)))
        self._stamp(eng, tok)
        self._commit(eng, tok, ins, outs)
        self.n_ops += 1
        return tok

    def dma(self, out, in_, queue="sp"):
        s = self.dma_rr[queue]
        self.dma_rr[queue] = (s + 1) % self.NDMA[queue]
        key = ("dma_" + queue, s)
        ins = [in_] if isinstance(in_, V) else []
        outs = [out] if isinstance(out, V) else []
        waits = dict(self._deps(queue, ins, outs))
        prev = self.dma_n[queue][s] * 16
        if prev and self.seen[queue].get(key, 0) < prev:
            waits[key] = prev
            self.seen[queue][key] = prev
        self.dma_n[queue][s] += 1
        tok = (key, self.dma_n[queue][s] * 16)
        oap = out.ap if isinstance(out, V) else out
        iap = in_.ap if isinstance(in_, V) else in_

        def fn(e, oap=oap, iap=iap):
            return e.dma_start(out=oap, in_=iap)

        self.ops[queue].append((fn, list(waits.items()), (key, 16)))
        self._stamp(queue, tok)
        self._commit("dma", tok, ins, outs)
        self.n_ops += 1
        return tok

    def barrier(self):
        toks = [(e, self.cnt[e]) for e in ENGS if self.cnt[e]]
        toks += [(("dma_" + q, i), self.dma_n[q][i] * 16) for q in self.NDMA for i in range(self.NDMA[q]) if self.dma_n[q][i]]
        for e in ENGS:
            waits = {}
            for t in toks:
                if t[0] == e:
                    continue
                self._need(e, waits, t)
            for k, v in waits.items():
                self.seen[e][k] = v
            self.ops[e].append((None, list(waits.items()), None))

    def emit(self):
        nc = self.nc
        with ExitStack() as es:
            sem = {}
            for e in ENGS:
                sem[e] = es.enter_context(nc.semaphore("s_" + e))
            for q in self.NDMA:
                for i in range(self.NDMA[q]):
                    sem[("dma_" + q, i)] = es.enter_context(nc.semaphore("s_dma_%s%d" % (q, i)))
            block = es.enter_context(nc.Block())
            ops = self.ops

            def run(e, name):
                for fn, waits, inc in ops[name]:
                    if fn is None:
                        for k, v in waits:
                            e.wait_ge(sem[k], v)
                        continue
                    embed = EMBED_WAIT and bool(waits) and not isinstance(inc[0], tuple)
                    for k, v in (waits[:-1] if embed else waits):
                        e.wait_ge(sem[k], v)
                    ins = fn(e)
                    if embed:
                        ins._wait_ge(sem[waits[-1][0]], waits[-1][1])
                    ins.then_inc(sem[inc[0]], inc[1])

            @block.tensor
            def _(e):
                run(e, "pe")

            @block.scalar
            def _(e):
                run(e, "act")

            @block.vector
            def _(e):
                run(e, "dve")

            @block.gpsimd
            def _(e):
                run(e, "pool")

            @block.sync
            def _(e):
                run(e, "sp")


class Sched:
    LAT = 0.18

    def __init__(self, prog, enable=True):
        self.P = prog
        self.items = []
        self.enable = enable

    @staticmethod
    def _cols(v):
        n = 1
        for d in v.ap.shape[1:]:
            n *= int(d)
        return n

    def op(self, eng, fn, ins=(), outs=(), aset=None):
        ins = list(ins); outs = list(outs)
        if eng == "pe":
            rhs = ins[1]
            n = self._cols(rhs)
            cost = 0.02 + n * 0.00042 * (4.0 if rhs.ap.dtype == F32 else 1.0)
        elif eng == "act":
            cost = 0.22 + self._cols(outs[0]) * 0.00075
        elif eng == "dve":
            cost = 0.07 + self._cols(outs[0]) * 0.0012
        else:
            cost = 0.15 + self._cols(outs[0]) * 0.0023
        self.items.append(["op", eng, fn, ins, outs, cost, cost, aset])

    def dma(self, out, in_, queue="sp"):
        v = out if isinstance(out, V) else in_
        nbytes = self._cols(v) * int(v.ap.shape[0]) * 4
        lat = 2.0 + nbytes / 200e3
        busy = 1.0 if queue == "pool" else 0.1
        self.items.append(["dma", queue, (out, in_), [in_] if isinstance(in_, V) else [],
                           [out] if isinstance(out, V) else [], busy, lat, None])

    def barrier(self):
        self.items.append(["bar"])

    def _schedule(self, seg):
        n = len(seg)
        last_w, readers, bank, pe_bank = {}, {}, {}, {}
        deps = [set() for _ in range(n)]
        keys_of = lambda x: x.key if isinstance(x.key, tuple) else (x.key,)
        for i, it in enumerate(seg):
            eng = it[1]
            d = deps[i]
            for x in it[3]:
                if not isinstance(x, V):
                    continue
                for k in keys_of(x):
                    if k.startswith("ps"):
                        for e2, j in bank.get(k, {}).items():
                            d.add(j)
                    elif k in last_w:
                        d.add(last_w[k])
            for x in it[4]:
                for k in keys_of(x):
                    if k.startswith("ps"):
                        for e2, j in bank.get(k, {}).items():
                            d.add(j)
                    else:
                        if k in last_w:
                            d.add(last_w[k])
                        d.update(readers.get(k, ()))
            d.discard(i)
            ceng = "dma" if it[0] == "dma" else eng
            for x in it[3]:
                if not isinstance(x, V):
                    continue
                for k in keys_of(x):
                    if k.startswith("ps"):
                        bank.setdefault(k, {})[ceng] = i
                    else:
                        readers.setdefault(k, []).append(i)
            for x in it[4]:
                for k in keys_of(x):
                    if k.startswith("ps"):
                        bank.setdefault(k, {})[ceng] = i
                    else:
                        last_w[k] = i
                        readers[k] = []
        succ = [[] for _ in range(n)]
        for i in range(n):
            for j in deps[i]:
                succ[j].append(i)
        prio = [0.0] * n
        for i in range(n - 1, -1, -1):
            m = 0.0
            for j in succ[i]:
                if prio[j] > m:
                    m = prio[j]
            prio[i] = seg[i][6] + self.LAT + m
        pmax = max(prio) if n else 1.0
        ndep = [len(deps[i]) for i in range(n)]
        ready_t = [0.0] * n
        fin = [0.0] * n
        avail = {}
        cur_set = {}
        cand = set(i for i in range(n) if ndep[i] == 0)
        order = []
        while cand:
            best = None; bkey = None
            for i in cand:
                it = seg[i]
                eng = it[1] if it[0] == "op" else "q_" + it[1]
                s_ = max(avail.get(eng, 0.0), ready_t[i])
                if it[7] is not None and cur_set.get(eng) not in (None, it[7]):
                    s_ += 1.3
                key = (s_ - 0.6 * prio[i] / pmax, i)
                if bkey is None or key < bkey:
                    bkey = key; best = i
            i = best
            cand.discard(i)
            it = seg[i]
            eng = it[1] if it[0] == "op" else "q_" + it[1]
            s_ = max(avail.get(eng, 0.0), ready_t[i])
            if it[7] is not None:
                if cur_set.get(eng) not in (None, it[7]):
                    s_ += 1.3
                cur_set[eng] = it[7]
            avail[eng] = s_ + it[5]
            fin[i] = s_ + it[6]
            order.append(i)
            for j in succ[i]:
                ndep[j] -= 1
                lat = 0.0 if (seg[j][0] == "op" and it[0] == "op" and seg[j][1] == it[1]) else self.LAT
                if fin[i] + lat > ready_t[j]:
                    ready_t[j] = fin[i] + lat
                if ndep[j] == 0:
                    cand.add(j)
        assert len(order) == n
        return order, max(fin) if n else 0.0

    def flush(self):
        seg = []
        tot = 0.0

        def run_seg(seg):
            nonlocal tot
            if not seg:
                return
            if self.enable:
                order, t_end = self._schedule(seg)
                tot += t_end
            else:
                order = range(len(seg))
            for i in order:
                it = seg[i]
                if it[0] == "op":
                    self.P.op(it[1], it[2], it[3], it[4])
                else:
                    self.P.dma(it[2][0], it[2][1], queue=it[1])
        for it in self.items:
            if it[0] == "bar":
                run_seg(seg)
                seg = []
                self.P.barrier()
            else:
                seg.append(it)
        run_seg(seg)
        print("scheduler: simulated total %.1f us" % tot)


class Arena:
    def __init__(self, ap, nwords):
        self.ap = ap
        self.n = nwords
        self.off = 0
        self.uid = 0

    def _take(self, words, name):
        assert self.off + words <= self.n, ("SBUF arena overflow", name, self.off, words, self.n)
        v = self.ap[:, self.off:self.off + words]
        self.off += words
        self.uid += 1
        return v, "%s#%d" % (name, self.uid)

    def f32(self, name, *shape):
        shape = shape[1:]
        w = int(np.prod(shape))
        ap, key = self._take(w, name)
        if len(shape) == 2:
            ap = ap.rearrange("p (a b) -> p a b", a=shape[0])
        elif len(shape) == 3:
            ap = ap.rearrange("p (a b c) -> p a b c", a=shape[0], b=shape[1])
        return V(ap, key)

    def bf16(self, name, *shape):
        shape = shape[1:]
        w = int(np.prod(shape))
        ap, key = self._take((w + 1) // 2, name)
        ap = ap.bitcast(BF16)[:, 0:w]
        if len(shape) == 2:
            ap = ap.rearrange("p (a b) -> p a b", a=shape[0])
        elif len(shape) == 3:
            ap = ap.rearrange("p (a b c) -> p a b c", a=shape[0], b=shape[1])
        return V(ap, key)


def _consts():
    f = np.float32
    lg = np.log(1.0 - 2.0 ** (-5.0 - np.arange(NH, dtype=np.float64)))
    idx = np.arange(128)
    c = {}
    c["IDF"] = np.eye(128, dtype=f)
    c["TRI"] = (idx[:, None] <= idx[None, :]).astype(f)
    c["STRICT"] = (idx[:, None] > idx[None, :]).astype(f)
    c["ONES"] = np.ones((128, 128), f)
    blk = idx // DSEQ
    same = (blk[:, None] == blk[None, :])
    c["TRIS"] = (c["TRI"] * same).astype(f)[:, 0:64]
    c["STRICTS"] = (c["STRICT"] * same).astype(f)[:, 0:64]
    loc = idx % DSEQ
    mr = np.zeros((128, NH, 128), f)
    mrs = np.zeros((128, NH, 128), f)
    for h in range(NH):
        mr[:, h, :] = (np.exp(-lg[h] * (idx[:, None] + 1.0)) * (idx[:, None] <= idx[None, :])).astype(f)
        mrs[:, h, :] = (np.exp(-lg[h] * (loc[:, None] + 1.0)) * (idx[:, None] <= idx[None, :]) * same).astype(f)
    c["MASKR"] = mr.reshape(128, NH * 128)
    c["MASKRS"] = np.ascontiguousarray(mrs[:, :, 0:64]).reshape(128, NH * 64)
    sc = np.zeros((128, 16), f)
    for h in range(NH):
        sc[:, h] = np.exp(lg[h] * (idx + 1.0))
        sc[:, 4 + h] = np.exp(lg[h] * (127.0 - idx))
        sc[:, 8 + h] = np.exp(lg[h] * np.maximum(15.0 - idx, 0))
        sc[:, 12 + h] = np.exp(lg[h] * (3.0 - loc))
    qs = np.zeros((128, 4), f)
    for h in range(NH):
        qs[:, h] = np.exp(lg[h] * (loc + 1.0))
    c["SC"] = sc
    c["QS"] = qs
    c["BMASK"] = (blk[:, None] == np.arange(16)[None, :]).astype(f)
    names = ["IDF", "TRI", "STRICT", "ONES", "TRIS", "STRICTS", "MASKR", "MASKRS", "SC", "QS", "BMASK"]
    offs = {}
    o = 0
    for n in names:
        offs[n] = (o, c[n].shape[1])
        o += c[n].shape[1]
    cst = np.concatenate([c[n] for n in names], axis=1).astype(f)
    cdec = {128: [float(np.exp(lg[h] * 128.0)) for h in range(NH)],
            16: [float(np.exp(lg[h] * 16.0)) for h in range(NH)],
            4: [float(np.exp(lg[h] * 4.0)) for h in range(NH)]}
    return cst, offs, cdec


def _rot_tables():
    half = HD // 2
    inv = (np.float32(10000.0) ** (-np.arange(half, dtype=np.float32) / np.float32(half))).astype(np.float32)
    ntile = NPT + 2
    pos = np.zeros((ntile, 128), np.float32)
    pos[0, :NMETA] = np.arange(NMETA)
    for t in range(NPT):
        pos[1 + t] = NMETA + t * 128 + np.arange(128)
    pos[NPT + 1, :NSAMP * DSEQ] = PAST + (np.arange(NSAMP * DSEQ) % DSEQ)
    ang = pos[:, :, None].astype(np.float32) * inv[None, None, :]
    cos = np.cos(ang).astype(np.float32)
    sin = np.sin(ang).astype(np.float32)
    ks = np.float32(HD ** -0.5)
    return np.concatenate([cos, cos * ks, sin, sin * ks], axis=2).astype(np.float32)


CST, COFF, CDEC = _consts()
ROT = _rot_tables()
NCST = CST.shape[1]
ROWP_N = 6 * D + 512 + 8


def build(dbg=False):
    nc = bass.Bass("TRN2", target_bir_lowering=False)
    di = lambda n, s: nc.dram_tensor(n, list(s), F32, kind="ExternalInput").ap()
    do = lambda n, s: nc.dram_tensor(n, list(s), F32, kind="ExternalOutput").ap()
    xp = di("xp", (SEQ, D)); xm = di("xm", (NMETA, D)); xs = di("xs", (NSAMP * DSEQ, D))
    sret = di("sret", (NSAMP, NH, HD, HD)); sgdn = di("sgdn", (NSAMP, NH, HD, HD))
    sconv = di("sconv", (NSAMP * 3, 1536))
    w_in = di("w_in", (D, INC)); w_out = di("w_out", (D, D))
    w_gu = di("w_gu", (D, 2 * DFF)); w_dn = di("w_dn", (DFF, D))
    rowp = di("rowp", (1, ROWP_N)); colp = di("colp", (128, 48))
    cst = di("cst", (128, NCST)); rot = di("rot", (NPT + 2, 128, 256))
    y_p = do("y_p", (SEQ, D)); y_s = do("y_s", (NSAMP * DSEQ, D))
    ret_p = do("ret_p", (NH, HD, HD)); gdn_p = do("gdn_p", (NH, HD, HD)); conv_p = do("conv_p", (3, 1536))
    ret_s = do("ret_s", (NSAMP, NH, HD, HD)); gdn_s = do("gdn_s", (NSAMP, NH, HD, HD))
    conv_s = do("conv_s", (NSAMP * 3, 1536))

    NW = 53200
    with ExitStack() as es:
        arena_t = es.enter_context(nc.sbuf_tensor("arena", [128, NW], F32))
        psb = [es.enter_context(nc.psum_tensor("psb%d" % i, [128, 512], F32)) for i in range(8)]
        PS = [V(psb[i][:], "ps%d" % i) for i in range(8)]
        P0 = Prog(nc)
        P = Sched(P0, enable=SCHED_ENABLE)
        A = Arena(arena_t[:], NW)

        def mm(out, lhsT, rhs, start=True, stop=True):
            P.op("pe", lambda e: e.matmul(out.ap, lhsT=lhsT.ap, rhs=rhs.ap, start=start, stop=stop),
                 ins=[lhsT, rhs], outs=[out])

        def act(out, in_, func, bias=0.0, scale=1.0, accum=None, eng="act"):
            ins = [in_] + [x for x in (bias, scale) if isinstance(x, V)]
            outs = [out] + ([accum] if accum is not None else [])
            b = bias.ap if isinstance(bias, V) else bias
            s = scale.ap if isinstance(scale, V) else scale
            a = accum.ap if accum is not None else None
            if accum is not None:
                ins.append(accum)
            aset = "silu" if func == AF.Silu else ("lnexp" if func in (AF.Exp, AF.Ln) else None)
            P.op("act", lambda e: e.activation(out=out.ap, in_=in_.ap, func=func, bias=b, scale=s, accum_out=a)
                 if a is not None else e.activation(out=out.ap, in_=in_.ap, func=func, bias=b, scale=s),
                 ins=ins, outs=outs, aset=aset)

        def tt(eng, out, a, b, op):
            P.op(eng, lambda e: e.tensor_tensor(out=out.ap, in0=a.ap, in1=b.ap, op=op), ins=[a, b], outs=[out])

        def ts(eng, out, a, s1, op0, s2=None, op1=None):
            ins = [a] + [x for x in (s1, s2) if isinstance(x, V)]
            v1 = s1.ap if isinstance(s1, V) else s1
            v2 = s2.ap if isinstance(s2, V) else s2
            if op1 is None:
                P.op(eng, lambda e: e.tensor_scalar(out=out.ap, in0=a.ap, scalar1=v1, scalar2=None, op0=op0),
                     ins=ins, outs=[out])
            else:
                P.op(eng, lambda e: e.tensor_scalar(out=out.ap, in0=a.ap, scalar1=v1, scalar2=v2, op0=op0, op1=op1),
                     ins=ins, outs=[out])

        def stt(eng, out, a, scalar, b, op0, op1):
            ins = [a, b] + ([scalar] if isinstance(scalar, V) else [])
            sv = scalar.ap if isinstance(scalar, V) else scalar
            P.op(eng, lambda e: e.scalar_tensor_tensor(out=out.ap, in0=a.ap, scalar=sv, in1=b.ap, op0=op0, op1=op1),
                 ins=ins, outs=[out])

        def cp(eng, out, in_):
            if eng == "act":
                P.op("act", lambda e: e.copy(out=out.ap, in_=in_.ap), ins=[in_], outs=[out])
            else:
                P.op(eng, lambda e: e.tensor_copy(out=out.ap, in_=in_.ap), ins=[in_], outs=[out])

        def memset(eng, out, val):
            P.op(eng, lambda e: e.memset(out.ap, val), ins=[], outs=[out])

        def recip(out, in_):
            P.op("dve", lambda e: e.reciprocal(out=out.ap, in_=in_.ap), ins=[in_], outs=[out])

        def rstd_from(out, ssq, scale, eps, tmp):
            act(tmp, ssq, AF.Ln, bias=eps, scale=scale)
            act(out, tmp, AF.Exp, scale=-0.5)

        H1 = A.bf16("H1", 128, NPT + 1, D)
        mark_persist = A.off

        WIN_Q = A.bf16("WINQ", 128, 8, 1536)
        WIN_G = A.bf16("WING", 128, 8, 512)
        WIN_X = A.bf16("WINX", 128, 8, 1536)
        WIN_Z = A.bf16("WINZ", 128, 8, 512)
        WIN_AB = A.bf16("WINAB", 128, 8, 8)
        _wq = WIN_Q.re("p k n -> p (k n)")
        WG0 = _wq[:, 0:4096].re("p (k n) -> p k n", k=8)
        WU0 = _wq[:, 4096:8192].re("p (k n) -> p k n", k=8)
        WD0 = _wq[:, 8192:12288].re("p (c n) -> p c n", c=4)
        _wgrp = [(WIN_AB, 4096, 8), (WIN_Q, 0, 1536), (WIN_X, 2048, 1536), (WIN_G, 1536, 512), (WIN_Z, 3584, 512)]

        def WIN_cols(k, c0, n):
            for buf, b0, bn in _wgrp:
                if b0 <= c0 and c0 + n <= b0 + bn:
                    return buf[:, k, c0 - b0:c0 - b0 + n]
            raise AssertionError((c0, n))
        WOUT = A.bf16("WOUT", 128, 8, D)
        CS = A.f32("CST", 128, NCST)
        RP = A.f32("ROWP", 128, 2 * D + 128 + 8)
        CW = A.f32("CW", 128, 12, 4)
        IDB = A.bf16("IDB", 128, 128)

        def C(name, rows=slice(0, 128)):
            o, n = COFF[name]
            return CS[rows, o:o + n]

        P.dma(CS, cst[:, :])
        P.dma(RP[:, 0:2 * D], rowp[:, 0:2 * D].partition_broadcast(128))
        P.dma(RP[:, 2 * D:2 * D + 128], rowp[:, 6 * D:6 * D + 128].partition_broadcast(128))
        P.dma(RP[:, 2 * D + 128:2 * D + 136], rowp[:, 6 * D + 512:6 * D + 520].partition_broadcast(128))
        P.dma(CW, colp.rearrange("p (c w) -> p c w", c=12))
        for buf, b0, bn in _wgrp:
            P.dma(buf, w_in[:, b0:b0 + bn].rearrange("(k p) n -> p k n", p=128), queue="pool")
        P.dma(WOUT, w_out.rearrange("(k p) n -> p k n", p=128), queue="pool")
        cp("dve", IDB, C("IDF"))
        G0 = RP[:, 0:D]; B0 = RP[:, D:2 * D]
        GNW = RP[:, 2 * D:2 * D + 128].bc(1, (128, 4, 128))
        DTB = RP[:, 2 * D + 128:2 * D + 132]
        NEGA = A.f32("NEGA", 128, 4)
        act(NEGA, RP[:, 2 * D + 132:2 * D + 136], AF.Exp)
        ts("dve", NEGA, NEGA, -1.0, ALU.mult)

        XT2 = [A.f32("XT0", 128, D), A.f32("XT1", 128, D)]
        _o_HB = A.off
        HB = A.bf16("HB", 128, D)
        HT = A.bf16("HT", 128, 8, 128)
        RT = A.f32("RT", 128, 256)
        _o_QK = [A.off, A.off + 512]
        QK2 = [A.bf16("QK0", 128, 2, 512), A.bf16("QK1", 128, 2, 512)]
        VV2 = [A.bf16("VV0", 128, 512), A.bf16("VV1", 128, 512)]
        SGT2 = [A.bf16("SGT0", 128, 512), A.bf16("SGT1", 128, 512)]
        SGZ2 = [A.bf16("SGZ0", 128, 512), A.bf16("SGZ1", 128, 512)]
        GAB = A.f32("GAB", 128, 8)
        XC = A.f32("XC", 128, 12, 131)
        ACC = A.f32("ACC", 128, 4, 128)
        T1 = ACC.re("p a b -> p (a b)")
        T2 = T1
        CV = A.bf16("CV", 128, 12, 128)
        ST = A.f32("ST", 128, 12)
        MV = A.f32("MV", 128, 2)
        SM8 = A.f32("SM8", 128, 8)
        JUNK = A.bf16("JUNK", 128, 128)
        MIX = A.bf16("MIX", 128, D)
        QDT = A.bf16("QDT", 128, 4, 128)
        KT = A.bf16("KT", 128, 4, 128)
        SMK = A.bf16("SMK", 128, 4, 128)
        SSO = A.f32("SSO", 128, 8)
        RSO = A.f32("RSO", 128, 8)
        RSO2 = A.f32("RSO2", 128, 8)
        GG = A.f32("GG", 128, 4)
        BETA = A.f32("BETA", 128, 4)
        NBETA = A.f32("NBETA", 128, 4)
        GSH = A.f32("GSH", 128, 2, 512)
        GSH0 = V(GSH.ap[:, 0, :], GSH.key + "/0")
        GSH1 = V(GSH.ap[:, 1, :], GSH.key + "/1")
        GTRI = GSH0.re("p (h i) -> p h i", h=4)
        TMPF = GTRI
        DIFF = GSH1.re("p (h i) -> p h i", h=4)
        GPP = A.f32("GPP", 128, 8)
        FF = A.bf16("FF", 128, 4, 128)
        D2M = A.bf16("D2M", 128, 4, 128)
        GAMBC = A.bf16("GAMBC", 128, 4, 128)
        GAMPP = A.f32("GAMPP", 128, 4)
        KTSC = A.f32("KTSC", 128, 4)
        CDV = A.f32("CDV", 128, 4)
        SSQ = A.f32("SSQ", 128, 8)
        RS = A.f32("RS", 128, 8)
        SCL = A.f32("SCL", 128, 16)
        TG = A.f32("TG", 128, 8)
        QN = A.bf16("QN", 128, 4, 128); KN = A.bf16("KN", 128, 4, 128); KBG = A.bf16("KBG", 128, 4, 128)
        KTS = A.bf16("KTS", 128, 4, 128); VB = A.bf16("VB", 128, 4, 128)
        KQT = A.bf16("KQT", 128, 2, 4, 128)
        MIXT = KQT.re("p a h i -> p (a h) i")
        QGT = A.bf16("QGT", 128, 4, 128)
        YB = A.bf16("YB", 128, 4, 128)
        MB = A.bf16("MB", 128, 4, 128)
        QQ = A.bf16("QQ", 128, 4, 128)
        ATT = A.bf16("ATT", 128, 4, 128)
        NWK = A.bf16("NWK", 128, 4, 128)
        UU = A.bf16("UU", 128, 4, 128)
        _o_SR = A.off
        SR = A.f32("SR", 128, 4, 128)
        _o_SG = A.off
        SGs = A.f32("SGs", 128, 4, 128)
        SRB = A.bf16("SRB", 128, 4, 128)
        SGB = A.bf16("SGB", 128, 4, 128)
        memset("pool", SR, 0.0); memset("pool", SGs, 0.0)
        memset("pool", SRB, 0.0); memset("pool", SGB, 0.0)
        memset("pool", XC, 0.0)
        memset("dve", SSO, 0.0)
        memset("dve", SSQ, 0.0)
        NS = NSAMP
        CS_ = NS * DSEQ
        SSq = [A.f32("SSq%d" % q, 128, 4, 128) for q in range(4)]
        _psamp0 = ((NPT if not dbg else dbg) + 1) % 2

        def _alias512(off, owner):
            return V(arena_t[:, off:off + 512].rearrange("p (s e) -> p s e", s=4), owner.key)
        SSq_b = [_alias512(_o_SR, SR), _alias512(_o_SG, SGs), _alias512(_o_QK[1 - _psamp0], QK2[1 - _psamp0]),
                 _alias512(_o_HB, HB)]
        SS_sets = [SSq, SSq_b]
        ss_unit = [0]
        _psamp = ((NPT if not dbg else dbg) + 1) % 2
        _xo = XT2[1 - _psamp]
        SSB = V(_xo.ap.bitcast(BF16)[:, 0:NS * 128].rearrange("p (s e) -> p s e", s=NS), _xo.key)
        ZA = A.bf16("ZA", 128, NS * 68)
        KM = A.bf16("KM", 128, NS // 4, 128)
        CDVS = A.f32("CDVS", 128, 4, NS)
        memset("pool", ZA, 0.0)
        print("phase A arena words", A.off)

        out_toks = []
        F0, F1, F2 = PS[0], PS[1], PS[2]
        R0, R1 = PS[3], PS[4]
        GA, GB, GC = PS[5], PS[6], PS[7]

        def v4(x, Ct, w=None):
            w = Ct if w is None else w
            return x.re("p (h i) -> p h i", h=4)[:, :, 0:w]

        def chain_Fa(kind, ti, Ct, rot_i, p):
            XT = XT2[p]; QK = QK2[p]; VV = VV2[p]
            st = []

            def s_load():
                if kind == "meta":
                    memset("pool", XT, 0.0)
                    P.dma(XT[0:NMETA, :], xm[:, :])
                elif kind == "p":
                    P.dma(XT, xp[ti * 128:(ti + 1) * 128, :])
                else:
                    memset("pool", XT, 0.0)
                    P.dma(XT[0:CS_, :], xs[:, :])
                P.dma(RT, rot[rot_i, :, :])
            st.append(s_load)

            def s_ln_a():
                for j in range(2):
                    P.op("dve", lambda e, j=j: e.bn_stats(out=ST.ap[:, j * 6:(j + 1) * 6], in_=XT.ap[:, j * 512:(j + 1) * 512]),
                         ins=[XT], outs=[ST])
                P.op("dve", lambda e: e.bn_aggr(out=MV.ap, in_=ST.ap.rearrange("p (a b) -> p a b", a=2)), ins=[ST], outs=[MV])
                rstd_from(SM8[:, 0:1], MV[:, 1:2], 1.0, LN_EPS, SM8[:, 1:2])
                stt("dve", SM8[:, 2:3], MV[:, 0:1], -1.0, SM8[:, 0:1], ALU.mult, ALU.mult)
            st.append(s_ln_a)

            def s_ln_b():
                act(XT, XT, AF.Identity, bias=SM8[:, 2:3], scale=SM8[:, 0:1])
                tt("dve", XT, XT, G0, ALU.mult)
            st.append(s_ln_b)

            def s_ln_c():
                tt("pool", XT, XT, B0, ALU.add)
                cp("act", HB, XT)
            st.append(s_ln_c)

            def s_tr(half):
                bank = F0 if half == 0 else F1
                for k in range(4 * half, 4 * half + 4):
                    mm(bank[:, (k % 4) * 128:(k % 4 + 1) * 128], HB[:, k * 128:(k + 1) * 128], IDB)
                cp("act", HT[:, 4 * half:4 * half + 4, :], bank.re("p (a b) -> p a b", a=4))
            st.append(lambda: s_tr(0))
            st.append(lambda: s_tr(1))

            def s_proj(bank, c0, half):
                for k in range(4 * half, 4 * half + 4):
                    mm(bank, HT[:, k, :], WIN_cols(k, c0, 512), start=(k == 0), stop=(k == 7))
            st.append(lambda: s_proj(F0, 0, 0))
            st.append(lambda: s_proj(F0, 0, 1))
            st.append(lambda: s_proj(F1, 512, 0))
            st.append(lambda: s_proj(F1, 512, 1))

            def s_rot(qi, bank, part):
                xv = bank.re("p (h t f) -> p h t f", h=4, t=2)
                x1 = xv[:, :, 0, :]; x2 = xv[:, :, 1, :]
                cosv = RT[:, qi * 64:(qi + 1) * 64].bc(1, (128, 4, 64))
                sinv = RT[:, 128 + qi * 64:128 + (qi + 1) * 64].bc(1, (128, 4, 64))
                ov = QK[:, qi, :].re("p (h t f) -> p h t f", h=4, t=2)
                if part == 0:
                    t1 = T2[:, 0:256].re("p (h f) -> p h f", h=4); t2 = T2[:, 256:512].re("p (h f) -> p h f", h=4)
                    tt("dve", t1, x1, cosv, ALU.mult)
                    tt("dve", t2, x2, sinv, ALU.mult)
                    tt("dve", ov[:, :, 0, :], t1, t2, ALU.subtract)
                else:
                    t3 = T2[:, 0:256].re("p (h f) -> p h f", h=4); t4 = T2[:, 256:512].re("p (h f) -> p h f", h=4)
                    tt("dve", t3, x1, sinv, ALU.mult)
                    tt("dve", t4, x2, cosv, ALU.mult)
                    tt("dve", ov[:, :, 1, :], t3, t4, ALU.add)
            st.append(lambda: s_rot(0, F0, 0))
            st.append(lambda: s_rot(0, F0, 1))
            st.append(lambda: s_proj(F2, 1024, 0))
            st.append(lambda: s_proj(F2, 1024, 1))
            st.append(lambda: s_rot(1, F1, 0))
            st.append(lambda: s_rot(1, F1, 1))
            st.append(lambda: cp("act", VV, F2))
            return st

        def chain_Fb(kind, Ct, p):
            SGT = SGT2[p]; SGZ = SGZ2[p]
            st = []

            def s_gate(bank, c0, dst, half):
                for k in range(4 * half, 4 * half + 4):
                    mm(bank, HT[:, k, :], WIN_cols(k, c0, 512), start=(k == 0), stop=(k == 7))
                if half == 1:
                    cp("act", dst, bank)
            st.append(lambda: s_gate(F0, 1536, SGT, 0))
            st.append(lambda: s_gate(F0, 1536, SGT, 1))
            st.append(lambda: s_gate(F1, 3584, SGZ, 0))
            st.append(lambda: s_gate(F1, 3584, SGZ, 1))

            def s_gab():
                for k in range(8):
                    mm(F2[:, 0:8], HT[:, k, :], WIN_cols(k, 4096, 8), start=(k == 0), stop=(k == 7))
                cp("dve", GAB, F2[:, 0:8])
            st.append(s_gab)

            def s_gq(g3, half=None):
                bank = PS[g3]
                cr = range(4 * g3, 4 * g3 + 4) if half is None else range(4 * g3 + 2 * half, 4 * g3 + 2 * half + 2)
                for c in cr:
                    ob = bank[:, (c % 4) * 128:(c % 4) * 128 + 128]
                    for k in range(8):
                        mm(ob, WIN_cols(k, 2048 + c * 128, 128), HT[:, k, :], start=(k == 0), stop=(k == 7))
            if kind != "s":
                def s_conv(g3, w):
                    cs = slice(4 * g3, 4 * g3 + 4)
                    b3 = (128, 4, Ct)
                    acc = ACC[:, :, 0:Ct]
                    tmp = (GSH0 if w % 2 else GSH1).re("p (c t) -> p c t", c=4)[:, :, 0:Ct]
                    if w == 0:
                        cp("act", XC[:, cs, 3:3 + Ct], PS[g3].re("p (a b) -> p a b", a=4)[:, :, 0:Ct])
                        tt("dve", acc, XC[:, cs, 0:Ct], CW[:, cs, 0].bc(2, b3), ALU.mult)
                    else:
                        tt("pool", tmp, XC[:, cs, w:w + Ct], CW[:, cs, w].bc(2, b3), ALU.mult)
                        tt("dve", CV[:, cs, 0:Ct] if w == 3 else acc, acc, tmp, ALU.add)
                    if w == 3:
                        cp("pool", XC[:, cs, 0:3], XC[:, cs, Ct:Ct + 3])
                for g3 in range(3):
                    st.append(lambda g3=g3: s_gq(g3, 0))
                    st.append(lambda g3=g3: s_gq(g3, 1))
                    for w in range(4):
                        st.append(lambda g3=g3, w=w: s_conv(g3, w))
            else:
                for g3 in range(3):
                    st.append(lambda g3=g3: s_gq(g3))
                st.append(conv_sample)

            def s_silu(i):
                if i == 0:
                    act(SGT, SGT, AF.Silu)
                    act(SGZ, SGZ, AF.Silu)
                    d3 = SGZ.re("p (h i) -> p h i", h=4)
                    tt("dve", d3, d3, GNW, ALU.mult)
                elif kind != "s":
                    act(CV[:, :, 0:Ct], CV[:, :, 0:Ct], AF.Silu)
            st.append(lambda: s_silu(0))
            st.append(lambda: s_silu(1))
            return st

        def conv_sample():
            XCs = XC[:, :, 0:NS * 7].re("p c (s w) -> p c s w", s=NS)
            SCB = GSH0[0:NS * 3, :]
            for g3 in range(3):
                cs = slice(4 * g3, 4 * g3 + 4)
                cp("act", XCs[:, cs, :, 3:7],
                   PS[g3].re("p (a b) -> p a b", a=4)[:, :, 0:CS_].re("p a (s c) -> p a s c", s=NS))
            for g3 in range(3):
                cs = slice(4 * g3, 4 * g3 + 4)
                P.dma(SCB, sconv[:, g3 * 512:(g3 + 1) * 512])
                for c in range(4):
                    mm(R0[:, c * 128:c * 128 + NS * 3], SCB[:, c * 128:(c + 1) * 128], C("IDF")[0:NS * 3, 0:NS * 3])
                cp("dve", XCs[:, cs, :, 0:3],
                   R0.re("p (a b) -> p a b", a=4)[:, :, 0:NS * 3].re("p a (s r) -> p a s r", s=NS))
            for g3 in range(3):
                cs = slice(4 * g3, 4 * g3 + 4)
                ACCs = ACC[:, :, 0:CS_].re("p c (s k) -> p c s k", s=NS)
                tmp4 = GSH0[:, 0:4 * CS_].re("p (c s k) -> p c s k", c=4, s=NS)
                b4 = (128, 4, NS, DSEQ)

                def cwb(w):
                    return CW[:, cs, w].bc(2, (128, 4, NS)).bc(3, b4)
                tt("dve", ACCs, XCs[:, cs, :, 0:4], cwb(0), ALU.mult)
                for w in range(1, 4):
                    tt("dve", tmp4, XCs[:, cs, :, w:w + 4], cwb(w), ALU.mult)
                    tt("dve", ACCs, ACCs, tmp4, ALU.add)
                act(CV[:, cs, 0:CS_], ACC[:, :, 0:CS_], AF.Silu)
            for g3 in range(3):
                cs = slice(4 * g3, 4 * g3 + 4)
                X48 = T1[:, 0:4 * NS * 3].re("p (c r) -> p c r", c=4)
                cp("pool", X48.re("p c (s r) -> p c s r", s=NS), XCs[:, cs, :, 4:7])
                for c in range(4):
                    mm(R0[0:NS * 3, c * 128:(c + 1) * 128], X48[:, c, :], C("IDF"))
                cp("dve", SCB, R0[0:NS * 3, :])
                out_toks.append(P.dma(conv_s[:, g3 * 512:(g3 + 1) * 512], SCB))

        def chain_R(Ct, p, kds_off, cd, samp):
            QK = QK2[p]; VV = VV2[p]; SGT = SGT2[p]
            idc = IDB[0:Ct, 0:Ct]
            q3 = QK[0:Ct, 0, :].re("c (h d) -> c h d", h=4)
            k3 = QK[0:Ct, 1, :].re("c (h d) -> c h d", h=4)
            st = []

            def s1():
                qsc = (C("QS") if samp else C("SC")[:, 0:4])[0:Ct, :]
                tt("dve", q3, q3, qsc.bc(2, (Ct, 4, 128)), ALU.mult)
                for h in range(NH):
                    mm(R0[:, h * 128:h * 128 + Ct], q3[:, h, :], idc)
                cp("act", QDT[:, :, 0:Ct], v4(R0, Ct))
            st.append(s1)

            def s2():
                for h in range(NH):
                    mm(R1[:, h * 128:h * 128 + Ct], k3[:, h, :], idc)
                cp("act", KT[:, :, 0:Ct], v4(R1, Ct))
                tt("dve", k3, k3, C("SC")[0:Ct, kds_off:kds_off + 4].bc(2, (Ct, 4, 128)), ALU.mult)
            st.append(s2)

            def s3():
                for h in range(NH):
                    mm(R0[0:Ct, h * 128:h * 128 + Ct], KT[:, h, 0:Ct], QDT[:, h, 0:Ct])
                mk = (C("MASKRS").re("p (h i) -> p h i", h=4) if samp else C("MASKR").re("p (h i) -> p h i", h=4))[0:Ct, :, 0:Ct]
                tt("dve", SMK[0:Ct, :, 0:Ct], v4(R0, Ct)[0:Ct], mk, ALU.mult)
            st.append(s3)

            if not samp:
                def s4():
                    for h in range(NH):
                        hs = slice(h * 128, (h + 1) * 128)
                        mm(R1[0:Ct, hs], SMK[0:Ct, h, 0:Ct], VV[0:Ct, hs], start=True, stop=False)
                        mm(R1[0:Ct, hs], QDT[:, h, 0:Ct], SRB[:, h, :], start=False, stop=True)
                    for h in range(NH):
                        hs = slice(h * 128, (h + 1) * 128)
                        mm(R0[:, hs], k3[:, h, :], VV[0:Ct, hs])
                st.append(s4)

                def s5():
                    for h in range(NH):
                        hs = slice(h * 128, (h + 1) * 128)
                        stt("dve", SR[:, h, :], SR[:, h, :], cd[h], R0[:, hs], ALU.mult, ALU.add)
                    cp("act", SRB, SR)
                st.append(s5)
            else:
                for h in range(NH):
                    st.append(lambda h=h: sample_state_ret(h, Ct, cd, k3, VV))

            def s6():
                for h in range(NH):
                    hs = slice(h * 128, (h + 1) * 128)
                    act(JUNK[0:Ct, :], R1[0:Ct, hs], AF.Square, accum=SSO[0:Ct, h:h + 1])
                rstd_from(RSO[0:Ct, 0:4], SSO[0:Ct, 0:4], 1.0 / HD, RMS_EPS, RSO2[0:Ct, 0:4])
                for h in range(NH):
                    hs = slice(h * 128, (h + 1) * 128)
                    stt("dve", MIX[0:Ct, hs], R1[0:Ct, hs], RSO[0:Ct, h:h + 1], SGT[0:Ct, hs], ALU.mult, ALU.mult)
                memset("dve", SSO[:, 0:4], 0.0)
            st.append(s6)
            return st

        def zfill(Z, src):
            cp("dve", Z.re("p (s w) -> p s w", s=NS)[:, :, 0:DSEQ], src.re("p (s c) -> p s c", s=NS))

        def load_states(src, h):
            cur = SS_sets[ss_unit[0] % 2]
            ss_unit[0] += 1
            for q in range(4):
                P.dma(cur[q], src[4 * q:4 * q + 4, h, :, :].rearrange("s d e -> d s e"))
                cp("act", SSB[:, 4 * q:4 * q + 4, :], cur[q])
            return cur

        def sample_state_ret(h, Ct, cd, k3, VV):
            hs = slice(h * 128, (h + 1) * 128)
            SSc = load_states(sret, h)
            zfill(ZA, QDT[:, h, 0:Ct])
            mm(R1[0:Ct, hs], SMK[0:Ct, h, 0:Ct], VV[0:Ct, hs], start=True, stop=False)
            for s_ in range(NS):
                mm(R1[0:Ct, hs], ZA[:, s_ * 64:(s_ + 1) * 64], SSB[:, s_, :], start=False, stop=(s_ == NS - 1))
            for q4 in range(4):
                bank = PS[q4 % 2]
                tt("dve", KM[0:Ct, :, :], k3[:, h, :].bc(1, (Ct, 4, 128)),
                   C("BMASK")[0:Ct, q4 * 4:(q4 + 1) * 4].bc(2, (Ct, 4, 128)), ALU.mult)
                for j in range(4):
                    mm(bank[:, j * 128:(j + 1) * 128], KM[0:Ct, j, :], VV[0:Ct, hs])
                v = SSc[q4]
                stt("dve", v, v, cd[h], bank.re("p (a b) -> p a b", a=4), ALU.mult, ALU.add)
                out_toks.append(P.dma(ret_s[4 * q4:4 * q4 + 4, h, :, :].rearrange("s d e -> d s e"), v, queue="pool"))

        def sample_state_gdn(h, Ct):
            hs = slice(h * 128, (h + 1) * 128)
            Qf = QQ[0:Ct, h, 0:Ct]
            SSc = load_states(sgdn, h)
            zfill(ZA, NWK[:, h, 0:Ct])
            mm(GA[0:Ct, hs], Qf, VB[0:Ct, h, :], start=True, stop=False)
            for s_ in range(NS):
                mm(GA[0:Ct, hs], ZA[:, s_ * 64:(s_ + 1) * 64], SSB[:, s_, :], start=False, stop=(s_ == NS - 1))
            cp("act", UU[0:Ct, h, :], GA[0:Ct, hs])
            zfill(ZA, QGT[:, h, 0:Ct])
            for s_ in range(NS):
                mm(GB[0:Ct, hs], ZA[:, s_ * 64:(s_ + 1) * 64], SSB[:, s_, :], start=(s_ == 0), stop=False)
            mm(GB[0:Ct, hs], ATT[0:Ct, h, 0:Ct], UU[0:Ct, h, :], start=False, stop=True)
            for q4 in range(4):
                bank = PS[q4 % 2]
                v = SSc[q4]
                tt("dve", v, v, CDVS[:, h, 4 * q4:4 * q4 + 4].bc(2, (128, 4, 128)), ALU.mult)
                tt("dve", KM[0:Ct, :, :], KTS[0:Ct, h, :].bc(1, (Ct, 4, 128)),
                   C("BMASK")[0:Ct, q4 * 4:(q4 + 1) * 4].bc(2, (Ct, 4, 128)), ALU.mult)
                for j in range(4):
                    mm(bank[:, j * 128:(j + 1) * 128], KM[0:Ct, j, :], UU[0:Ct, h, :])
                tt("dve", v, v, bank.re("p (a b) -> p a b", a=4), ALU.add)
                out_toks.append(P.dma(gdn_s[4 * q4:4 * q4 + 4, h, :, :].rearrange("s d e -> d s e"), v, queue="pool"))

        def chain_G(Ct, nlev, samp, p):
            SGZ = SGZ2[p]
            idc = IDB[0:Ct, 0:Ct]
            tri = (C("TRIS") if samp else C("TRI"))[0:Ct, 0:Ct]
            strict = (C("STRICTS") if samp else C("STRICT"))[0:Ct, 0:Ct]
            sh = (Ct, 4, Ct)
            st = []

            def g1():
                tt("dve", TG[0:Ct, 0:4], GAB[0:Ct, 0:4], DTB[0:Ct, :], ALU.add)
                act(TG[0:Ct, 0:4], TG[0:Ct, 0:4], AF.Exp)
                act(TG[0:Ct, 0:4], TG[0:Ct, 0:4], AF.Ln, bias=1.0)
                tt("dve", GG[0:Ct, :], TG[0:Ct, 0:4], NEGA[0:Ct, :], ALU.mult)
                act(TG[0:Ct, 4:8], GAB[0:Ct, 4:8], AF.Exp, scale=-1.0)
                ts("dve", TG[0:Ct, 4:8], TG[0:Ct, 4:8], 1.0, ALU.add)
                recip(BETA[0:Ct, :], TG[0:Ct, 4:8])
                ts("dve", NBETA[0:Ct, :], BETA[0:Ct, :], -1.0, ALU.mult)
            st.append(g1)

            def g2():
                tt("dve", GTRI[0:Ct, :, 0:Ct], tri.bc(1, sh), GG[0:Ct, :].bc(2, sh), ALU.mult)
                mm(GA[0:Ct, 0:4], tri, GG[0:Ct, :])
                mm(GA[0:Ct, 4:8], strict, GG[0:Ct, :])
                cp("dve", GPP[0:Ct, :], GA[0:Ct, 0:8])
                for h in range(NH):
                    mm(GB[:, h * 128:h * 128 + Ct], C("ONES")[0:Ct, :], GTRI[0:Ct, h, 0:Ct])
            st.append(g2)
            gbc = GB.re("p (h i) -> p h i", h=4)

            def g3():
                tt("dve", DIFF[0:Ct, :, 0:Ct], gbc[0:Ct, :, 0:Ct], GPP[0:Ct, 0:4].bc(2, sh), ALU.subtract)
                act(GAMBC[:, :, 0:Ct], gbc[:, :, 0:Ct], AF.Exp)
                if not samp:
                    act(CDV, gbc[:, :, Ct - 1], AF.Exp)
                else:
                    act(CDVS, gbc[:, :, 0:Ct].re("p h (s c) -> p h s c", c=DSEQ)[:, :, :, DSEQ - 1], AF.Exp)
                act(GAMPP[0:Ct, :], GPP[0:Ct, 0:4], AF.Exp)
                act(KTSC[0:Ct, :], GPP[0:Ct, 4:8], AF.Exp)
            st.append(g3)

            def g4():
                ts("dve", TMPF[0:Ct, :, 0:Ct], DIFF[0:Ct, :, 0:Ct], 0.0, ALU.min)
                act(D2M[0:Ct, :, 0:Ct], TMPF[0:Ct, :, 0:Ct], AF.Exp)
                tt("dve", D2M[0:Ct, :, 0:Ct], D2M[0:Ct, :, 0:Ct], tri.bc(1, sh), ALU.mult)
            st.append(g4)

            def g5():
                ts("dve", TMPF[0:Ct, :, 0:Ct], DIFF[0:Ct, :, 0:Ct], -1.0, ALU.mult, 0.0, ALU.min)
                act(FF[0:Ct, :, 0:Ct], TMPF[0:Ct, :, 0:Ct], AF.Exp)
                tt("dve", FF[0:Ct, :, 0:Ct], FF[0:Ct, :, 0:Ct], NBETA[0:Ct, :].bc(2, sh), ALU.mult)
                tt("dve", FF[0:Ct, :, 0:Ct], FF[0:Ct, :, 0:Ct], strict.bc(1, sh), ALU.mult)
            st.append(g5)

            def m1():
                for j, bank in enumerate((GA, GB, GC)):
                    for h in range(NH):
                        mm(bank[0:Ct, h * 128:(h + 1) * 128], CV[:, j * 4 + h, 0:Ct], IDB)
                for h in range(NH):
                    hs = slice(h * 128, (h + 1) * 128)
                    act(JUNK[0:Ct, :], GA[0:Ct, hs], AF.Square, accum=SSQ[0:Ct, h:h + 1])
                    act(JUNK[0:Ct, :], GB[0:Ct, hs], AF.Square, accum=SSQ[0:Ct, 4 + h:5 + h])
                rstd_from(RS[0:Ct, :], SSQ[0:Ct, :], 1.0, RMS_EPS, SCL[0:Ct, 8:16])
                memset("dve", SSQ, 0.0)
                ts("dve", SCL[0:Ct, 0:4], RS[0:Ct, 0:4], float(HD ** -0.5), ALU.mult)
                tt("dve", SCL[0:Ct, 4:8], RS[0:Ct, 4:8], KTSC[0:Ct, :], ALU.mult)
                tt("dve", SCL[0:Ct, 8:12], RS[0:Ct, 4:8], BETA[0:Ct, :], ALU.mult)
                tt("dve", SCL[0:Ct, 8:12], SCL[0:Ct, 8:12], GAMPP[0:Ct, :], ALU.mult)
            st.append(m1)

            def m1b():
                b3 = (Ct, 4, 128)
                for h in range(NH):
                    hs = slice(h * 128, (h + 1) * 128)
                    act(QN[0:Ct, h, :], GA[0:Ct, hs], AF.Copy, scale=SCL[0:Ct, h:h + 1])
                tt("dve", KN[0:Ct], v4(GB, Ct, 128)[0:Ct], RS[0:Ct, 4:8].bc(2, b3), ALU.mult)
                tt("dve", KBG[0:Ct], v4(GB, Ct, 128)[0:Ct], SCL[0:Ct, 8:12].bc(2, b3), ALU.mult)
                tt("dve", KTS[0:Ct], v4(GB, Ct, 128)[0:Ct], SCL[0:Ct, 4:8].bc(2, b3), ALU.mult)
                for h in range(NH):
                    hs = slice(h * 128, (h + 1) * 128)
                    act(VB[0:Ct, h, :], GC[0:Ct, hs], AF.Copy, scale=BETA[0:Ct, h:h + 1])
            st.append(m1b)

            def m2():
                for h in range(NH):
                    mm(GA[:, h * 128:h * 128 + Ct], KN[0:Ct, h, :], idc)
                for h in range(NH):
                    mm(GB[:, h * 128:h * 128 + Ct], QN[0:Ct, h, :], idc)
                cp("act", KQT[:, 0, :, 0:Ct], v4(GA, Ct))
                cp("act", KQT[:, 1, :, 0:Ct], v4(GB, Ct))
                tt("dve", QGT[:, :, 0:Ct], v4(GB, Ct), GAMBC[:, :, 0:Ct], ALU.mult)
            st.append(m2)

            def m3():
                for h in range(NH):
                    mm(GC[0:Ct, h * 128:h * 128 + Ct], KQT[:, 0, h, 0:Ct], KQT[:, 0, h, 0:Ct])
                for h in range(NH):
                    mm(GA[0:Ct, h * 128:h * 128 + Ct], KQT[:, 0, h, 0:Ct], KQT[:, 1, h, 0:Ct])
                tt("dve", MB[0:Ct, :, 0:Ct], v4(GC, Ct)[0:Ct], FF[0:Ct, :, 0:Ct], ALU.mult)
                tt("dve", ATT[0:Ct, :, 0:Ct], v4(GA, Ct)[0:Ct], D2M[0:Ct, :, 0:Ct], ALU.mult)
            st.append(m3)

            def m4():
                for h in range(NH):
                    mm(GB[0:Ct, h * 128:h * 128 + Ct], MB[0:Ct, h, 0:Ct], idc)
                cp("act", YB[0:Ct, :, 0:Ct], v4(GB, Ct)[0:Ct])
                tt("dve", QQ[0:Ct, :, 0:Ct], v4(GB, Ct)[0:Ct], C("IDF")[0:Ct, 0:Ct].bc(1, sh), ALU.add)
            st.append(m4)

            def lvl_sq(lv):
                for h in range(NH):
                    mm(GC[0:Ct, h * 128:h * 128 + Ct], MB[0:Ct, h, 0:Ct], YB[0:Ct, h, 0:Ct])
                for h in range(NH):
                    mm(GA[0:Ct, h * 128:h * 128 + Ct], YB[0:Ct, h, 0:Ct], MB[0:Ct, h, 0:Ct])
                cp("act", YB[0:Ct, :, 0:Ct], v4(GC, Ct)[0:Ct])
                cp("dve", MB[0:Ct, :, 0:Ct], v4(GA, Ct)[0:Ct])

            def lvl_q(lv):
                for h in range(NH):
                    mm(GB[0:Ct, h * 128:h * 128 + Ct], idc, QQ[0:Ct, h, 0:Ct], start=True, stop=False)
                    mm(GB[0:Ct, h * 128:h * 128 + Ct], MB[0:Ct, h, 0:Ct], QQ[0:Ct, h, 0:Ct], start=False, stop=True)
                cp("act" if lv % 2 else "dve", QQ[0:Ct, :, 0:Ct], v4(GB, Ct)[0:Ct])
            def lvl_fused(j):
                for h in range(NH):
                    mm(GB[0:Ct, h * 128:h * 128 + Ct], idc, QQ[0:Ct, h, 0:Ct], start=True, stop=False)
                    mm(GB[0:Ct, h * 128:h * 128 + Ct], MB[0:Ct, h, 0:Ct], QQ[0:Ct, h, 0:Ct], start=False, stop=True)
                for h in range(NH):
                    mm(GC[0:Ct, h * 128:h * 128 + Ct], MB[0:Ct, h, 0:Ct], YB[0:Ct, h, 0:Ct])
                for h in range(NH):
                    mm(GA[0:Ct, h * 128:h * 128 + Ct], YB[0:Ct, h, 0:Ct], MB[0:Ct, h, 0:Ct])
                cp("act", YB[0:Ct, :, 0:Ct], v4(GC, Ct)[0:Ct])
                cp("dve" if j % 2 else "act", MB[0:Ct, :, 0:Ct], v4(GA, Ct)[0:Ct])
                cp("act" if j % 2 else "dve", QQ[0:Ct, :, 0:Ct], v4(GB, Ct)[0:Ct])
            st.append(lambda: lvl_sq(0))
            for j in range(1, nlev):
                st.append(lambda j=j: lvl_fused(j))
            st.append(lambda: lvl_q(nlev - 1))

            def m6():
                for h in range(NH):
                    mm(GC[:, h * 128:h * 128 + Ct], KBG[0:Ct, h, :], QQ[0:Ct, h, 0:Ct])
                act(NWK[:, :, 0:Ct], v4(GC, Ct), AF.Copy, scale=-1.0)
            st.append(m6)

            if not samp:
                def m7():
                    for h in range(NH):
                        hs = slice(h * 128, (h + 1) * 128)
                        mm(GA[0:Ct, hs], QQ[0:Ct, h, 0:Ct], VB[0:Ct, h, :], start=True, stop=False)
                        mm(GA[0:Ct, hs], NWK[:, h, 0:Ct], SGB[:, h, :], start=False, stop=True)
                    cp("act", UU[0:Ct], v4(GA, Ct, 128)[0:Ct])
                st.append(m7)

                def m8():
                    for h in range(NH):
                        hs = slice(h * 128, (h + 1) * 128)
                        mm(GB[0:Ct, hs], QGT[:, h, 0:Ct], SGB[:, h, :], start=True, stop=False)
                        mm(GB[0:Ct, hs], ATT[0:Ct, h, 0:Ct], UU[0:Ct, h, :], start=False, stop=True)
                    for h in range(NH):
                        hs = slice(h * 128, (h + 1) * 128)
                        mm(GC[:, hs], KTS[0:Ct, h, :], UU[0:Ct, h, :])
                    for h in range(NH):
                        hs = slice(h * 128, (h + 1) * 128)
                        stt("dve", SGs[:, h, :], SGs[:, h, :], CDV[:, h:h + 1], GC[:, hs], ALU.mult, ALU.add)
                    cp("act", SGB, SGs)
                st.append(m8)
            else:
                for h in range(NH):
                    st.append(lambda h=h: sample_state_gdn(h, Ct))

            def m9():
                for h in range(NH):
                    hs = slice(h * 128, (h + 1) * 128)
                    act(JUNK[0:Ct, :], GB[0:Ct, hs], AF.Square, accum=SSO[0:Ct, 4 + h:5 + h])
                rstd_from(RSO[0:Ct, 4:8], SSO[0:Ct, 4:8], 1.0 / HD, RMS_EPS, RSO2[0:Ct, 4:8])
                for h in range(NH):
                    hs = slice(h * 128, (h + 1) * 128)
                    stt("dve", MIX[0:Ct, 512 + h * 128:512 + (h + 1) * 128], GB[0:Ct, hs], RSO[0:Ct, 4 + h:5 + h],
                        SGZ[0:Ct, hs], ALU.mult, ALU.mult)
                memset("dve", SSO[:, 4:8], 0.0)
            st.append(m9)
            return st

        def chain_E(slot, p):
            XT = XT2[p]
            st = []

            def e1(half):
                bank = F0 if half == 0 else F1
                for k in range(4 * half, 4 * half + 4):
                    mm(bank[:, (k % 4) * 128:(k % 4 + 1) * 128], MIX[:, k * 128:(k + 1) * 128], IDB)
                cp("act", MIXT[:, 4 * half:4 * half + 4, :], bank.re("p (a b) -> p a b", a=4))
            st.append(lambda: e1(0))
            st.append(lambda: e1(1))

            def e2(hf):
                bank = F0 if hf == 0 else F1
                for k in range(8):
                    mm(bank, MIXT[:, k, :], WOUT[:, k, hf * 512:(hf + 1) * 512], start=(k == 0), stop=(k == 7))
                stt("dve", H1[:, slot, hf * 512:(hf + 1) * 512], XT[:, hf * 512:(hf + 1) * 512], ALPHA, bank,
                    ALU.mult, ALU.add)
            st.append(lambda: e2(0))
            st.append(lambda: e2(1))
            return st

        def merge(*chains):
            items = []
            for ci, ch in enumerate(chains):
                n = len(ch)
                for j, f in enumerate(ch):
                    items.append(((j + 0.5) / n, ci, j, f))
            items.sort(key=lambda x: (x[0], x[1]))
            for _, _, _, f in items:
                f()

        def run(ch):
            for f in ch:
                f()

        def load_w(wg, wu, wd, c0, n):
            P.dma(wg[:, :, 0:n * 128], w_gu[:, c0 * 128:(c0 + n) * 128].rearrange("(k p) n -> p k n", p=128), queue="pool")
            P.dma(wu[:, :, 0:n * 128], w_gu[:, DFF + c0 * 128:DFF + (c0 + n) * 128].rearrange("(k p) n -> p k n", p=128),
                  queue="pool")
            P.dma(wd[:, 0:n, :], w_dn[c0 * 128:(c0 + n) * 128, :].rearrange("(c p) n -> p c n", p=128), queue="pool")

        def load_block0_early():
            load_w(WG0, WU0, WD0, 0, 4)

        npt = NPT if not dbg else dbg
        tiles = [("meta", 0, NMETA, 0, None, 8, CDEC[16], 3)]
        for t in range(npt):
            tiles.append(("p", t, 128, 1 + t, t, 4, CDEC[128], 6))
        tiles.append(("s", 0, CS_, NPT + 1, npt, 12, CDEC[4], 1))
        run(chain_Fa(tiles[0][0], tiles[0][1], tiles[0][2], tiles[0][3], 0))
        run(chain_Fb(tiles[0][0], tiles[0][2], 0))
        pend_E = None
        NG_HEAD = 7
        for i, (kind, ti, Ct, rot_i, slot, kds, cd, nlev) in enumerate(tiles):
            p = i % 2
            samp = (kind == "s")
            nxt = tiles[i + 1] if i + 1 < len(tiles) else None
            G = chain_G(Ct, nlev, samp, p)
            R = chain_R(Ct, p, kds, cd, samp)
            partA = [G[:NG_HEAD], R]
            if pend_E is not None:
                if samp:
                    run(pend_E)
                else:
                    partA.append(pend_E)
                pend_E = None
            merge(*partA)
            partB = [G[NG_HEAD:]]
            if nxt is not None and nxt[0] == "s":
                def conv_out():
                    for g3 in range(3):
                        for c in range(4):
                            mm(F0[0:3, c * 128:(c + 1) * 128], XC[:, g3 * 4 + c, 0:3], C("IDF"))
                        cp("dve", T2[0:3, :], F0[0:3, :])
                        out_toks.append(P.dma(conv_p[:, g3 * 512:(g3 + 1) * 512], T2[0:3, :]))
                conv_out()
            if nxt is not None:
                partB.append(chain_Fa(nxt[0], nxt[1], nxt[2], nxt[3], 1 - p) + chain_Fb(nxt[0], nxt[2], 1 - p))
            merge(*partB)
            if nxt is not None and nxt[0] == "s":
                for h in range(NH):
                    out_toks.append(P.dma(ret_p[h, :, :], SR[:, h, :]))
                    out_toks.append(P.dma(gdn_p[h, :, :], SGs[:, h, :]))
                load_block0_early()
            if slot is not None:
                pend_E = chain_E(slot, p)
        run(pend_E)

        P.barrier()
        A.off = mark_persist + 6144
        NT = npt + 1
        ACCB_t = [A.f32("ACCB%d" % t, 128, D) for t in range(NT)]
        groups = [(g0, min(4, NT - g0)) for g0 in range(0, NT, 4)]
        H1T_g = [A.bf16("H1T%d" % gi, 128, 8, gn * 128) for gi, (g0, gn) in enumerate(groups)]
        RP2 = A.f32("RP2", 128, 2 * D)
        IDB2 = A.bf16("IDB2", 128, 128)
        CS2 = A.f32("IDF2", 128, 128)
        FB = 4
        blocks = [(c0, min(FB, NCH - c0)) for c0 in range(0, NCH, FB)]
        WG = [WG0, A.bf16("WG1", 128, 8, FB * 128)]
        WU = [WU0, A.bf16("WU1", 128, 8, FB * 128)]
        WD = [WD0, A.bf16("WD1", 128, FB, D)]
        SGB2 = A.f32("SGB2", 128, 512)
        ACT_T = A.bf16("ACTT", 128, FB, 512)
        NROT = 3
        ST2_r = [A.f32("ST2_%d" % i, 128, 12) for i in range(NROT)]
        MV2_r = [A.f32("MV2_%d" % i, 128, 2) for i in range(NROT)]
        SM82_r = [A.f32("SM82_%d" % i, 128, 8) for i in range(NROT)]
        HB2_r = [A.bf16("HB2_%d" % i, 128, D) for i in range(2)]
        rot_ctr = [0]
        print("phase B arena words", A.off)
        P.dma(RP2, rowp[:, 2 * D:4 * D].partition_broadcast(128))
        P.dma(CS2, cst[:, COFF["IDF"][0]:COFF["IDF"][0] + 128])
        cp("dve", IDB2, CS2)

        def load_block(bi):
            c0, n = blocks[bi]
            s = bi % 2
            load_w(WG[s], WU[s], WD[s], c0, n)

        GRP_ROT = 3
        MVG_r = [A.f32("MVG%d" % i, 128, 4, 2) for i in range(GRP_ROT)]
        SMG_r = [A.f32("SMG%d" % i, 128, 3, 4) for i in range(GRP_ROT)]
        grp_ctr = [0]

        def ln_group(tiles, x_in_of, x_of, out_of, gv, bv):
            r = grp_ctr[0] % GRP_ROT
            grp_ctr[0] += 1
            MVG = MVG_r[r]; SMG = SMG_r[r]
            n = len(tiles)
            pre = []

            def s_stats(i, t):
                ST2 = ST2_r[i % NROT]
                x_in = x_in_of(t)
                for j in range(2):
                    P.op("dve", lambda e, j=j: e.bn_stats(out=ST2.ap[:, j * 6:(j + 1) * 6], in_=x_in.ap[:, j * 512:(j + 1) * 512]),
                         ins=[x_in], outs=[ST2])
                P.op("dve", lambda e: e.bn_aggr(out=MVG.ap[:, i, :], in_=ST2.ap.rearrange("p (a b) -> p a b", a=2)),
                     ins=[ST2], outs=[MVG])
            for i, t in enumerate(tiles):
                pre.append(lambda i=i, t=t: s_stats(i, t))

            def s_rstd():
                rstd_from(SMG[:, 0, 0:n], MVG[:, 0:n, 1], 1.0, LN_EPS, SMG[:, 1, 0:n])
                stt("dve", SMG[:, 2, 0:n], MVG[:, 0:n, 0], -1.0, SMG[:, 0, 0:n], ALU.mult, ALU.mult)
            pre.append(s_rstd)
            per = {}
            for i, t in enumerate(tiles):
                def s2(i=i, t=t):
                    x = x_of(t)
                    act(x, x_in_of(t), AF.Identity, bias=SMG[:, 2, i:i + 1], scale=SMG[:, 0, i:i + 1])
                    tt("dve", x, x, gv, ALU.mult)
                    tt("dve", out_of(t), x, bv, ALU.add)
                per[t] = [s2]
            return pre, per

        def b0_tail_steps(t):
            x = ACCB_t[t]
            HB2 = HB2_r[t % 2]
            H1T = H1T_g[t // 4]
            tl = t % 4
            pa = PS[4 + 2 * (t % 2)]; pb_ = PS[5 + 2 * (t % 2)]

            def s3():
                cp("act", HB2, x)
                act(x, x, AF.Copy, scale=ALPHA)
                for k in range(8):
                    mm((pa if k < 4 else pb_)[:, (k % 4) * 128:(k % 4 + 1) * 128], HB2[:, k * 128:(k + 1) * 128], IDB2)

            def s4():
                cp("act", H1T[:, 0:4, tl * 128:(tl + 1) * 128], pa.re("p (a b) -> p a b", a=4))
                cp("dve", H1T[:, 4:8, tl * 128:(tl + 1) * 128], pb_.re("p (a b) -> p a b", a=4))
            return [s3, s4]

        def out_step(t):
            def s():
                x = ACCB_t[t]
                if t < npt:
                    out_toks.append(P.dma(y_p[t * 128:(t + 1) * 128, :], x))
                else:
                    out_toks.append(P.dma(y_s[:, :], x[0:CS_, :]))
            return s

        class Pump:
            def __init__(self):
                self.seq = []
                self.done = {}
                self.pos = 0

            def add_seq(self, steps):
                self.seq.extend(steps)

            def add(self, tiles_steps):
                waves = {}
                for i, (t, st) in enumerate(tiles_steps):
                    for j, f in enumerate(st):
                        waves.setdefault(i + j, []).append((t, j == len(st) - 1, f))
                for w in sorted(waves):
                    for t, last, f in waves[w]:
                        self.seq.append(f)
                        if last:
                            self.done[t] = len(self.seq)

            def pump(self, n):
                e = min(len(self.seq), self.pos + n)
                while self.pos < e:
                    self.seq[self.pos]()
                    self.pos += 1

            def ensure(self, t):
                self.pump(self.done[t] - self.pos)

            def flush(self):
                self.pump(len(self.seq))

        pump1 = Pump()
        for gi, (g0, gn) in enumerate(groups):
            tl_ = list(range(g0, g0 + gn))
            pre, per = ln_group(tl_, lambda t: H1[:, t, :], lambda t: ACCB_t[t], lambda t: ACCB_t[t],
                                RP2[:, 0:D], RP2[:, D:2 * D])
            pump1.add_seq(pre)
            pump1.add([(t, per[t] + b0_tail_steps(t)) for t in tl_])
        pump2 = Pump()
        pump1.ensure(groups[0][0] + groups[0][1] - 1)

        for bi, (c0, n) in enumerate(blocks):
            if bi + 1 < len(blocks):
                load_block(bi + 1)
            s = bi % 2
            for gi, (g0, gn) in enumerate(groups):
                if bi == 0:
                    pump1.ensure(g0 + gn - 1)
                ntok = gn * 128
                for c in range(n):
                    if bi == 0:
                        pump1.pump(5)
                    if bi == len(blocks) - 1:
                        pump2.pump(3)
                    pg = PS[(c % 2) * 2]; pu = PS[(c % 2) * 2 + 1]
                    for k in range(8):
                        mm(pg[:, 0:ntok], WG[s][:, k, c * 128:(c + 1) * 128], H1T_g[gi][:, k, 0:ntok],
                           start=(k == 0), stop=(k == 7))
                    for k in range(8):
                        mm(pu[:, 0:ntok], WU[s][:, k, c * 128:(c + 1) * 128], H1T_g[gi][:, k, 0:ntok],
                           start=(k == 0), stop=(k == 7))
                    act(SGB2[:, 0:ntok], pg[:, 0:ntok], AF.Silu)
                    tt("dve", ACT_T[:, c, 0:ntok], SGB2[:, 0:ntok], pu[:, 0:ntok], ALU.mult)
                for ti in range(gn):
                    t = g0 + ti
                    for hf in range(2):
                        pb = PS[4 + (ti % 2) * 2 + hf]
                        for c in range(n):
                            mm(pb, ACT_T[:, c, ti * 128:(ti + 1) * 128], WD[s][:, c, hf * 512:(hf + 1) * 512],
                               start=(c == 0), stop=(c == n - 1))
                        tt("dve", ACCB_t[t][:, hf * 512:(hf + 1) * 512], ACCB_t[t][:, hf * 512:(hf + 1) * 512], pb, ALU.add)
                if bi == len(blocks) - 1:
                    tl_ = list(range(g0, g0 + gn))
                    pre, per = ln_group(tl_, lambda t: ACCB_t[t], lambda t: ACCB_t[t], lambda t: ACCB_t[t],
                                        RP2[:, 0:D], RP2[:, D:2 * D])
                    pump2.add_seq(pre)
                    pump2.add([(t, per[t] + [out_step(t)]) for t in tl_])
            if bi == 0:
                pump1.flush()
                P.dma(RP2, rowp[:, 4 * D:6 * D].partition_broadcast(128))
        pump2.flush()

        P.barrier()
        P.flush()
        print("ops", P0.n_ops, {e: len(P0.ops[e]) for e in ENGS}, "arena words", A.off)
        P0.emit()
    return nc


def _prep_inputs(inp):
    f = np.float32
    g = lambda k: np.ascontiguousarray(np.asarray(inp[k], dtype=f))
    rowp = np.concatenate([g("emb_ln_g"), g("emb_ln_b"), g("ln1_g")[0], g("ln1_b")[0], g("ln2_g")[0], g("ln2_b")[0],
                           np.tile(g("gdn_norm_w")[0], 4), g("dt_bias")[0], g("a_log")[0]]).reshape(1, -1).astype(f)
    cw = g("conv_w")[0]
    colp = np.ascontiguousarray(cw.reshape(4, 12, 128).transpose(2, 1, 0).reshape(128, 48))
    shared = {"w_in": g("w_in")[0], "w_out": g("w_out")[0], "w_gu": g("w_gate_up")[0], "w_dn": g("w_down")[0],
              "rowp": rowp, "colp": colp, "cst": CST, "rot": ROT, "xm": g("meta_tokens")}
    xp = g("x_prompt"); xs = g("x_sample")
    sr = g("state_ret")[0]; sg = g("state_gdn")[0]; sc = g("state_conv")[0]
    maps = []
    for b in range(NCORES):
        m = dict(shared)
        m["xp"] = xp[b]
        m["xs"] = np.ascontiguousarray(xs[b * NSAMP:(b + 1) * NSAMP].reshape(NSAMP * DSEQ, D))
        m["sret"] = np.ascontiguousarray(sr[b * NSAMP:(b + 1) * NSAMP])
        m["sgdn"] = np.ascontiguousarray(sg[b * NSAMP:(b + 1) * NSAMP])
        m["sconv"] = np.ascontiguousarray(sc[b * NSAMP:(b + 1) * NSAMP].reshape(NSAMP * 3, 1536))
        maps.append(m)
    return maps


_NC_CACHE = {}


def kernel(**inputs):
    if "nc" not in _NC_CACHE:
        _NC_CACHE["nc"] = build()
    nc = _NC_CACHE["nc"]
    maps = _prep_inputs(inputs)
    res = run_bass_kernel_spmd(nc, maps, core_ids=list(range(NCORES)))
    r = res.results
    f = np.float32
    y_prompt = np.stack([r[b]["y_p"] for b in range(NCORES)]).astype(f)
    y_sample = np.concatenate([r[b]["y_s"].reshape(NSAMP, DSEQ, D) for b in range(NCORES)]).astype(f)
    ret_p = np.stack([r[b]["ret_p"] for b in range(NCORES)])[None].astype(f)
    gdn_p = np.stack([r[b]["gdn_p"] for b in range(NCORES)])[None].astype(f)
    conv_p = np.stack([r[b]["conv_p"] for b in range(NCORES)])[None].astype(f)
    ret_s = np.concatenate([r[b]["ret_s"] for b in range(NCORES)])[None].astype(f)
    gdn_s = np.concatenate([r[b]["gdn_s"] for b in range(NCORES)])[None].astype(f)
    conv_s = np.concatenate([r[b]["conv_s"].reshape(NSAMP, 3, 1536) for b in range(NCORES)])[None].astype(f)
    return (y_prompt, y_sample, ret_p, gdn_p, conv_p, ret_s, gdn_s, conv_s)
```
